# Optimizing a Trainium2 kernel written in Bass

```python
import jax, jax.numpy as jnp
from jax import lax
import numpy as np

D_MODEL = 1024
BATCH = 32
SEQ = 2048
DEPTH = 2

GRID_W = 64
CTX_LEN = 256
HEAD_DIM = 64
N_HEADS_TOTAL = D_MODEL // HEAD_DIM
A_GROUPS = N_HEADS_TOTAL // 4
NA_HEADS = N_HEADS_TOTAL // 4
SW_Q_HEADS = N_HEADS_TOTAL - A_GROUPS - NA_HEADS
SW_KV_HEADS = max(1, SW_Q_HEADS // 4)
SW_GROUP = SW_Q_HEADS // SW_KV_HEADS
A_WIDTH = A_GROUPS * HEAD_DIM
NA_WIDTH = NA_HEADS * HEAD_DIM
SW_Q_WIDTH = SW_Q_HEADS * HEAD_DIM
SW_KV_WIDTH = SW_KV_HEADS * HEAD_DIM
MIX_WIDTH = A_WIDTH + NA_WIDTH + SW_Q_WIDTH
PROJ_WIDTHS = (A_WIDTH, A_WIDTH, NA_WIDTH, NA_WIDTH, NA_WIDTH, SW_Q_WIDTH, SW_KV_WIDTH, SW_KV_WIDTH)
IN_WIDTH = sum(PROJ_WIDTHS)
CHUNK = 128
NA_ROWS = 8
NA_COLS = 16
SW_WINDOW = 128
SW_BLOCK = 128
ROPE_BASE = 10000.0
ROPE_HALF = HEAD_DIM // 2
ROPE_FREQS = ROPE_HALF // 2
D_FF = 4 * D_MODEL
DEEPNORM_ALPHA = (2 * DEPTH) ** 0.25
DEEPNORM_BETA = (8 * DEPTH) ** -0.25
LN_EPS = 1e-5
NEG_INF = -1e30

kernel_name = 'hybrid_parallel_heads_diffusion_trunk'


def layer_norm(x, g, b):
    xf = x.astype(jnp.float32)
    mu = jnp.mean(xf, axis=-1, keepdims=True)
    var = jnp.mean(jnp.square(xf - mu), axis=-1, keepdims=True)
    return ((xf - mu) * lax.rsqrt(var + LN_EPS)).astype(x.dtype) * g + b


def modulate(x, shift, scale):
    return x * (1.0 + scale) + shift


def post_norm_residual(x, y, g, b):
    return layer_norm(DEEPNORM_ALPHA * x + y, g, b)


def split_projection(p):
    idx = [int(i) for i in np.cumsum(PROJ_WIDTHS)[:-1]]
    return jnp.split(p, idx, axis=-1)


def to_heads(t, n):
    b, l, _ = t.shape
    return t.reshape(b, l, n, HEAD_DIM).transpose(0, 2, 1, 3)


def to_gqa_q(t):
    b, l, _ = t.shape
    return t.reshape(b, l, SW_KV_HEADS, SW_GROUP, HEAD_DIM).transpose(0, 2, 3, 1, 4)


def merge_heads(t):
    b, h, l, d = t.shape
    return t.transpose(0, 2, 1, 3).reshape(b, l, h * d)


def merge_gqa(t):
    b, kv, g, l, d = t.shape
    return t.transpose(0, 3, 1, 2, 4).reshape(b, l, kv * g * d)


def _rotate(t, ang):
    t1, t2 = jnp.split(t, 2, axis=-1)
    cos, sin = jnp.cos(ang), jnp.sin(ang)
    return jnp.concatenate([t1 * cos - t2 * sin, t1 * sin + t2 * cos], axis=-1)


def axial_rope(t, row_pos, col_pos):
    inv_freq = ROPE_BASE ** (-jnp.arange(ROPE_FREQS, dtype=jnp.float32) / ROPE_FREQS)
    ang_r = row_pos.astype(jnp.float32)[:, None] * inv_freq
    ang_c = col_pos.astype(jnp.float32)[:, None] * inv_freq
    tf = t.astype(jnp.float32)
    out = jnp.concatenate([_rotate(tf[..., :ROPE_HALF], ang_r), _rotate(tf[..., ROPE_HALF:], ang_c)], axis=-1)
    return out.astype(t.dtype)


def attn_probs(scores, sink=None):
    parts = [s.astype(jnp.float32) for s in scores]
    if sink is not None:
        parts.append(jnp.broadcast_to(sink.astype(jnp.float32), parts[0].shape[:-1] + (1,)))
    p = jax.nn.softmax(jnp.concatenate(parts, axis=-1), axis=-1)
    sizes = [s.shape[-1] for s in scores]
    idx = [int(i) for i in np.cumsum(sizes)[:-1]]
    return jnp.split(p[..., :sum(sizes)], idx, axis=-1)


def gmlp_chunk_mix(u, v, ln_g, ln_b, w_s, b_s):
    b, l, _ = v.shape
    v = layer_norm(v, ln_g, ln_b).reshape(b, l // CHUNK, CHUNK, A_GROUPS, HEAD_DIM)
    mixed = jnp.einsum('gij,bnjgc->bnigc', w_s, v) + b_s.T[None, None, :, :, None]
    return u * mixed.reshape(b, l, A_WIDTH)


def ctx_self_attention(q, k, v, sink):
    s = jnp.einsum('bkgqd,bkcd->bkgqc', q, k) * (HEAD_DIM ** -0.5)
    (p,) = attn_probs([s], sink)
    return jnp.einsum('bkgqc,bkcd->bkgqd', p.astype(v.dtype), v)


def neighbourhood_attention(q, k, v, kc, vc, rpb):
    b, h, s, d = q.shape
    rows = s // GRID_W
    win_r = min(NA_ROWS, rows)
    qg = q.reshape(b, h, rows, GRID_W, d)
    kg = k.reshape(b, h, rows, GRID_W, d)
    vg = v.reshape(b, h, rows, GRID_W, d)
    cq = jnp.arange(GRID_W)
    cs = jnp.clip(cq - NA_COLS // 2, 0, GRID_W - NA_COLS)
    col_ok = (cq[None, :] >= cs[:, None]) & (cq[None, :] < cs[:, None] + NA_COLS)
    dc = jnp.clip(cq[None, :] - cq[:, None], -(NA_COLS - 1), NA_COLS - 1) + NA_COLS - 1
    rpb_col = rpb[:, :, dc]
    mask = jnp.tile(col_ok, (1, win_r))
    scale = HEAD_DIM ** -0.5

    def one_row(args):
        q_r, r = args
        rs = jnp.clip(r - win_r // 2, 0, rows - win_r)
        k_r = lax.dynamic_slice_in_dim(kg, rs, win_r, axis=2).reshape(b, h, win_r * GRID_W, d)
        v_r = lax.dynamic_slice_in_dim(vg, rs, win_r, axis=2).reshape(b, h, win_r * GRID_W, d)
        dr = rs + jnp.arange(win_r) - r + NA_ROWS - 1
        bias = jnp.take(rpb_col, dr, axis=1).transpose(0, 2, 1, 3).reshape(h, GRID_W, win_r * GRID_W)
        s_loc = jnp.einsum('bhqd,bhkd->bhqk', q_r, k_r).astype(jnp.float32) * scale + bias.astype(jnp.float32)
        s_loc = jnp.where(mask, s_loc, NEG_INF)
        s_ctx = jnp.einsum('bhqd,bhcd->bhqc', q_r, kc) * scale
        p_loc, p_ctx = attn_probs([s_loc, s_ctx])
        return (jnp.einsum('bhqk,bhkd->bhqd', p_loc.astype(v.dtype), v_r)
                + jnp.einsum('bhqc,bhcd->bhqd', p_ctx.astype(v.dtype), vc))

    out = lax.map(one_row, (jnp.moveaxis(qg, 2, 0), jnp.arange(rows)))
    return jnp.moveaxis(out, 0, 2).reshape(b, h, s, d)


def sliding_window_attention(q, k, v, kc, vc, sink):
    b, kv, g, s, d = q.shape
    nb = s // SW_BLOCK
    pad = ((0, 0), (0, 0), (SW_BLOCK, SW_BLOCK), (0, 0))
    kp = jnp.pad(k, pad).reshape(b, kv, nb + 2, SW_BLOCK, d)
    vp = jnp.pad(v, pad).reshape(b, kv, nb + 2, SW_BLOCK, d)
    band = lambda t: jnp.concatenate([t[:, :, :-2], t[:, :, 1:-1], t[:, :, 2:]], axis=3)
    kb, vb = band(kp), band(vp)
    qb = q.reshape(b, kv, g, nb, SW_BLOCK, d)
    k_rel = jnp.arange(3 * SW_BLOCK) - SW_BLOCK
    band_ok = jnp.abs(k_rel[None, :] - jnp.arange(SW_BLOCK)[:, None]) <= SW_WINDOW
    scale = HEAD_DIM ** -0.5

    def one_block(args):
        q_i, k_i, v_i, i = args
        k_abs = i * SW_BLOCK + k_rel
        ok = band_ok & ((k_abs >= 0) & (k_abs < s))[None, :]
        s_loc = jnp.einsum('bkgqd,bknd->bkgqn', q_i, k_i).astype(jnp.float32) * scale
        s_loc = jnp.where(ok, s_loc, NEG_INF)
        s_ctx = jnp.einsum('bkgqd,bkcd->bkgqc', q_i, kc) * scale
        p_loc, p_ctx = attn_probs([s_loc, s_ctx], sink)
        return (jnp.einsum('bkgqn,bknd->bkgqd', p_loc.astype(v.dtype), v_i)
                + jnp.einsum('bkgqc,bkcd->bkgqd', p_ctx.astype(v.dtype), vc))

    out = lax.map(one_block, (jnp.moveaxis(qb, 3, 0), jnp.moveaxis(kb, 2, 0), jnp.moveaxis(vb, 2, 0), jnp.arange(nb)))
    return jnp.moveaxis(out, 0, 3).reshape(b, kv, g, s, d)


def squared_relu_mlp(h, w1, w2):
    return jnp.square(jax.nn.relu(h @ w1)) @ w2


def trunk_layer(x, xc, c, c_ctx, w_mod, b_mod, w_in, a_ln_g, a_ln_b, a_ws, a_bs, na_rpb, sw_sink,
                w_out, ln1_g, ln1_b, w1, w2, ln2_g, ln2_b, update_ctx):
    s = x.shape[1]
    pos = jnp.arange(s)
    row_pos, col_pos = pos // GRID_W, pos % GRID_W
    mod = jax.nn.silu(c) @ w_mod + b_mod
    mod_c = jax.nn.silu(c_ctx)[None] @ w_mod + b_mod
    sh1, sc1, g1, sh2, sc2, g2 = [m[:, None, :] for m in jnp.split(mod, 6, axis=-1)]
    sh1c, sc1c, g1c, sh2c, sc2c, g2c = [m[:, None, :] for m in jnp.split(mod_c, 6, axis=-1)]

    a_u, a_v, na_q, na_k, na_v, sw_q, sw_k, sw_v = split_projection(modulate(x, sh1, sc1) @ w_in)
    ca_u, ca_v, cna_q, cna_k, cna_v, csw_q, csw_k, csw_v = split_projection(modulate(xc, sh1c, sc1c) @ w_in)
    sink = sw_sink.reshape(SW_KV_HEADS, SW_GROUP)[None, :, :, None, None]

    kc_na, vc_na = to_heads(cna_k, NA_HEADS), to_heads(cna_v, NA_HEADS)
    kc_sw, vc_sw = to_heads(csw_k, SW_KV_HEADS), to_heads(csw_v, SW_KV_HEADS)

    y_a = gmlp_chunk_mix(jax.nn.gelu(a_u), jax.nn.gelu(a_v), a_ln_g, a_ln_b, a_ws, a_bs)
    y_na = neighbourhood_attention(to_heads(na_q, NA_HEADS), to_heads(na_k, NA_HEADS), to_heads(na_v, NA_HEADS),
                                   kc_na, vc_na, na_rpb)
    q_sw = axial_rope(to_gqa_q(sw_q), row_pos, col_pos)
    k_sw = axial_rope(to_heads(sw_k, SW_KV_HEADS), row_pos, col_pos)
    y_sw = sliding_window_attention(q_sw, k_sw, to_heads(sw_v, SW_KV_HEADS), kc_sw, vc_sw, sink)
    y = jnp.concatenate([y_a, merge_heads(y_na), merge_gqa(y_sw)], axis=-1) @ w_out
    x_new = post_norm_residual(x, g1 * y, ln1_g, ln1_b)
    x_new = post_norm_residual(x_new, g2 * squared_relu_mlp(modulate(x_new, sh2, sc2), w1, w2), ln2_g, ln2_b)

    if update_ctx:
        yc_a = gmlp_chunk_mix(jax.nn.gelu(ca_u), jax.nn.gelu(ca_v), a_ln_g, a_ln_b, a_ws, a_bs)
        yc_na = ctx_self_attention(to_heads(cna_q, NA_HEADS)[:, :, None], kc_na, vc_na, None)[:, :, 0]
        yc_sw = ctx_self_attention(to_gqa_q(csw_q), kc_sw, vc_sw, sink)
        yc = jnp.concatenate([yc_a, merge_heads(yc_na), merge_gqa(yc_sw)], axis=-1) @ w_out
        xc = post_norm_residual(xc, g1c * yc, ln1_g, ln1_b)
        xc = post_norm_residual(xc, g2c * squared_relu_mlp(modulate(xc, sh2c, sc2c), w1, w2), ln2_g, ln2_b)
    return x_new, xc


def setup_inputs(seed: int = 0) -> dict:
    key = jax.random.key(seed)
    ks = jax.random.split(key, 20)
    n = lambda k, shape: jax.random.normal(k, shape, jnp.float32)
    return {
        'x': n(ks[0], (BATCH, SEQ, D_MODEL)),
        'c': n(ks[1], (BATCH, D_MODEL)),
        'ctx': n(ks[2], (BATCH, CTX_LEN, D_MODEL)),
        'c_ctx': n(ks[3], (D_MODEL,)),
        'w_mod': n(ks[4], (DEPTH, D_MODEL, 6 * D_MODEL)) * D_MODEL ** -0.5,
        'b_mod': n(ks[5], (DEPTH, 6 * D_MODEL)) * 0.02,
        'w_in': n(ks[6], (DEPTH, D_MODEL, IN_WIDTH)) * D_MODEL ** -0.5,
        'a_ln_g': 1.0 + 0.02 * n(ks[7], (DEPTH, A_WIDTH)),
        'a_ln_b': 0.02 * n(ks[8], (DEPTH, A_WIDTH)),
        'a_ws': n(ks[9], (DEPTH, A_GROUPS, CHUNK, CHUNK)) * CHUNK ** -0.5,
        'a_bs': 1.0 + 0.02 * n(ks[10], (DEPTH, A_GROUPS, CHUNK)),
        'na_rpb': 0.1 * n(ks[11], (DEPTH, NA_HEADS, 2 * NA_ROWS - 1, 2 * NA_COLS - 1)),
        'sw_sink': n(ks[12], (DEPTH, SW_Q_HEADS)),
        'w_out': n(ks[13], (DEPTH, MIX_WIDTH, D_MODEL)) * (MIX_WIDTH ** -0.5 * DEEPNORM_BETA),
        'ln1_g': 1.0 + 0.02 * n(ks[14], (DEPTH, D_MODEL)),
        'ln1_b': 0.02 * n(ks[15], (DEPTH, D_MODEL)),
        'w1': n(ks[16], (DEPTH, D_MODEL, D_FF)) * D_MODEL ** -0.5,
        'w2': n(ks[17], (DEPTH, D_FF, D_MODEL)) * (D_FF ** -0.5 * DEEPNORM_BETA),
        'ln2_g': 1.0 + 0.02 * n(ks[18], (DEPTH, D_MODEL)),
        'ln2_b': 0.02 * n(ks[19], (DEPTH, D_MODEL)),
    }


def reference(x, c, ctx, c_ctx, w_mod, b_mod, w_in, a_ln_g, a_ln_b, a_ws, a_bs, na_rpb, sw_sink,
              w_out, ln1_g, ln1_b, w1, w2, ln2_g, ln2_b):
    xc = ctx
    for layer in range(DEPTH):
        x, xc = trunk_layer(x, xc, c, c_ctx, w_mod[layer], b_mod[layer], w_in[layer], a_ln_g[layer], a_ln_b[layer],
                            a_ws[layer], a_bs[layer], na_rpb[layer], sw_sink[layer], w_out[layer],
                            ln1_g[layer], ln1_b[layer], w1[layer], w2[layer], ln2_g[layer], ln2_b[layer],
                            update_ctx=layer < DEPTH - 1)
    return x
```

```python
import numpy as np
import ml_dtypes
from contextlib import ExitStack
import concourse.bass as bass
import concourse.mybir as mybir
from concourse.bass_utils import run_bass_kernel_spmd

F32 = mybir.dt.float32
BF16 = mybir.dt.bfloat16
AF = mybir.ActivationFunctionType
ALU = mybir.AluOpType

D = 1024
KC = 8
S = 2048
CL = 256
STOT = S + CL
NL = 2
DFF = 4096
ALPHA = float((2 * NL) ** 0.25)
LN_EPS = 1e-5
NEG = -30000.0
NCORES = 8
BPC = 4
SAME_SYNC = True

P_KV0, P_KV1, P_Q0, P_Q1, P_Q2, P_WO0, P_WO1 = 0, 1, 2, 3, 4, 5, 6
P_W1 = 7
P_W2 = 15
NPIECE = 23


class Tok:
    __slots__ = ("name", "w", "r", "sem", "cnt", "excl")

    def __init__(self, name, excl=False):
        self.name = name
        self.w = {}
        self.r = {}
        self.sem = None
        self.cnt = 0
        self.excl = excl


class Prog:
    ENGS = ("pe", "act", "dve", "pool", "sp")

    def __init__(self, nc, es):
        self.nc = nc
        self.es = es
        self.ops = {e: [] for e in self.ENGS}
        self.seen = {e: {} for e in self.ENGS}
        self.cur = {}
        self.cnt = {}
        self.nsem = 0
        self.new_epoch()

    def alloc_sem(self, name):
        self.nsem += 1
        return self.es.enter_context(self.nc.semaphore(f"{name}_{self.nsem}"))

    def new_epoch(self):
        for e in ("pe", "act", "dve", "pool"):
            self.cur[e] = self.alloc_sem("e" + e)
            self.cnt[e] = 0

    def op(self, eng, fn, reads=(), writes=(), dma=None, nosig=False):
        deps = {}

        def add(rec):
            sem, val, oeng, isdma = rec
            if (not isdma) and oeng == eng and (eng == "pe" or not SAME_SYNC):
                return
            k = id(sem)
            if k not in deps or deps[k][1] < val:
                deps[k] = (sem, val)

        for t in reads:
            for rec in t.w.values():
                add(rec)
            if t.excl:
                for rec in t.r.values():
                    add(rec)
        for t in writes:
            for rec in t.w.values():
                add(rec)
            for rec in t.r.values():
                add(rec)
        waits = []
        sn = self.seen[eng]
        for (s, v) in deps.values():
            if sn.get(id(s), 0) < v:
                waits.append((s, v))
                sn[id(s)] = v
        if nosig:
            self.ops[eng].append((fn, waits, None, 0))
            return
        if dma is not None:
            if dma.sem is None:
                dma.sem = self.alloc_sem("d" + dma.name)
            dma.cnt += 1
            sig = (dma.sem, 16 * dma.cnt)
            key = ("dma", id(dma.sem))
            rec = (sig[0], sig[1], eng, True)
            inc = 16
        else:
            self.cnt[eng] += 1
            sig = (self.cur[eng], self.cnt[eng])
            key = eng
            rec = (sig[0], sig[1], eng, False)
            inc = 1
        for t in reads:
            (t.w if t.excl else t.r)[key] = rec
        for t in writes:
            t.w[key] = rec
        self.ops[eng].append((fn, waits, sig, inc))

    def emit(self):
        with self.nc.Block() as block:
            def mk(name):
                def body(e):
                    for fn, waits, sig, inc in self.ops[name]:
                        for s, v in waits:
                            e.wait_ge(s, v)
                        if fn is None:
                            continue
                        ins = fn(e)
                        if sig is not None:
                            ins.then_inc(sig[0], inc)
                return body
            block.tensor(mk("pe"))
            block.scalar(mk("act"))
            block.vector(mk("dve"))
            block.gpsimd(mk("pool"))
            block.sync(mk("sp"))


def host_consts(na_rpb):
    c = {}
    c["ident_f"] = np.eye(128, dtype=np.float32)
    c["ident_b"] = np.eye(128, dtype=np.float32).astype(ml_dtypes.bfloat16)
    Pm = np.zeros((128, 128), np.float32)
    for m in range(128):
        if (m % 32) < 16:
            Pm[m, m + 16] = -1.0
        else:
            Pm[m, m - 16] = 1.0
    c["prope"] = np.ascontiguousarray(Pm.T).astype(ml_dtypes.bfloat16)
    inv_freq = (np.float32(10000.0) ** (-np.arange(16, dtype=np.float32) / np.float32(16))).astype(np.float32)
    pos = np.arange(S)
    row = (pos // 64).astype(np.float32)
    col = (pos % 64).astype(np.float32)
    cs = np.zeros((128, 2, S), np.float32)
    for p in range(128):
        d = p % 64
        if d < 32:
            ang = row * inv_freq[d % 16]
        else:
            ang = col * inv_freq[(d - 32) % 16]
        ang = ang.astype(np.float32)
        cs[p, 0] = np.cos(ang)
        cs[p, 1] = np.sin(ang)
    c["cs"] = cs
    ki = np.arange(128)[:, None]
    qi = np.arange(128)[None, :]
    msk = np.zeros((128, 2, 128), np.float32)
    msk[:, 0, :] = np.where(qi <= ki, 0.0, NEG)
    msk[:, 1, :] = np.where(ki <= qi, 0.0, NEG)
    c["swmask"] = msk.astype(ml_dtypes.bfloat16)
    cq = np.arange(64)
    cst = np.clip(cq - 8, 0, 48)
    col_ok = (cq[None, :] >= cst[:, None]) & (cq[None, :] < cst[:, None] + 16)
    dc = np.clip(cq[None, :] - cq[:, None], -15, 15) + 15
    Tb = na_rpb[:, :, :, dc]
    Tb = np.where(col_ok[None, None, None], Tb, np.float32(NEG)).astype(np.float32)
    Tb = Tb.transpose(0, 1, 2, 4, 3)
    lib = np.empty((NL, 2, 64, 4, 14, 64), np.float32)
    for dr0 in range(14):
        lib[:, 0, :, :, dr0, :] = Tb[:, :, dr0].transpose(0, 2, 1, 3)
        lib[:, 1, :, :, dr0, :] = Tb[:, :, dr0 + 1].transpose(0, 2, 1, 3)
    c["bmlib"] = np.ascontiguousarray(lib.reshape(NL, 128, 4 * 14 * 64))
    return c


def build_program(nseq=BPC, nlayers=NL, dbg=None):
    nc = bass.Bass("TRN2", target_bir_lowering=False)
    es = ExitStack()
    P = Prog(nc, es)

    def din(name, shape, dt=F32):
        return nc.dram_tensor(name, list(shape), dt, kind="ExternalInput").ap()

    x_d = din("x", [nseq, S, D])
    c_d = din("c", [nseq, D])
    ctx_d = din("ctx", [nseq, CL, D])
    cctx_d = din("c_ctx", [D])
    wmod_d = din("w_mod", [NL, D, 6 * D])
    bmod_d = din("b_mod", [NL, 6 * D])
    win_d = din("w_in", [NL, D, 2048])
    alng_d = din("a_ln_g", [NL, 256])
    alnb_d = din("a_ln_b", [NL, 256])
    aws_d = din("a_ws", [NL, 4, 128, 128])
    abs_d = din("a_bs", [NL, 4, 128])
    sink_d = din("sw_sink", [NL, 8])
    wout_d = din("w_out", [NL, D, D])
    ln1g_d = din("ln1_g", [NL, D])
    ln1b_d = din("ln1_b", [NL, D])
    w1_d = din("w1", [NL, D, DFF])
    w2_d = din("w2", [NL, DFF, D])
    ln2g_d = din("ln2_g", [NL, D])
    ln2b_d = din("ln2_b", [NL, D])
    identf_d = din("ident_f", [128, 128])
    identb_d = din("ident_b", [128, 128], BF16)
    prope_d = din("prope", [128, 128], BF16)
    cs_d = din("cs", [128, 2, S])
    swmask_d = din("swmask", [128, 2, 128], BF16)
    bmlib_d = din("bmlib", [NL, 128, 3584])
    y_d = nc.dram_tensor("y", [nseq, S, D], F32, kind="ExternalOutput").ap()
    wsc_d = nc.dram_tensor("wsc", [NL, NPIECE, 128, 4096], BF16, kind="Internal").ap()
    bmsc_d = nc.dram_tensor("bmsc", [NL, 128, 3584], BF16, kind="Internal").ap()
    dbg_out = {}
    if dbg:
        for name, shape in dbg.items():
            dbg_out[name] = nc.dram_tensor("dbg_" + name, list(shape), F32, kind="ExternalOutput").ap()

    def sb(name, shape, dt):
        return es.enter_context(nc.sbuf_tensor(name, list(shape), dt))

    xT = sb("xT", [128, KC, S], F32)
    xcT = sb("xcT", [128, KC, CL], F32)
    kTna = sb("kTna", [128, 2, STOT], BF16)
    kTsw = sb("kTsw", [128, 2, STOT], BF16)
    Vall = sb("Vall", [128, 18, 8, 64], BF16)
    wsl = [sb(f"wsl{i}", [128, 4096], BF16) for i in range(3)]
    hyT = sb("hyT", [128, KC, 512], BF16)
    hid = sb("hid", [128, 8, 512], BF16)
    uT = sb("uT", [128, 2, 512], BF16)
    qTna = sb("qTna", [128, 2, 512], BF16)
    qTsw = sb("qTsw", [128, 4, 512], BF16)
    qb = [sb(f"qb{i}", [128, 256], BF16) for i in range(2)]
    vln = sb("vln", [128, 4, 256], BF16)
    PT = [sb(f"PT{i}", [128, 640], BF16) for i in range(3)]
    tmpF = [sb(f"tmpF{i}", [128, 256], F32) for i in range(3)]
    zbs = [sb(f"zb{i}", [128, 256], BF16) for i in range(2)]
    zsqs = [sb(f"zsq{i}", [128, 256], BF16) for i in range(2)]
    st_mean = sb("st_mean", [128, 512], F32)
    st_rstd = sb("st_rstd", [128, 512], F32)
    st_nmr = sb("st_nmr", [128, 512], F32)
    csb = sb("csb", [128, 2, 512], F32)
    bmlib = sb("bmlib_sb", [128, 4, 14, 64], BF16)
    Bb = sb("Bb", [128, NL, 2, 128], F32)
    WsT = sb("WsT", [128, NL, 4, 128], BF16)
    alnG = sb("alnG", [128, NL, 256], F32)
    alnB = sb("alnB", [128, NL, 256], F32)
    modT = sb("modT", [128, NL, 48, 5], F32)
    lnc = sb("lnc", [128, NL, 4, KC], F32)
    es_t = sb("es_t", [128, NL, 4], F32)
    lnh2 = sb("lnh2", [128, NL, 5, 2, KC], F32)
    lnt = [sb(f"lnt{i}", [128, 256], F32) for i in range(3)]
    TA = sb("TA", [128, 128], F32)
    TB = sb("TB", [128, 72], F32)
    rowsA = sb("rowsA", [128, 128], F32)
    rowsB = sb("rowsB", [72, 128], F32)
    csT = sb("csT", [128, 5, 8], F32)
    ones_f = sb("ones_f", [1, 128], F32)
    ident_f = sb("ident_f_sb", [128, 128], F32)
    ident_b = sb("ident_b_sb", [128, 128], BF16)
    prope = sb("prope_sb", [128, 128], BF16)
    swmask = sb("swmask_sb", [128, 2, 128], BF16)
    ones_b = sb("ones_b", [128, 128], BF16)
    mv = sb("mv", [128, 4, 8], F32)
    epsc = sb("epsc", [128, 1], F32)
    vg = [sb(f"vg{i}", [128, 256], F32) for i in range(2)]
    mvr = sb("mvr", [128, 4], F32)

    ps = [es.enter_context(nc.psum_tensor(f"ps{i}", [128, 512], F32)) for i in range(8)]
    psT = [Tok(f"ps{i}", excl=True) for i in range(8)]

    tk = {}

    def T(name):
        if name not in tk:
            tk[name] = Tok(name)
        return tk[name]

    rr = {}

    def nxt(name, n):
        rr[name] = (rr.get(name, -1) + 1) % n
        return rr[name]

    def acc_bank():
        i = nxt("acc", 2)
        return ps[i], psT[i]

    def s_bank():
        i = 2 + nxt("sb", 2)
        return ps[i], psT[i]

    def o_bank():
        i = 4 + nxt("ob", 2)
        return ps[i], psT[i]

    def aux_bank():
        i = 6 + nxt("aux", 2)
        return ps[i], psT[i]

    def io_bank():
        i = nxt("iob", 8)
        return ps[i], psT[i]

    def tmp():
        i = nxt("tmpF", 3)
        return tmpF[i], T(f"tmpF{i}")

    def pt():
        i = nxt("PT", 3)
        return PT[i], T(f"PT{i}")

    def dma(eng, out, in_, reads, writes, tok):
        P.op(eng, lambda e: e.dma_start(out=out, in_=in_), reads=reads, writes=writes, dma=tok)

    def mm_group(out, pairs, reads, writes):
        n = len(pairs)

        def fn(e):
            ins = None
            for i, (l, r) in enumerate(pairs):
                ins = e.matmul(out, lhsT=l, rhs=r, start=(i == 0), stop=(i == n - 1))
            return ins
        P.op("pe", fn, reads=reads, writes=writes)

    def pe_fn(fn, reads, writes):
        P.op("pe", fn, reads=reads, writes=writes)

    def act(out, in_, func, reads, writes, bias=None, scale=None):
        kw = {}
        if bias is not None:
            kw["bias"] = bias
        if scale is not None:
            kw["scale"] = scale
            if func == AF.Copy:
                func = AF.Identity
        P.op("act", lambda e: e.activation(out=out, in_=in_, func=func, **kw), reads=reads, writes=writes)

    def tt(eng, out, in0, in1, op, reads, writes):
        P.op(eng, lambda e: e.tensor_tensor(out=out, in0=in0, in1=in1, op=op), reads=reads, writes=writes)

    def ts(eng, out, in0, s1, op0, reads, writes, s2=None, op1=None):
        if op1 is None:
            P.op(eng, lambda e: e.tensor_scalar(out=out, in0=in0, scalar1=s1, scalar2=None, op0=op0), reads=reads, writes=writes)
        else:
            P.op(eng, lambda e: e.tensor_scalar(out=out, in0=in0, scalar1=s1, scalar2=s2, op0=op0, op1=op1), reads=reads, writes=writes)

    def stt(eng, out, in0, scalar, in1, op0, op1, reads, writes):
        P.op(eng, lambda e: e.scalar_tensor_tensor(out=out, in0=in0, scalar=scalar, in1=in1, op0=op0, op1=op1),
             reads=reads, writes=writes)

    def cp(eng, out, in_, reads, writes):
        if eng == "act":
            act(out, in_, AF.Copy, reads, writes)
        else:
            P.op(eng, lambda e: e.tensor_copy(out=out, in_=in_), reads=reads, writes=writes)

    def recip(out, in_, reads, writes):
        P.op("dve", lambda e: e.reciprocal(out=out, in_=in_), reads=reads, writes=writes)

    def memset(eng, ap, val, writes):
        P.op(eng, lambda e: e.memset(ap, val), writes=writes)

    def dbg_dump(name, ap, tok):
        if name in dbg_out:
            dma("sp", dbg_out[name], ap, [tok], [], T("dbgsem"))

    castT = {}

    def cast_tok(l, g):
        k = (l, g)
        if k not in castT:
            castT[k] = Tok(f"cast{l}_{g}")
        return castT[k]

    def piece_group(pi):
        if pi <= P_Q2:
            return 0
        if pi <= P_WO1:
            return 1
        if pi < P_W2:
            return 2
        return 3

    castq = []

    def emit_casts(l, sink=None):
        def c(pi, dst, src):
            t = cast_tok(l, piece_group(pi))
            if sink is None:
                dma("pool", dst, src, [], [t], t)
            else:
                sink.append(lambda: dma("pool", dst, src, [], [t], t))
        winv = win_d[l].rearrange("(k p) c -> p k c", p=128)

        def pv(pi, ncols):
            return wsc_d[l, pi][:, 0:8 * ncols].rearrange("p (k c) -> p k c", k=8)
        kv0 = pv(P_KV0, 512)
        c(P_KV0, kv0[:, :, 0:256], winv[:, :, 768:1024])
        c(P_KV0, kv0[:, :, 256:320], winv[:, :, 1792:1856])
        c(P_KV0, kv0[:, :, 320:384], winv[:, :, 1792:1856])
        c(P_KV0, kv0[:, :, 384:448], winv[:, :, 1856:1920])
        c(P_KV0, kv0[:, :, 448:512], winv[:, :, 1856:1920])
        kv1 = pv(P_KV1, 384)
        c(P_KV1, kv1[:, :, 0:256], winv[:, :, 1024:1280])
        c(P_KV1, kv1[:, :, 256:384], winv[:, :, 1920:2048])
        q0 = pv(P_Q0, 512)
        c(P_Q0, q0[:, :, 0:256], winv[:, :, 0:256])
        c(P_Q0, q0[:, :, 256:512], winv[:, :, 512:768])
        c(P_Q1, pv(P_Q1, 512), winv[:, :, 1280:1792])
        c(P_Q2, pv(P_Q2, 256), winv[:, :, 256:512])
        wov = wout_d[l].rearrange("(k p) c -> p k c", p=128)
        c(P_WO0, pv(P_WO0, 512), wov[:, :, 0:512])
        c(P_WO1, pv(P_WO1, 512), wov[:, :, 512:1024])
        w1v = w1_d[l].rearrange("(k p) c -> p k c", p=128)
        for i in range(8):
            c(P_W1 + i, pv(P_W1 + i, 512), w1v[:, :, 512 * i:512 * (i + 1)])
        w2v = w2_d[l].rearrange("(j p) (c d) -> p c j d", p=128, d=128)
        for qq in range(4):
            for hf in range(2):
                pi = P_W2 + qq * 2 + hf
                dst = wsc_d[l, pi].rearrange("p (c j d) -> p c j d", c=4, j=8)
                for cc in range(4):
                    c(pi, dst[:, cc], w2v[:, 4 * hf + cc, 8 * qq:8 * qq + 8, :])

    piece_used = {P_KV0: 4096, P_KV1: 3072, P_Q0: 4096, P_Q1: 4096, P_Q2: 2048, P_WO0: 4096, P_WO1: 4096}
    wseq = []
    wstate = {"issued": 0, "next": 0}

    def w_issue_upto(n):
        while wstate["issued"] < min(n, len(wseq)):
            i = wstate["issued"]
            l, pi = wseq[i]
            si = i % 3
            used = piece_used.get(pi, 4096)
            dma("sp", wsl[si][:, 0:used], wsc_d[l, pi][:, 0:used], [cast_tok(l, piece_group(pi))], [T(f"wsl{si}")], T(f"wsl{si}"))
            wstate["issued"] += 1

    def w_get(l, pi, la=2):
        i = wstate["next"]
        assert wseq[i] == (l, pi), (wseq[i], l, pi)
        w_issue_upto(i + 1 + la)
        wstate["next"] += 1
        si = i % 3
        return wsl[si], T(f"wsl{si}")

    CT = T("consts")
    for (dst, src) in [(ident_f[:, :], identf_d), (ident_b[:, :], identb_d), (prope[:, :], prope_d),
                       (swmask[:, :, :], swmask_d)]:
        dma("sp", dst, src, [], [CT], CT)
    dma("sp", rowsA[0:96, :], bmod_d.rearrange("l (j p) -> (l j) p", p=128), [], [CT], CT)
    for l in range(NL):
        dma("sp", rowsA[96 + 16 * l:104 + 16 * l, :], ln1g_d[l].rearrange("(k p) -> k p", p=128), [], [CT], CT)
        dma("sp", rowsA[104 + 16 * l:112 + 16 * l, :], ln1b_d[l].rearrange("(k p) -> k p", p=128), [], [CT], CT)
        dma("sp", rowsB[16 * l:16 * l + 8, :], ln2g_d[l].rearrange("(k p) -> k p", p=128), [], [CT], CT)
        dma("sp", rowsB[16 * l + 8:16 * l + 16, :], ln2b_d[l].rearrange("(k p) -> k p", p=128), [], [CT], CT)
    memset("dve", rowsB[32:64, :], 0.0, [CT])
    dma("sp", rowsB[32:32 + 8 * nseq, :], c_d.rearrange("b (k p) -> (b k) p", p=128), [], [CT], CT)
    dma("sp", rowsB[64:72, :], cctx_d.rearrange("(k p) -> k p", p=128), [], [CT], CT)
    rowbuf = {(0, 0): (st_mean, "st_mean"), (0, 1): (st_rstd, "st_rstd"), (1, 0): (st_nmr, "st_nmr"), (1, 1): (csb[:, 0, :], "csb")}
    wsraw = csb[:, 1, :].rearrange("p (g j) -> p g j", g=4)
    for l in range(NL):
        r0, r0n = rowbuf[(l, 0)]
        r1, r1n = rowbuf[(l, 1)]
        dma("sp", r0[0:1, 0:256], alng_d[l:l + 1, :], [], [T(r0n)], T(r0n))
        dma("sp", r0[0:1, 256:512], alnb_d[l:l + 1, :], [], [T(r0n)], T(r0n))
        dma("sp", r1[0:1, 0:512], abs_d[l:l + 1].rearrange("o g i -> o (g i)"), [], [T(r1n)], T(r1n))
    sinkr = sb("sinkr", [1, 16], F32)
    dma("sp", sinkr[0:1, 0:16], sink_d.rearrange("(o l) h -> o (l h)", o=1), [], [CT], CT)
    memset("dve", ones_f[:, :], 1.0, [T("ones_f")])
    memset("dve", ones_b[:, :], 1.0, [T("ones_b")])
    memset("dve", epsc[:, :], LN_EPS, [T("epsc")])

    emit_casts(0)
    BMC = Tok("bmcast")
    for l in range(NL):
        dma("pool", bmsc_d[l], bmlib_d[l], [], [BMC], BMC)

    b0, bt0 = aux_bank()
    pe_fn(lambda e: e.transpose(b0[:, 0:128], rowsA[:, :], ident_f[:, :]), [CT], [bt0])
    cp("dve", TA[:, :], b0[:, 0:128], [bt0], [T("TA")])
    b1, bt1 = aux_bank()
    pe_fn(lambda e: e.transpose(b1[:, 0:72], rowsB[0:72, :], ident_f[0:72, 0:72]), [CT], [bt1])
    cp("dve", TB[:, :], b1[:, 0:72], [bt1], [T("TB")])
    act(csT[:, :, :], TB[:, 32:72].rearrange("p (b k) -> p b k", k=8), AF.Silu, [T("TB")], [T("csT")])

    def mod_layer(l, bufs):
        mb_, mbt = aux_bank()
        j = 0
        bi = 0
        while j < 48:
            ap, tok_, ncol = bufs[bi % len(bufs)]
            bi += 1
            nj = ncol // 128
            dma("sp", ap, wmod_d[l].rearrange("(k p) c -> p k c", p=128)[:, :, 128 * j:128 * j + ncol], [], [tok_], tok_)
            for jj in range(nj):
                mm_group(mb_[:, 5 * (j + jj):5 * (j + jj) + 5],
                         [(ap[:, k, 128 * jj:128 * (jj + 1)], csT[:, :, k]) for k in range(8)],
                         [tok_, T("csT")], [mbt])
            j += nj
        for b in range(5):
            tt("dve", modT[:, l, :, b], mb_[:, 0:240].rearrange("p (j b) -> p j b", b=5)[:, :, b],
               TA[:, 48 * l:48 * (l + 1)], ALU.add, [mbt, T("TA")], [T("modT")])
        for kind in (1, 4):
            ts("dve", modT[:, l, 8 * kind:8 * kind + 8, :], modT[:, l, 8 * kind:8 * kind + 8, :], 1.0, ALU.add,
               [T("modT")], [T("modT")], s2=1.0 / ALPHA, op1=ALU.mult)

    def derive_lnh2(l):
        for b in range(5):
            tt("dve", lnh2[:, l, b, 0, :], lnc[:, l, 0, :], modT[:, l, 32:40, b], ALU.mult, [T("lnc"), T("modT")], [T("lnh2")])
            tt("dve", lnh2[:, l, b, 1, :], lnc[:, l, 1, :], modT[:, l, 32:40, b], ALU.mult, [T("lnc"), T("modT")], [T("lnh2")])
            tt("dve", lnh2[:, l, b, 1, :], lnh2[:, l, b, 1, :], modT[:, l, 24:32, b], ALU.add, [T("lnh2"), T("modT")], [T("lnh2")])

    mod_layer(0, [(wsl[i][:, :].bitcast(F32).rearrange("p (k c) -> p k c", k=8), T(f"wsl{i}"), 256) for i in range(3)])
    for l in range(NL):
        a2 = ALPHA if l < nlayers - 1 else 1.0
        ts("dve", lnc[:, l, 0, :], TA[:, 96 + 16 * l:104 + 16 * l], ALPHA, ALU.mult, [T("TA")], [T("lnc")])
        ts("dve", lnc[:, l, 1, :], TA[:, 104 + 16 * l:112 + 16 * l], ALPHA, ALU.mult, [T("TA")], [T("lnc")])
        ts("dve", lnc[:, l, 2, :], TB[:, 16 * l:16 * l + 8], a2, ALU.mult, [T("TB")], [T("lnc")])
        ts("dve", lnc[:, l, 3, :], TB[:, 16 * l + 8:16 * l + 16], a2, ALU.mult, [T("TB")], [T("lnc")])
    derive_lnh2(0)
    for l in range(NL):
        bb_, bbt = aux_bank()
        r0, r0n = rowbuf[(l, 0)]
        r1, r1n = rowbuf[(l, 1)]
        mm_group(bb_[:, 0:512], [(ones_f[0:1, :], r0[0:1, 0:512])], [T(r0n), T("ones_f")], [bbt])
        cp("dve", alnG[:, l, :], bb_[:, 0:256], [bbt], [T("alnGB")])
        cp("dve", alnB[:, l, :], bb_[:, 256:512], [bbt], [T("alnGB")])
        b2_, b2t = aux_bank()

        def fbb(e, l=l, b2_=b2_, r1=r1):
            ins = None
            for m in range(2):
                for hh in range(2):
                    g = 2 * m + hh
                    ins = e.matmul(b2_[64 * hh:64 * hh + 64, 128 * m:128 * m + 128], lhsT=ones_f[0:1, 0:64],
                                   rhs=r1[0:1, 128 * g:128 * g + 128],
                                   start=True, stop=True)
            return ins
        pe_fn(fbb, [T(r1n), T("ones_f")], [b2t])
        cp("dve", Bb[:, l, :, :], b2_[:, 0:256].rearrange("p (m i) -> p m i", m=2), [b2t], [T("Bb")])
        b3_, b3t = aux_bank()

        def fes(e, l=l, b3_=b3_):
            sr = sinkr[0:1, 8 * l:8 * l + 8].rearrange("o (i t) -> o i t", t=2)
            e.matmul(b3_[64:128, 0:4], lhsT=ones_f[0:1, 0:64], rhs=sr[:, :, 0], start=True, stop=True)
            return e.matmul(b3_[0:64, 0:4], lhsT=ones_f[0:1, 0:64], rhs=sr[:, :, 1], start=True, stop=True)
        pe_fn(fes, [CT, T("ones_f")], [b3t])
        act(es_t[:, l, :], b3_[:, 0:4], AF.Exp, [b3t], [T("es_t")])
        dma("sp", wsraw, aws_d[l].rearrange("g i j -> i g j"), [], [T("wsraw")], T("wsraw"))
        b4_, b4t = aux_bank()

        def fws(e, b4_=b4_):
            ins = None
            for g in range(4):
                ins = e.transpose(b4_[:, 128 * g:128 * g + 128], wsraw[:, g, :], ident_f[:, :])
            return ins
        pe_fn(fws, [T("wsraw"), CT], [b4t])
        cp("dve", WsT[:, l, :, :], b4_[:, 0:512].rearrange("p (g i) -> p g i", g=4), [b4t], [T("WsT")])

    def tile_pieces(l, nh):
        out = [(l, P_Q2), (l, P_Q0), (l, P_Q1), (l, P_WO0), (l, P_WO1)]
        for qq in range(4):
            q4 = [(l, P_W1 + 2 * qq), (l, P_W1 + 2 * qq + 1), (l, P_W2 + 2 * qq), (l, P_W2 + 2 * qq + 1)]
            out += q4
            if qq == 0 and nh == 2:
                out += q4
        return out
    for s_ in range(nseq):
        for l in range(nlayers):
            wseq.extend([(l, P_KV0), (l, P_KV1)])
            for _ in range(4):
                wseq.extend(tile_pieces(l, 2))
            if l < nlayers - 1:
                wseq.extend(tile_pieces(l, 1))

    HY = [T("hyT0"), T("hyT1")]

    def mcol(l, kind, k, b):
        return modT[:, l, 8 * kind + k, b:b + 1]

    def modulate(src_of_k, srcTs, l, kind_sh, kind_s, b, c0, Tn, hyts):
        for k in range(KC):
            act(hyT[:, k, c0:c0 + Tn], src_of_k(k), AF.Identity, list(srcTs) + [T("modT")], hyts,
                bias=mcol(l, kind_sh, k, b), scale=mcol(l, kind_s, k, b))

    def rope_evac(acc, acct, dst, dstT, cc0, Tn, scale):
        i = nxt("qb", 2)
        q_b, q_bt = qb[i], T(f"qb{i}")
        act(q_b[:, 0:Tn], acc[:, 0:Tn], AF.Copy, [acct], [q_bt], scale=scale)

        def rest():
            ab, abt = aux_bank()
            mm_group(ab[:, 0:Tn], [(prope[:, :], q_b[:, 0:Tn])], [CT, q_bt], [abt])
            t1, t1t = tmp()
            stt("dve", t1[:, 0:Tn], acc[:, 0:Tn], scale, csb[:, 0, cc0:cc0 + Tn], ALU.mult, ALU.mult, [acct, T("csb")], [t1t])
            t2, t2t = tmp()
            tt("dve", t2[:, 0:Tn], ab[:, 0:Tn], csb[:, 1, cc0:cc0 + Tn], ALU.mult, [abt, T("csb")], [t2t])
            tt("pool", dst, t1[:, 0:Tn], t2[:, 0:Tn], ALU.add, [t1t, t2t], [dstT])
        return rest

    def load_cs(tok0, Tn):
        dma("sp", csb[:, :, 0:Tn], cs_d[:, :, tok0:tok0 + Tn], [], [T("csb")], T("csb"))

    def pass1_all(l, b, wk, wkt, wv, wvt):
        wk3 = wk[:, :].rearrange("p (k c) -> p k c", k=8)
        wv3 = wv[:, 0:3072].rearrange("p (k c) -> p k c", k=8)
        groups = []
        for g in range(8):
            t, h = g // 2, g % 2
            groups.append(dict(src=(lambda k, g=g: xT[:, k, 256 * g:256 * g + 256]), srcT=[T(f"xT{t}_{h}")], tok0=256 * g,
                               is_ctx=False, b=b))
        groups.append(dict(src=(lambda k: xcT[:, k, :]), srcT=[T("xcT")], tok0=S, is_ctx=True, b=4))

        def mod(gi):
            g = groups[gi]
            hh = gi % 2
            modulate(g["src"], g["srcT"], l, 0, 1, g["b"], 256 * hh, 256, [HY[hh]])
        mod(0)
        pend = []
        for gi, g in enumerate(groups):
            hh = gi % 2
            c0 = 256 * hh
            tok0 = g["tok0"]
            if gi + 1 < len(groups):
                mod(gi + 1)
            if (not g["is_ctx"]) and (tok0 % 512 == 0):
                load_cs(tok0, 512)
            for ch in range(4):
                acc, acct = acc_bank()
                mm_group(acc[:, 0:256], [(wk3[:, k, 128 * ch:128 * ch + 128], hyT[:, k, c0:c0 + 256]) for k in range(KC)],
                         [wkt, HY[hh]], [acct])
                while pend:
                    pend.pop(0)()
                if ch < 2:
                    cp("act", kTna[:, ch, tok0:tok0 + 256], acc[:, 0:256], [acct], [T("kTna")])
                elif g["is_ctx"]:
                    cp("act", kTsw[:, ch - 2, tok0:tok0 + 256], acc[:, 0:256], [acct], [T("kTsw")])
                else:
                    pend.append(rope_evac(acc, acct, kTsw[:, ch - 2, tok0:tok0 + 256], T("kTsw"), tok0 % 512, 256, 1.0))
            for tt_ in range(2):
                acc, acct = acc_bank()
                mm_group(acc[:, 0:384], [(hyT[:, k, c0 + 128 * tt_:c0 + 128 * tt_ + 128], wv3[:, k, :]) for k in range(KC)],
                         [wvt, HY[hh]], [acct])
                while pend:
                    pend.pop(0)()
                vt_ = tok0 // 128 + tt_
                a3 = acc[:, 0:384].rearrange("p (b d) -> p b d", d=64)
                cp("dve", Vall[:, vt_, 0:3:2, :], a3[:, 0:2, :], [acct], [T("Vall")])
                cp("dve", Vall[:, vt_, 3:6:2, :], a3[:, 2:4, :], [acct], [T("Vall")])
                cp("dve", Vall[:, vt_, 6:8, :], a3[:, 4:6, :], [acct], [T("Vall")])

    class Half:
        def __init__(self, l, b, t, h, is_ctx):
            self.l, self.b, self.t, self.h, self.is_ctx = l, b, t, h, is_ctx
            self.Tn = 256
            self.c0 = 0 if is_ctx else 256 * h
            self.tok0 = 0 if is_ctx else 512 * t + 256 * h
            self.cs = slice(self.c0, self.c0 + 256)
            self.xTok = T("xcT") if is_ctx else T(f"xT{t}_{h}")
            self.hy = T(f"hyT{h}")
            self.uTt = T(f"uT{h}")
            self.qnat = T(f"qTna{h}")
            self.qswt = T(f"qTsw{h}")
            self.vlnt = T(f"vln{h}")
            self.hidt = T(f"hid{h}")
            self.mvt = T(f"mv{h}")
            self.stt_ = T(f"st{h}")
            sb_ = (6, 7) if h == 0 else (4, 5)
            self.s1, self.s1t, self.s2, self.s2t = ps[sb_[0]], psT[sb_[0]], ps[sb_[1]], psT[sb_[1]]

        def xs(self, k):
            if self.is_ctx:
                return xcT[:, k, :]
            return xT[:, k, self.tok0:self.tok0 + 256]

    def finish_head(ob, obt, hp, ncols, dst, dstT, l, es_pair):
        dp = 1 - hp
        dsl = slice(64 * dp, 64 * dp + 64)
        osl = slice(64 * hp, 64 * hp + 64)
        t1, t1t = tmp()
        if es_pair is not None:
            ts("dve", t1[dsl, 0:ncols], ob[dsl, 0:ncols], es_t[dsl, l, es_pair:es_pair + 1], ALU.add,
               [obt, T("es_t")], [t1t])
            recip(t1[dsl, 0:ncols], t1[dsl, 0:ncols], [t1t], [t1t])
        else:
            recip(t1[dsl, 0:ncols], ob[dsl, 0:ncols], [obt], [t1t])
        tt("dve", dst, ob[osl, 0:ncols], t1[dsl, 0:ncols], ALU.mult, [obt, t1t], [dstT])

    def pv_group(e, ob, col0, n, hp, slots):
        ns = len(slots)
        ins = None
        for i, (vl, pr, ksl) in enumerate(slots):
            e.matmul(ob[64 * hp:64 * hp + 64, col0:col0 + n], lhsT=vl, rhs=pr, start=(i == 0), stop=(i == ns - 1))
            ins = e.matmul(ob[64 * (1 - hp):64 * (1 - hp) + 64, col0:col0 + n], lhsT=ones_b[ksl, 0:64], rhs=pr,
                           start=(i == 0), stop=(i == ns - 1))
        return ins

    def run_items(items):
        prev = None
        for it in items:
            it["qk"]()
            bg_step(1)
            it["ex"]()
            if prev is not None:
                prev["pv"]()
            bg_step(1)
            if castq:
                castq.pop(0)()
            prev = it
        prev["pv"]()

    def attention(H):
        l, t = H.l, H.t
        obs = {}

        def get_ob(key):
            if key not in obs:
                obs[key] = o_bank()
            return obs[key]
        items = []

        def mk_dense(kind, h):
            hp, hc = h % 2, h // 2
            psl = slice(64 * hp, 64 * hp + 64)
            if kind == "na":
                kT_, kTt, kch, qT_, qTt, vc0, ych, esp = kTna, T("kTna"), hc, qTna, H.qnat, [0, 2, 3, 5][h], 2 + hc, None
            else:
                kv = h // 4
                kT_, kTt, kch, qT_, qTt, vc0, ych, esp = kTsw, T("kTsw"), kv, qTsw, H.qswt, 6 + kv, 4 + hc, hc
            st = {}

            def qk():
                sbk, sbt = s_bank()
                st["s"] = (sbk, sbt)

                def f(e):
                    ins = None
                    for ci in range(2):
                        ins = e.matmul(sbk[:, 256 * ci:256 * ci + 256], lhsT=kT_[psl, kch, S + 128 * ci:S + 128 * ci + 128],
                                       rhs=qT_[psl, hc, H.cs], start=True, stop=True)
                    return ins
                pe_fn(f, [kTt, qTt], [sbt])

            def ex():
                sbk, sbt = st["s"]
                p_, p_t = pt()
                st["p"] = (p_, p_t)
                act(p_[:, 0:512], sbk[:, 0:512], AF.Exp, [sbt], [p_t])

            def pv():
                p_, p_t = st["p"]
                ob, obt = get_ob((kind, h))
                pe_fn(lambda e: pv_group(e, ob, 0, 256, hp,
                                         [(Vall[:, 16 + ci, vc0, :], p_[:, 256 * ci:256 * ci + 256], slice(0, 128)) for ci in range(2)]),
                      [p_t, T("Vall"), T("ones_b")], [obt])
                finish_head(ob, obt, hp, 256, hyT[psl, ych, H.cs], H.hy, l, esp)
            return {"qk": qk, "ex": ex, "pv": pv}

        def mk_na(h, rr_):
            hp, hc = h % 2, h // 2
            psl = slice(64 * hp, 64 * hp + 64)
            r = 8 * t + 4 * H.h + rr_
            rs = min(max(r - 4, 0), 24)
            p = rs % 2
            jt0 = (rs - p) // 2
            nsl = 5 if p else 4
            dr0 = (rs - p) - r + 7
            q0 = H.c0 + 64 * rr_
            ncol = 64 * (nsl + 2)
            b0 = [0, 1, 3, 4][h]
            st = {}

            def half(i):
                if p == 1 and i == 0:
                    return 1
                if p == 1 and i == nsl - 1:
                    return 0
                return None

            def qk():
                sbk, sbt = s_bank()
                st["s"] = (sbk, sbt)

                def f(e):
                    e.matmul(sbk[:, 0:64 * nsl], lhsT=ident_b[:, :],
                             rhs=bmlib[:, h, dr0:dr0 + 2 * nsl - 1:2, :], start=True, stop=False)
                    for i in range(nsl):
                        jt = jt0 + i
                        hf = half(i)
                        last = (i == nsl - 1)
                        if hf is None:
                            e.matmul(sbk[:, 64 * i:64 * i + 64], lhsT=kTna[psl, hc, 128 * jt:128 * jt + 128],
                                     rhs=qTna[psl, hc, q0:q0 + 64], start=False, stop=last)
                        else:
                            e.matmul(sbk[64 * hf:64 * hf + 64, 64 * i:64 * i + 64],
                                     lhsT=kTna[psl, hc, 128 * jt + 64 * hf:128 * jt + 64 * hf + 64],
                                     rhs=qTna[psl, hc, q0:q0 + 64], start=False, stop=last)
                    ins = None
                    for ci in range(2):
                        ins = e.matmul(sbk[:, 64 * (nsl + ci):64 * (nsl + ci) + 64],
                                       lhsT=kTna[psl, hc, S + 128 * ci:S + 128 * ci + 128],
                                       rhs=qTna[psl, hc, q0:q0 + 64], start=True, stop=True)
                    return ins
                pe_fn(f, [T("kTna"), H.qnat, T("bmlib"), CT], [sbt])

            def ex():
                sbk, sbt = st["s"]
                p_, p_t = pt()
                st["p"] = (p_, p_t)
                act(p_[:, 0:ncol], sbk[:, 0:ncol], AF.Exp, [sbt], [p_t])

            def pv():
                p_, p_t = st["p"]
                ob, obt = get_ob(("na", h))
                slots = []
                for i in range(nsl):
                    hf = half(i)
                    ksl = slice(0, 128) if hf is None else slice(64 * hf, 64 * hf + 64)
                    slots.append((Vall[ksl, jt0 + i, b0:b0 + 2, :].rearrange("p a d -> p (a d)"), p_[ksl, 64 * i:64 * i + 64]))
                for ci in range(2):
                    slots.append((Vall[:, 16 + ci, b0:b0 + 2, :].rearrange("p a d -> p (a d)"), p_[:, 64 * (nsl + ci):64 * (nsl + ci) + 64]))

                def fpv(e):
                    ins = None
                    ns_ = len(slots)
                    for i, (vl, pr) in enumerate(slots):
                        ins = e.matmul(ob[:, 64 * rr_:64 * rr_ + 64], lhsT=vl, rhs=pr, start=(i == 0), stop=(i == ns_ - 1))
                    return ins
                pe_fn(fpv, [p_t, T("Vall")], [obt])
                if rr_ == 3:
                    finish_head(ob, obt, hp, 256, hyT[psl, 2 + hc, H.cs], H.hy, l, None)
            return {"qk": qk, "ex": ex, "pv": pv}

        def mk_sw(h, bb):
            hp, hc, kv = h % 2, h // 2, h // 4
            psl = slice(64 * hp, 64 * hp + 64)
            vc0 = 6 + kv
            qbk = 4 * t + 2 * H.h + bb
            q0 = H.c0 + 128 * bb
            valid = [0 <= qbk - 1 + i <= 15 for i in range(3)]
            i0 = 0 if valid[0] else 1
            i1 = 3 if valid[2] else 2
            st = {}

            def qk():
                sbk, sbt = s_bank()
                sck, sct = aux_bank()
                st["s"] = (sbk, sbt, sck, sct)

                def f(e):
                    ins = None
                    for i in range(3):
                        if not valid[i]:
                            continue
                        kb = qbk - 1 + i
                        o_ = sbk[:, 128 * i:128 * i + 128]
                        if i != 1:
                            e.matmul(o_, lhsT=ident_b[:, :], rhs=swmask[:, 0 if i == 0 else 1, :], start=True, stop=False)
                        ins = e.matmul(o_, lhsT=kTsw[psl, kv, 128 * kb:128 * kb + 128], rhs=qTsw[psl, hc, q0:q0 + 128],
                                       start=(i == 1), stop=True)
                    return ins
                pe_fn(f, [T("kTsw"), H.qswt, CT], [sbt])

                def f2(e):
                    ins = None
                    for ci in range(2):
                        ins = e.matmul(sck[:, 128 * ci:128 * ci + 128], lhsT=kTsw[psl, kv, S + 128 * ci:S + 128 * ci + 128],
                                       rhs=qTsw[psl, hc, q0:q0 + 128], start=True, stop=True)
                    return ins
                pe_fn(f2, [T("kTsw"), H.qswt], [sct])

            def ex():
                sbk, sbt, sck, sct = st["s"]
                p_, p_t = pt()
                st["p"] = (p_, p_t)
                act(p_[:, 128 * i0:128 * i1], sbk[:, 128 * i0:128 * i1], AF.Exp, [sbt], [p_t])
                act(p_[:, 384:640], sck[:, 0:256], AF.Exp, [sct], [p_t])

            def pv():
                p_, p_t = st["p"]
                ob, obt = get_ob(("sw", h))
                slots = []
                for i in range(i0, i1):
                    kb = qbk - 1 + i
                    slots.append((Vall[:, kb, vc0, :], p_[:, 128 * i:128 * i + 128], slice(0, 128)))
                for ci in range(2):
                    slots.append((Vall[:, 16 + ci, vc0, :], p_[:, 384 + 128 * ci:384 + 128 * ci + 128], slice(0, 128)))
                pe_fn(lambda e: pv_group(e, ob, 128 * bb, 128, hp, slots), [p_t, T("Vall"), T("ones_b")], [obt])
                if bb == 1:
                    finish_head(ob, obt, hp, 256, hyT[psl, 4 + hc, H.cs], H.hy, l, hc)
            return {"qk": qk, "ex": ex, "pv": pv}

        if H.is_ctx:
            items = [mk_dense("na", h) for h in range(4)] + [mk_dense("sw", h) for h in range(8)]
        else:
            items = [mk_na(h, rr_) for h in range(4) for rr_ in range(4)] + [mk_sw(h, bb) for h in range(8) for bb in range(2)]
        run_items(items)

    bg = []

    def bg_step(n=1):
        for _ in range(n):
            if bg:
                bg.pop(0)()

    def bg_drain():
        while bg:
            bg.pop(0)()

    ln_pending = []

    def ln_flush():
        while ln_pending:
            ln_pending.pop(0)()

    def ln_stats_chunk(H, k):
        ln_flush()
        i = nxt("zb", 2)
        zb, zsq = zbs[i], zsqs[i]
        cp("dve", zb[:, 0:256], H.xs(k), [H.xTok], [T(f"zb{i}")])
        tt("pool", zsq[:, 0:256], H.xs(k), H.xs(k), ALU.mult, [H.xTok], [T(f"zsq{i}")])

        def f(e):
            e.matmul(H.s1[:, 0:256], lhsT=ones_b[:, :], rhs=zb[:, 0:256], start=(k == 0), stop=(k == KC - 1))
            return e.matmul(H.s2[:, 0:256], lhsT=ones_b[:, :], rhs=zsq[:, 0:256], start=(k == 0), stop=(k == KC - 1))
        ln_pending.append(lambda: pe_fn(f, [T(f"zb{i}"), T(f"zsq{i}"), T("ones_b")], [H.s1t, H.s2t]))

    def ln_finalize_ops(H):
        cs_ = H.cs
        st = H.stt_
        return [
            lambda: (ln_flush(), ts("dve", st_mean[:, cs_], H.s1[:, 0:256], 1.0 / D, ALU.mult, [H.s1t], [st])),
            lambda: tt("dve", st_nmr[:, cs_], st_mean[:, cs_], st_mean[:, cs_], ALU.mult, [st], [st]),
            lambda: stt("dve", st_rstd[:, cs_], H.s2[:, 0:256], 1.0 / D, st_nmr[:, cs_], ALU.mult, ALU.subtract, [H.s2t, st], [st]),
            lambda: act(st_rstd[:, cs_], st_rstd[:, cs_], AF.Ln, [st, T("epsc")], [st], bias=epsc[:, 0:1], scale=1.0),
            lambda: act(st_rstd[:, cs_], st_rstd[:, cs_], AF.Exp, [st], [st], scale=-0.5),
            lambda: stt("dve", st_nmr[:, cs_], st_mean[:, cs_], -1.0, st_rstd[:, cs_], ALU.mult, ALU.mult, [st], [st]),
        ]

    def ln_apply_ops(H, k, gcol, bcol, h2):
        box = {}

        def o1():
            i_ = nxt("lnt", 3)
            box["t"] = (lnt[i_], T(f"lnt{i_}"))
            t1, t1t = box["t"]
            tt("pool", t1[:, 0:256], H.xs(k), st_rstd[:, H.cs], ALU.mult, [H.xTok, H.stt_], [t1t])

        def o2():
            t1, t1t = box["t"]
            tt("dve", t1[:, 0:256], t1[:, 0:256], st_nmr[:, H.cs], ALU.add, [t1t, H.stt_], [t1t])

        def o3():
            t1, t1t = box["t"]
            if h2:
                act(hyT[:, k, H.cs], t1[:, 0:256], AF.Identity, [t1t, T("lnh2")], [H.hy],
                    bias=lnh2[:, H.l, H.b, 1, k:k + 1], scale=lnh2[:, H.l, H.b, 0, k:k + 1])
                ts("dve", H.xs(k), t1[:, 0:256], gcol, ALU.mult, [t1t, T("lnc")], [H.xTok], s2=bcol, op1=ALU.add)
            else:
                act(H.xs(k), t1[:, 0:256], AF.Identity, [t1t, T("lnc")], [H.xTok], bias=bcol, scale=gcol)
        return [o1, o2, o3]

    def ln_push(H, which, h2, defer_fin=True):
        l = H.l
        fin = ln_finalize_ops(H)
        nimm = 3 if defer_fin else 6
        for f_ in fin[:nimm]:
            f_()
        bg.extend(fin[nimm:])
        chains = [ln_apply_ops(H, k, lnc[:, l, which, k:k + 1], lnc[:, l, which + 1, k:k + 1], h2) for k in range(KC)]
        for step in range(KC + 2):
            for k in range(KC):
                j = step - k
                if 0 <= j < 3:
                    bg.append(chains[k][j])

    def ph_modulate(H):
        modulate(H.xs, [H.xTok], H.l, 0, 1, H.b, H.c0, 256, [H.hy])

    def ph_proj(Hs, l):
        w2_, w2t = w_get(l, P_Q2)
        w23 = w2_[:, 0:2048].rearrange("p (k c) -> p k c", k=8)
        vgs = {}
        for H in Hs:
            for tt_ in range(2):
                vi = 2 * H.h + tt_
                acc, acct = acc_bank()
                mm_group(acc[:, 0:256], [(hyT[:, k, H.c0 + 128 * tt_:H.c0 + 128 * tt_ + 128], w23[:, k, :]) for k in range(KC)],
                         [w2t, H.hy], [acct])
                vg_, vgt = ((vg[vi], T(f"vg{vi}")) if vi < 2 else (lnt[vi - 2], T(f"lnt{vi - 2}")))
                vgs[vi] = (vg_, vgt, H)
                act(vg_[:, :], acc[:, 0:256], AF.Gelu_apprx_tanh, [acct], [vgt])
                P.op("dve", lambda e, vg_=vg_, vi=vi: e.bn_stats(out=mv[:, vi, 0:6], in_=vg_[:, :]), reads=[vgt], writes=[T("mv")])
                P.op("dve", lambda e, vi=vi: e.bn_aggr(out=mv[:, vi, 6:8], in_=mv[:, vi, 0:6]), reads=[T("mv")], writes=[T("mv")])
        nv = 2 * len(Hs)
        cp("dve", mvr[:, 0:nv], mv[:, 0:nv, 7], [T("mv")], [T("mvr")])
        act(mvr[:, 0:nv], mvr[:, 0:nv], AF.Sqrt, [T("mvr"), T("epsc")], [T("mvr")], bias=epsc[:, 0:1], scale=1.0)
        recip(mvr[:, 0:nv], mvr[:, 0:nv], [T("mvr")], [T("mvr")])
        for vi in range(nv):
            vg_, vgt, H = vgs[vi]
            ts("dve", vg_[:, :], vg_[:, :], mv[:, vi, 6:7], ALU.subtract, [vgt, T("mv"), T("mvr")], [vgt], s2=mvr[:, vi:vi + 1], op1=ALU.mult)
            tt("pool", vg_[:, :], vg_[:, :], alnG[:, l, :], ALU.mult, [vgt, T("alnGB")], [vgt])
            tt("dve", vln[:, vi, :], vg_[:, :], alnB[:, l, :], ALU.add, [vgt, T("alnGB")], [H.vlnt])
        w0, w0t = w_get(l, P_Q0)
        w03 = w0[:, :].rearrange("p (k c) -> p k c", k=8)
        for H in Hs:
            for ch in range(4):
                acc, acct = acc_bank()
                mm_group(acc[:, 0:256], [(w03[:, k, 128 * ch:128 * ch + 128], hyT[:, k, H.cs]) for k in range(KC)],
                         [w0t, H.hy], [acct])
                if ch < 2:
                    act(uT[:, ch, H.cs], acc[:, 0:256], AF.Gelu_apprx_tanh, [acct], [H.uTt])
                else:
                    act(qTna[:, ch - 2, H.cs], acc[:, 0:256], AF.Copy, [acct], [H.qnat], scale=0.125)
        w1_, w1t = w_get(l, P_Q1)
        w13 = w1_[:, :].rearrange("p (k c) -> p k c", k=8)
        pend = []
        for H in Hs:
            for ch in range(4):
                acc, acct = acc_bank()
                mm_group(acc[:, 0:256], [(w13[:, k, 128 * ch:128 * ch + 128], hyT[:, k, H.cs]) for k in range(KC)],
                         [w1t, H.hy], [acct])
                while pend:
                    pend.pop(0)()
                if H.is_ctx:
                    act(qTsw[:, ch, H.cs], acc[:, 0:256], AF.Copy, [acct], [H.qswt], scale=0.125)
                else:
                    pend.append(rope_evac(acc, acct, qTsw[:, ch, H.cs], H.qswt, H.c0, 256, 0.125))
        while pend:
            pend.pop(0)()
        return pend

    def ph_gmlp(H):
        l = H.l
        for m in range(2):
            ab, abt = acc_bank()

            def fmix(e, ab=ab, m=m):
                ins = None
                for n in range(2):
                    for hh in range(2):
                        g = 2 * m + hh
                        ins = e.matmul(ab[64 * hh:64 * hh + 64, 128 * n:128 * n + 128], lhsT=vln[:, 2 * H.h + n, 64 * g:64 * g + 64],
                                       rhs=WsT[:, l, g, :], start=True, stop=True)
                return ins
            pe_fn(fmix, [H.vlnt, T("WsT")], [abt])
            t1, t1t = tmp()
            for n in range(2):
                tt("dve", t1[:, 128 * n:128 * n + 128], ab[:, 128 * n:128 * n + 128], Bb[:, l, m, :], ALU.add,
                   [abt, T("Bb")], [t1t])
            tt("pool", hyT[:, m, H.cs], t1[:, 0:256], uT[:, m, H.cs], ALU.mult, [t1t, H.uTt], [H.hy])

    def ph_wout(H, wos):
        l, b = H.l, H.b
        bg_drain()
        for dch in range(KC):
            wo3, wot = wos[dch // 4]
            dl = dch % 4
            acc, acct = acc_bank()
            mm_group(acc[:, 0:256], [(wo3[:, k, 128 * dl:128 * dl + 128], hyT[:, k, H.cs]) for k in range(KC)],
                     [wot, H.hy], [acct])
            stt("dve", H.xs(dch), acc[:, 0:256], mcol(l, 2, dch, b), H.xs(dch), ALU.mult, ALU.add, [acct, T("modT"), H.xTok], [H.xTok])
            ln_stats_chunk(H, dch)
        ln_push(H, 0, True, defer_fin=(H.h == 0 and not H.is_ctx))

    def ffn_quarter(Hs, l, qq, hook=None):
        for hh in range(2):
            wa, wat = w_get(l, P_W1 + 2 * qq + hh)
            wa3 = wa[:, :].rearrange("p (k c) -> p k c", k=8)
            for H in Hs:
                for jj in range(4):
                    j = 4 * hh + jj
                    acc, acct = acc_bank()
                    mm_group(acc[:, 0:256], [(wa3[:, k, 128 * jj:128 * jj + 128], hyT[:, k, H.cs]) for k in range(KC)],
                             [wat, H.hy], [acct])
                    t1, t1t = tmp()
                    act(t1[:, 0:256], acc[:, 0:256], AF.Relu, [acct], [t1t])
                    tt("pool", hid[:, j, H.cs], t1[:, 0:256], t1[:, 0:256], ALU.mult, [t1t], [H.hidt])
                    if qq == 0:
                        bg_step(2)
        if hook is not None:
            hook()
        for hf in range(2):
            wb, wbt = w_get(l, P_W2 + 2 * qq + hf)
            wb4 = wb[:, :].rearrange("p (c j d) -> p c j d", c=4, j=8)
            for H in Hs:
                for dl in range(4):
                    dch = 4 * hf + dl
                    acc, acct = acc_bank()
                    mm_group(acc[:, 0:256], [(wb4[:, dl, j, :], hid[:, j, H.cs]) for j in range(8)],
                             [wbt, H.hidt], [acct])
                    stt("dve", H.xs(dch), acc[:, 0:256], mcol(l, 5, dch, H.b), H.xs(dch), ALU.mult, ALU.add,
                        [acct, T("modT"), H.xTok], [H.xTok])
                    if qq == 3:
                        ln_stats_chunk(H, dch)
                    if qq == 0:
                        bg_step(2)

    def ph_ffn(Hs, l, hook=None):
        if len(Hs) == 1:
            bg_drain()
            ffn_quarter(Hs, l, 0)
        else:
            ffn_quarter(Hs[0:1], l, 0)
            bg_drain()
            ffn_quarter(Hs[1:2], l, 0)
        for qq in range(1, 4):
            ffn_quarter(Hs, l, qq, hook if qq == 3 else None)
        for H in Hs:
            ln_push(H, 2, False)

    premod = set()

    def mk_halves(l, b, t, is_ctx):
        return [Half(l, b, t, 0, True)] if is_ctx else [Half(l, b, t, 0, False), Half(l, b, t, 1, False)]

    def pre_modulate(l, b, t, is_ctx):
        if not is_ctx:
            load_cs(512 * t, 512)
        for H in mk_halves(l, b, t, is_ctx):
            ph_modulate(H)
        premod.add((l, b, t, is_ctx))

    def tile_pass2(l, b, t, is_ctx, nxt_tile=None):
        Hs = mk_halves(l, b, t, is_ctx)
        if (l, b, t, is_ctx) not in premod:
            pre_modulate(l, b, t, is_ctx)
        pend = ph_proj(Hs, l)
        wos = None
        for H in Hs:
            ph_gmlp(H)
            while pend:
                pend.pop(0)()
            attention(H)
            if wos is None:
                wo0, wo0t = w_get(l, P_WO0, la=2)
                wo1, wo1t = w_get(l, P_WO1, la=1)
                wos = [(wo0[:, :].rearrange("p (k c) -> p k c", k=8), wo0t), (wo1[:, :].rearrange("p (k c) -> p k c", k=8), wo1t)]
            ph_wout(H, wos)
        hook = None
        if nxt_tile is not None:
            hook = lambda: pre_modulate(*nxt_tile)
        ph_ffn(Hs, l, hook)

    stage = [hid[:, 0:4, :].rearrange("p a b -> p (a b)").bitcast(F32), hid[:, 4:8, :].rearrange("p a b -> p (a b)").bitcast(F32),
             hyT[:, 0:4, :].rearrange("p a b -> p (a b)").bitcast(F32), hyT[:, 4:8, :].rearrange("p a b -> p (a b)").bitcast(F32)]
    memset("dve", Vall[:, :, 1, :], 1.0, [T("Vall")])
    memset("dve", Vall[:, :, 4, :], 1.0, [T("Vall")])
    STG = [T("hid"), T("hid0"), T("hid1"), T("hyT0"), T("hyT1")]

    for s_ in range(nseq):
        if s_ > 0:
            P.new_epoch()
        for i in range(2 + 16):
            si = i % 4
            if i < 2:
                src = ctx_d[s_, 128 * i:128 * i + 128, :]
            else:
                src = x_d[s_, 128 * (i - 2):128 * (i - 1), :]
            dma("sp", stage[si], src, [], STG, T("hid"))
            for half_ in range(2):
                ab, abt = io_bank()

                def ftr(e, ab=ab, si=si, half_=half_):
                    ins = None
                    for c4 in range(4):
                        k = 4 * half_ + c4
                        ins = e.transpose(ab[:, 128 * c4:128 * c4 + 128], stage[si][:, 128 * k:128 * k + 128], ident_f[:, :])
                    return ins
                pe_fn(ftr, STG + [CT], [abt])
                if i < 2:
                    dst = xcT[:, 4 * half_:4 * half_ + 4, 128 * i:128 * i + 128]
                    dT = T("xcT")
                else:
                    tok = 128 * (i - 2)
                    dst = xT[:, 4 * half_:4 * half_ + 4, tok:tok + 128]
                    dT = T(f"xT{tok // 512}_{(tok % 512) // 256}")
                if half_ == 0:
                    act(dst, ab[:, 0:512].rearrange("p (c t) -> p c t", c=4), AF.Copy, [abt], [dT], scale=ALPHA)
                else:
                    ts("dve", dst, ab[:, 0:512].rearrange("p (c t) -> p c t", c=4), ALPHA, ALU.mult, [abt], [dT])
        for l in range(nlayers):
            if s_ == 0 and l == 1:
                pass
            dma("sp", bmlib[:, :, :, :].rearrange("p h d q -> p (h d q)"), bmsc_d[l], [BMC], [T("bmlib")], T("bmlib"))
            bg_drain()
            if l == 1:
                while castq:
                    castq.pop(0)()
            if s_ == 0 and l == 1:
                P.op("sp", None, writes=STG, nosig=True)
                mod_layer(1, [(stage[i].rearrange("p (k c) -> p k c", k=8), T(["stgm0", "stgm1", "hyT0", "hyT1"][i]), 128) for i in range(4)])
                derive_lnh2(1)
            wk, wkt = w_get(l, P_KV0, la=1)
            wv, wvt = w_get(l, P_KV1, la=1)
            pass1_all(l, s_, wk, wkt, wv, wvt)
            if s_ == 0 and l == 0 and nlayers > 1:
                emit_casts(1, castq)
            tiles = [(l, s_, t, False) for t in range(4)]
            if l < nlayers - 1:
                tiles.append((l, 4, 0, True))
            for ti, tl in enumerate(tiles):
                tile_pass2(*tl, nxt_tile=(tiles[ti + 1] if ti + 1 < len(tiles) else None))
        bg_drain()
        for i in range(16):
            si = i % 4
            tok = 128 * i
            for half_ in range(2):
                ab, abt = io_bank()

                def ftr2(e, ab=ab, tok=tok, half_=half_):
                    ins = None
                    for c4 in range(4):
                        k = 4 * half_ + c4
                        ins = e.transpose(ab[:, 128 * c4:128 * c4 + 128], xT[:, k, tok:tok + 128], ident_f[:, :])
                    return ins
                pe_fn(ftr2, [T(f"xT{tok // 512}_{(tok % 512) // 256}"), CT], [abt])
                cp("act" if half_ == 0 else "dve", stage[si][:, 512 * half_:512 * half_ + 512], ab[:, 0:512], [abt], STG)
            dma("sp", y_d[s_, tok:tok + 128, :], stage[si], STG, [], T("hid"))
    if dbg:
        pass
    P.op("sp", None, writes=STG, nosig=True)
    if "dbgsem" in tk:
        P.op("sp", None, writes=[T("dbgsem")], nosig=True)
    P.emit()
    return nc


_CACHE = {}


def kernel(**inputs):
    x = np.ascontiguousarray(inputs["x"], dtype=np.float32)
    B = x.shape[0]
    assert B == NCORES * BPC
    consts = host_consts(np.asarray(inputs["na_rpb"], dtype=np.float32))
    if "nc" not in _CACHE:
        _CACHE["nc"] = build_program(BPC, NL)
    nc = _CACHE["nc"]
    shared = {k: np.ascontiguousarray(np.asarray(inputs[k], dtype=np.float32)) for k in
              ["c_ctx", "w_mod", "b_mod", "w_in", "a_ln_g", "a_ln_b", "a_ws", "a_bs", "sw_sink", "w_out",
               "ln1_g", "ln1_b", "w1", "w2", "ln2_g", "ln2_b"]}
    shared.update(consts)
    in_maps = []
    for i in range(NCORES):
        m = dict(shared)
        m["x"] = x[BPC * i:BPC * (i + 1)]
        m["c"] = np.ascontiguousarray(np.asarray(inputs["c"], dtype=np.float32)[BPC * i:BPC * (i + 1)])
        m["ctx"] = np.ascontiguousarray(np.asarray(inputs["ctx"], dtype=np.float32)[BPC * i:BPC * (i + 1)])
        in_maps.append(m)
    res = run_bass_kernel_spmd(nc, in_maps, core_ids=list(range(NCORES)))
    out = np.concatenate([np.asarray(r["y"], dtype=np.float32) for r in res.results], axis=0)
    return out
```

```python
import numpy as np
import ml_dtypes
from contextlib import ExitStack
import concourse.bass as bass
import concourse.mybir as mybir
from concourse.bass_utils import run_bass_kernel_spmd

F32 = mybir.dt.float32
BF16 = mybir.dt.bfloat16
AF = mybir.ActivationFunctionType
ALU = mybir.AluOpType

D = 1024
KC = 8
S = 2048
CL = 256
STOT = S + CL
NL = 2
DFF = 4096
ALPHA = float((2 * NL) ** 0.25)
LN_EPS = 1e-5
NEG = -30000.0
NCORES = 8
BPC = 4
SAME_SYNC = True

P_KV0, P_KV1, P_Q0, P_Q1, P_Q2, P_WO0, P_WO1 = 0, 1, 2, 3, 4, 5, 6
P_W1 = 7
P_W2 = 15
NPIECE = 23


class Tok:
    __slots__ = ("name", "w", "r", "sem", "cnt", "excl")

    def __init__(self, name, excl=False):
        self.name = name
        self.w = {}
        self.r = {}
        self.sem = None
        self.cnt = 0
        self.excl = excl


class Prog:
    ENGS = ("pe", "act", "dve", "pool", "sp")

    def __init__(self, nc, es):
        self.nc = nc
        self.es = es
        self.ops = {e: [] for e in self.ENGS}
        self.seen = {e: {} for e in self.ENGS}
        self.cur = {}
        self.cnt = {}
        self.nsem = 0
        self.new_epoch()

    def alloc_sem(self, name):
        self.nsem += 1
        return self.es.enter_context(self.nc.semaphore(f"{name}_{self.nsem}"))

    def new_epoch(self):
        for e in ("pe", "act", "dve", "pool"):
            self.cur[e] = self.alloc_sem("e" + e)
            self.cnt[e] = 0

    def op(self, eng, fn, reads=(), writes=(), dma=None, nosig=False):
        deps = {}

        def add(rec):
            sem, val, oeng, isdma = rec
            if (not isdma) and oeng == eng and (eng == "pe" or not SAME_SYNC):
                return
            k = id(sem)
            if k not in deps or deps[k][1] < val:
                deps[k] = (sem, val)

        for t in reads:
            for rec in t.w.values():
                add(rec)
            if t.excl:
                for rec in t.r.values():
                    add(rec)
        for t in writes:
            for rec in t.w.values():
                add(rec)
            for rec in t.r.values():
                add(rec)
        waits = []
        sn = self.seen[eng]
        for (s, v) in deps.values():
            if sn.get(id(s), 0) < v:
                waits.append((s, v))
                sn[id(s)] = v
        if nosig:
            self.ops[eng].append((fn, waits, None, 0))
            return
        if dma is not None:
            if dma.sem is None:
                dma.sem = self.alloc_sem("d" + dma.name)
            dma.cnt += 1
            sig = (dma.sem, 16 * dma.cnt)
            key = ("dma", id(dma.sem))
            rec = (sig[0], sig[1], eng, True)
            inc = 16
        else:
            self.cnt[eng] += 1
            sig = (self.cur[eng], self.cnt[eng])
            key = eng
            rec = (sig[0], sig[1], eng, False)
            inc = 1
        for t in reads:
            (t.w if t.excl else t.r)[key] = rec
        for t in writes:
            t.w[key] = rec
        self.ops[eng].append((fn, waits, sig, inc))

    def emit(self):
        with self.nc.Block() as block:
            def mk(name):
                def body(e):
                    for fn, waits, sig, inc in self.ops[name]:
                        for s, v in waits:
                            e.wait_ge(s, v)
                        if fn is None:
                            continue
                        ins = fn(e)
                        if sig is not None:
                            ins.then_inc(sig[0], inc)
                return body
            block.tensor(mk("pe"))
            block.scalar(mk("act"))
            block.vector(mk("dve"))
            block.gpsimd(mk("pool"))
            block.sync(mk("sp"))


def host_consts(na_rpb):
    c = {}
    c["ident_f"] = np.eye(128, dtype=np.float32)
    c["ident_b"] = np.eye(128, dtype=np.float32).astype(ml_dtypes.bfloat16)
    Pm = np.zeros((128, 128), np.float32)
    for m in range(128):
        if (m % 32) < 16:
            Pm[m, m + 16] = -1.0
        else:
            Pm[m, m - 16] = 1.0
    c["prope"] = np.ascontiguousarray(Pm.T).astype(ml_dtypes.bfloat16)
    inv_freq = (np.float32(10000.0) ** (-np.arange(16, dtype=np.float32) / np.float32(16))).astype(np.float32)
    pos = np.arange(S)
    row = (pos // 64).astype(np.float32)
    col = (pos % 64).astype(np.float32)
    cs = np.zeros((128, 2, S), np.float32)
    for p in range(128):
        d = p % 64
        if d < 32:
            ang = row * inv_freq[d % 16]
        else:
            ang = col * inv_freq[(d - 32) % 16]
        ang = ang.astype(np.float32)
        cs[p, 0] = np.cos(ang)
        cs[p, 1] = np.sin(ang)
    c["cs"] = cs
    ki = np.arange(128)[:, None]
    qi = np.arange(128)[None, :]
    msk = np.zeros((128, 2, 128), np.float32)
    msk[:, 0, :] = np.where(qi <= ki, 0.0, NEG)
    msk[:, 1, :] = np.where(ki <= qi, 0.0, NEG)
    c["swmask"] = msk.astype(ml_dtypes.bfloat16)
    cq = np.arange(64)
    cst = np.clip(cq - 8, 0, 48)
    col_ok = (cq[None, :] >= cst[:, None]) & (cq[None, :] < cst[:, None] + 16)
    dc = np.clip(cq[None, :] - cq[:, None], -15, 15) + 15
    Tb = na_rpb[:, :, :, dc]
    Tb = np.where(col_ok[None, None, None], Tb, np.float32(NEG)).astype(np.float32)
    Tb = Tb.transpose(0, 1, 2, 4, 3)
    lib = np.empty((NL, 2, 64, 4, 14, 64), np.float32)
    for dr0 in range(14):
        lib[:, 0, :, :, dr0, :] = Tb[:, :, dr0].transpose(0, 2, 1, 3)
        lib[:, 1, :, :, dr0, :] = Tb[:, :, dr0 + 1].transpose(0, 2, 1, 3)
    c["bmlib"] = np.ascontiguousarray(lib.reshape(NL, 128, 4 * 14 * 64))
    return c


def build_program(nseq=BPC, nlayers=NL, dbg=None):
    nc = bass.Bass("TRN2", target_bir_lowering=False)
    es = ExitStack()
    P = Prog(nc, es)

    def din(name, shape, dt=F32):
        return nc.dram_tensor(name, list(shape), dt, kind="ExternalInput").ap()

    x_d = din("x", [nseq, S, D])
    c_d = din("c", [nseq, D])
    ctx_d = din("ctx", [nseq, CL, D])
    cctx_d = din("c_ctx", [D])
    wmod_d = din("w_mod", [NL, D, 6 * D])
    bmod_d = din("b_mod", [NL, 6 * D])
    win_d = din("w_in", [NL, D, 2048])
    alng_d = din("a_ln_g", [NL, 256])
    alnb_d = din("a_ln_b", [NL, 256])
    aws_d = din("a_ws", [NL, 4, 128, 128])
    abs_d = din("a_bs", [NL, 4, 128])
    sink_d = din("sw_sink", [NL, 8])
    wout_d = din("w_out", [NL, D, D])
    ln1g_d = din("ln1_g", [NL, D])
    ln1b_d = din("ln1_b", [NL, D])
    w1_d = din("w1", [NL, D, DFF])
    w2_d = din("w2", [NL, DFF, D])
    ln2g_d = din("ln2_g", [NL, D])
    ln2b_d = din("ln2_b", [NL, D])
    identf_d = din("ident_f", [128, 128])
    identb_d = din("ident_b", [128, 128], BF16)
    prope_d = din("prope", [128, 128], BF16)
    cs_d = din("cs", [128, 2, S])
    swmask_d = din("swmask", [128, 2, 128], BF16)
    bmlib_d = din("bmlib", [NL, 128, 3584])
    y_d = nc.dram_tensor("y", [nseq, S, D], F32, kind="ExternalOutput").ap()
    wsc_d = nc.dram_tensor("wsc", [NL, NPIECE, 128, 4096], BF16, kind="Internal").ap()
    bmsc_d = nc.dram_tensor("bmsc", [NL, 128, 3584], BF16, kind="Internal").ap()
    dbg_out = {}
    if dbg:
        for name, shape in dbg.items():
            dbg_out[name] = nc.dram_tensor("dbg_" + name, list(shape), F32, kind="ExternalOutput").ap()

    def sb(name, shape, dt):
        return es.enter_context(nc.sbuf_tensor(name, list(shape), dt))

    xT = sb("xT", [128, KC, S], F32)
    xcT = sb("xcT", [128, KC, CL], F32)
    kTna = sb("kTna", [128, 2, STOT], BF16)
    kTsw = sb("kTsw", [128, 2, STOT], BF16)
    Vall = sb("Vall", [128, 18, 8, 64], BF16)
    wsl = [sb(f"wsl{i}", [128, 4096], BF16) for i in range(3)]
    hyT = sb("hyT", [128, KC, 512], BF16)
    hid = sb("hid", [128, 8, 512], BF16)
    uT = sb("uT", [128, 2, 512], BF16)
    qTna = sb("qTna", [128, 2, 512], BF16)
    qTsw = sb("qTsw", [128, 4, 512], BF16)
    qb = [sb(f"qb{i}", [128, 256], BF16) for i in range(2)]
    vln = sb("vln", [128, 4, 256], BF16)
    PT = [sb(f"PT{i}", [128, 640], BF16) for i in range(3)]
    tmpF = [sb(f"tmpF{i}", [128, 256], F32) for i in range(3)]
    zbs = [sb(f"zb{i}", [128, 256], BF16) for i in range(2)]
    zsqs = [sb(f"zsq{i}", [128, 256], BF16) for i in range(2)]
    st_mean = sb("st_mean", [128, 512], F32)
    st_rstd = sb("st_rstd", [128, 512], F32)
    st_nmr = sb("st_nmr", [128, 512], F32)
    csb = sb("csb", [128, 2, 512], F32)
    bmlib = sb("bmlib_sb", [128, 4, 14, 64], BF16)
    Bb = sb("Bb", [128, NL, 2, 128], F32)
    WsT = sb("WsT", [128, NL, 4, 128], BF16)
    alnG = sb("alnG", [128, NL, 256], F32)
    alnB = sb("alnB", [128, NL, 256], F32)
    modT = sb("modT", [128, NL, 48, 5], F32)
    lnc = sb("lnc", [128, NL, 4, KC], F32)
    es_t = sb("es_t", [128, NL, 4], F32)
    lnh2 = sb("lnh2", [128, NL, 5, 2, KC], F32)
    lnt = [sb(f"lnt{i}", [128, 256], F32) for i in range(3)]
    TA = sb("TA", [128, 128], F32)
    TB = sb("TB", [128, 72], F32)
    rowsA = sb("rowsA", [128, 128], F32)
    rowsB = sb("rowsB", [72, 128], F32)
    csT = sb("csT", [128, 5, 8], F32)
    ones_f = sb("ones_f", [1, 128], F32)
    ident_f = sb("ident_f_sb", [128, 128], F32)
    ident_b = sb("ident_b_sb", [128, 128], BF16)
    prope = sb("prope_sb", [128, 128], BF16)
    swmask = sb("swmask_sb", [128, 2, 128], BF16)
    ones_b = sb("ones_b", [128, 128], BF16)
    mv = sb("mv", [128, 4, 8], F32)
    epsc = sb("epsc", [128, 1], F32)
    vg = [sb(f"vg{i}", [128, 256], F32) for i in range(2)]
    mvr = sb("mvr", [128, 4], F32)

    ps = [es.enter_context(nc.psum_tensor(f"ps{i}", [128, 512], F32)) for i in range(8)]
    psT = [Tok(f"ps{i}", excl=True) for i in range(8)]

    tk = {}

    def T(name):
        if name not in tk:
            tk[name] = Tok(name)
        return tk[name]

    rr = {}

    def nxt(name, n):
        rr[name] = (rr.get(name, -1) + 1) % n
        return rr[name]

    def acc_bank():
        i = nxt("acc", 2)
        return ps[i], psT[i]

    def s_bank():
        i = 2 + nxt("sb", 2)
        return ps[i], psT[i]

    def o_bank():
        i = 4 + nxt("ob", 2)
        return ps[i], psT[i]

    def aux_bank():
        i = 6 + nxt("aux", 2)
        return ps[i], psT[i]

    def io_bank():
        i = nxt("iob", 8)
        return ps[i], psT[i]

    def tmp():
        i = nxt("tmpF", 3)
        return tmpF[i], T(f"tmpF{i}")

    def pt():
        i = nxt("PT", 3)
        return PT[i], T(f"PT{i}")

    def dma(eng, out, in_, reads, writes, tok):
        P.op(eng, lambda e: e.dma_start(out=out, in_=in_), reads=reads, writes=writes, dma=tok)

    def mm_group(out, pairs, reads, writes):
        n = len(pairs)

        def fn(e):
            ins = None
            for i, (l, r) in enumerate(pairs):
                ins = e.matmul(out, lhsT=l, rhs=r, start=(i == 0), stop=(i == n - 1))
            return ins
        P.op("pe", fn, reads=reads, writes=writes)

    def pe_fn(fn, reads, writes):
        P.op("pe", fn, reads=reads, writes=writes)

    def act(out, in_, func, reads, writes, bias=None, scale=None):
        kw = {}
        if bias is not None:
            kw["bias"] = bias
        if scale is not None:
            kw["scale"] = scale
            if func == AF.Copy:
                func = AF.Identity
        P.op("act", lambda e: e.activation(out=out, in_=in_, func=func, **kw), reads=reads, writes=writes)

    def tt(eng, out, in0, in1, op, reads, writes):
        P.op(eng, lambda e: e.tensor_tensor(out=out, in0=in0, in1=in1, op=op), reads=reads, writes=writes)

    def ts(eng, out, in0, s1, op0, reads, writes, s2=None, op1=None):
        if op1 is None:
            P.op(eng, lambda e: e.tensor_scalar(out=out, in0=in0, scalar1=s1, scalar2=None, op0=op0), reads=reads, writes=writes)
        else:
            P.op(eng, lambda e: e.tensor_scalar(out=out, in0=in0, scalar1=s1, scalar2=s2, op0=op0, op1=op1), reads=reads, writes=writes)

    def stt(eng, out, in0, scalar, in1, op0, op1, reads, writes):
        P.op(eng, lambda e: e.scalar_tensor_tensor(out=out, in0=in0, scalar=scalar, in1=in1, op0=op0, op1=op1),
             reads=reads, writes=writes)

    def cp(eng, out, in_, reads, writes):
        if eng == "act":
            act(out, in_, AF.Copy, reads, writes)
        else:
            P.op(eng, lambda e: e.tensor_copy(out=out, in_=in_), reads=reads, writes=writes)

    def recip(out, in_, reads, writes):
        P.op("dve", lambda e: e.reciprocal(out=out, in_=in_), reads=reads, writes=writes)

    def memset(eng, ap, val, writes):
        P.op(eng, lambda e: e.memset(ap, val), writes=writes)

    def dbg_dump(name, ap, tok):
        if name in dbg_out:
            dma("sp", dbg_out[name], ap, [tok], [], T("dbgsem"))

    castT = {}

    def cast_tok(l, g):
        k = (l, g)
        if k not in castT:
            castT[k] = Tok(f"cast{l}_{g}")
        return castT[k]

    def piece_group(pi):
        if pi <= P_Q2:
            return 0
        if pi <= P_WO1:
            return 1
        if pi < P_W2:
            return 2
        return 3

    castq = []

    def emit_casts(l, sink=None):
        def c(pi, dst, src):
            t = cast_tok(l, piece_group(pi))
            if sink is None:
                dma("pool", dst, src, [], [t], t)
            else:
                sink.append(lambda: dma("pool", dst, src, [], [t], t))
        winv = win_d[l].rearrange("(k p) c -> p k c", p=128)

        def pv(pi, ncols):
            return wsc_d[l, pi][:, 0:8 * ncols].rearrange("p (k c) -> p k c", k=8)
        kv0 = pv(P_KV0, 512)
        c(P_KV0, kv0[:, :, 0:256], winv[:, :, 768:1024])
        c(P_KV0, kv0[:, :, 256:320], winv[:, :, 1792:1856])
        c(P_KV0, kv0[:, :, 320:384], winv[:, :, 1792:1856])
        c(P_KV0, kv0[:, :, 384:448], winv[:, :, 1856:1920])
        c(P_KV0, kv0[:, :, 448:512], winv[:, :, 1856:1920])
        kv1 = pv(P_KV1, 384)
        c(P_KV1, kv1[:, :, 0:256], winv[:, :, 1024:1280])
        c(P_KV1, kv1[:, :, 256:384], winv[:, :, 1920:2048])
        q0 = pv(P_Q0, 512)
        c(P_Q0, q0[:, :, 0:256], winv[:, :, 0:256])
        c(P_Q0, q0[:, :, 256:512], winv[:, :, 512:768])
        c(P_Q1, pv(P_Q1, 512), winv[:, :, 1280:1792])
        c(P_Q2, pv(P_Q2, 256), winv[:, :, 256:512])
        wov = wout_d[l].rearrange("(k p) c -> p k c", p=128)
        c(P_WO0, pv(P_WO0, 512), wov[:, :, 0:512])
        c(P_WO1, pv(P_WO1, 512), wov[:, :, 512:1024])
        w1v = w1_d[l].rearrange("(k p) c -> p k c", p=128)
        for i in range(8):
            c(P_W1 + i, pv(P_W1 + i, 512), w1v[:, :, 512 * i:512 * (i + 1)])
        w2v = w2_d[l].rearrange("(j p) d -> p j d", p=128)
        for qq in range(4):
            for hf in range(2):
                pi = P_W2 + qq * 2 + hf
                c(pi, pv(pi, 512), w2v[:, 8 * qq:8 * qq + 8, 512 * hf:512 * hf + 512])

    piece_used = {P_KV0: 4096, P_KV1: 3072, P_Q0: 4096, P_Q1: 4096, P_Q2: 2048, P_WO0: 4096, P_WO1: 4096}
    wseq = []
    wstate = {"issued": 0, "next": 0}

    def w_issue_upto(n):
        while wstate["issued"] < min(n, len(wseq)):
            i = wstate["issued"]
            l, pi = wseq[i]
            si = i % 3
            used = piece_used.get(pi, 4096)
            dma("sp", wsl[si][:, 0:used], wsc_d[l, pi][:, 0:used], [cast_tok(l, piece_group(pi))], [T(f"wsl{si}")], T(f"wsl{si}"))
            wstate["issued"] += 1

    def w_get(l, pi, la=2):
        i = wstate["next"]
        assert wseq[i] == (l, pi), (wseq[i], l, pi)
        w_issue_upto(i + 1 + la)
        wstate["next"] += 1
        si = i % 3
        return wsl[si], T(f"wsl{si}")

    CT = T("consts")
    for (dst, src) in [(ident_f[:, :], identf_d), (ident_b[:, :], identb_d), (prope[:, :], prope_d),
                       (swmask[:, :, :], swmask_d)]:
        dma("sp", dst, src, [], [CT], CT)
    dma("sp", rowsA[0:96, :], bmod_d.rearrange("l (j p) -> (l j) p", p=128), [], [CT], CT)
    for l in range(NL):
        dma("sp", rowsA[96 + 16 * l:104 + 16 * l, :], ln1g_d[l].rearrange("(k p) -> k p", p=128), [], [CT], CT)
        dma("sp", rowsA[104 + 16 * l:112 + 16 * l, :], ln1b_d[l].rearrange("(k p) -> k p", p=128), [], [CT], CT)
        dma("sp", rowsB[16 * l:16 * l + 8, :], ln2g_d[l].rearrange("(k p) -> k p", p=128), [], [CT], CT)
        dma("sp", rowsB[16 * l + 8:16 * l + 16, :], ln2b_d[l].rearrange("(k p) -> k p", p=128), [], [CT], CT)
    memset("dve", rowsB[32:64, :], 0.0, [CT])
    dma("sp", rowsB[32:32 + 8 * nseq, :], c_d.rearrange("b (k p) -> (b k) p", p=128), [], [CT], CT)
    dma("sp", rowsB[64:72, :], cctx_d.rearrange("(k p) -> k p", p=128), [], [CT], CT)
    rowbuf = {(0, 0): (st_mean, "st_mean"), (0, 1): (st_rstd, "st_rstd"), (1, 0): (st_nmr, "st_nmr"), (1, 1): (csb[:, 0, :], "csb")}
    wsraw = csb[:, 1, :].rearrange("p (g j) -> p g j", g=4)
    for l in range(NL):
        r0, r0n = rowbuf[(l, 0)]
        r1, r1n = rowbuf[(l, 1)]
        dma("sp", r0[0:1, 0:256], alng_d[l:l + 1, :], [], [T(r0n)], T(r0n))
        dma("sp", r0[0:1, 256:512], alnb_d[l:l + 1, :], [], [T(r0n)], T(r0n))
        dma("sp", r1[0:1, 0:512], abs_d[l:l + 1].rearrange("o g i -> o (g i)"), [], [T(r1n)], T(r1n))
    sinkr = sb("sinkr", [1, 16], F32)
    dma("sp", sinkr[0:1, 0:16], sink_d.rearrange("(o l) h -> o (l h)", o=1), [], [CT], CT)
    memset("dve", ones_f[:, :], 1.0, [T("ones_f")])
    memset("dve", ones_b[:, :], 1.0, [T("ones_b")])
    memset("dve", epsc[:, :], LN_EPS, [T("epsc")])

    BMC = Tok("bmcast")
    for l in range(NL):
        dma("pool", bmsc_d[l], bmlib_d[l], [], [BMC], BMC)
    emit_casts(0)

    b0, bt0 = aux_bank()
    pe_fn(lambda e: e.transpose(b0[:, 0:128], rowsA[:, :], ident_f[:, :]), [CT], [bt0])
    cp("dve", TA[:, :], b0[:, 0:128], [bt0], [T("TA")])
    b1, bt1 = aux_bank()
    pe_fn(lambda e: e.transpose(b1[:, 0:72], rowsB[0:72, :], ident_f[0:72, 0:72]), [CT], [bt1])
    cp("dve", TB[:, :], b1[:, 0:72], [bt1], [T("TB")])
    act(csT[:, :, :], TB[:, 32:72].rearrange("p (b k) -> p b k", k=8), AF.Silu, [T("TB")], [T("csT")])

    def mod_layer(l, bufs):
        mb_, mbt = aux_bank()
        j = 0
        bi = 0
        while j < 48:
            ap, tok_, ncol = bufs[bi % len(bufs)]
            bi += 1
            nj = ncol // 128
            dma("sp", ap, wmod_d[l].rearrange("(k p) c -> p k c", p=128)[:, :, 128 * j:128 * j + ncol], [], [tok_], tok_)
            for jj in range(nj):
                mm_group(mb_[:, 5 * (j + jj):5 * (j + jj) + 5],
                         [(ap[:, k, 128 * jj:128 * (jj + 1)], csT[:, :, k]) for k in range(8)],
                         [tok_, T("csT")], [mbt])
            j += nj
        for b in range(5):
            tt("dve", modT[:, l, :, b], mb_[:, 0:240].rearrange("p (j b) -> p j b", b=5)[:, :, b],
               TA[:, 48 * l:48 * (l + 1)], ALU.add, [mbt, T("TA")], [T("modT")])
        for kind in (1, 4):
            ts("dve", modT[:, l, 8 * kind:8 * kind + 8, :], modT[:, l, 8 * kind:8 * kind + 8, :], 1.0, ALU.add,
               [T("modT")], [T("modT")], s2=1.0 / ALPHA, op1=ALU.mult)

    def derive_lnh2(l):
        for b in range(5):
            tt("dve", lnh2[:, l, b, 0, :], lnc[:, l, 0, :], modT[:, l, 32:40, b], ALU.mult, [T("lnc"), T("modT")], [T("lnh2")])
            tt("dve", lnh2[:, l, b, 1, :], lnc[:, l, 1, :], modT[:, l, 32:40, b], ALU.mult, [T("lnc"), T("modT")], [T("lnh2")])
            tt("dve", lnh2[:, l, b, 1, :], lnh2[:, l, b, 1, :], modT[:, l, 24:32, b], ALU.add, [T("lnh2"), T("modT")], [T("lnh2")])

    mod_layer(0, [(wsl[i][:, :].bitcast(F32).rearrange("p (k c) -> p k c", k=8), T(f"wsl{i}"), 256) for i in range(3)])
    for l in range(NL):
        a2 = ALPHA if l < nlayers - 1 else 1.0
        ts("dve", lnc[:, l, 0, :], TA[:, 96 + 16 * l:104 + 16 * l], ALPHA, ALU.mult, [T("TA")], [T("lnc")])
        ts("dve", lnc[:, l, 1, :], TA[:, 104 + 16 * l:112 + 16 * l], ALPHA, ALU.mult, [T("TA")], [T("lnc")])
        ts("dve", lnc[:, l, 2, :], TB[:, 16 * l:16 * l + 8], a2, ALU.mult, [T("TB")], [T("lnc")])
        ts("dve", lnc[:, l, 3, :], TB[:, 16 * l + 8:16 * l + 16], a2, ALU.mult, [T("TB")], [T("lnc")])
    derive_lnh2(0)
    for l in range(NL):
        bb_, bbt = aux_bank()
        r0, r0n = rowbuf[(l, 0)]
        r1, r1n = rowbuf[(l, 1)]
        mm_group(bb_[:, 0:512], [(ones_f[0:1, :], r0[0:1, 0:512])], [T(r0n), T("ones_f")], [bbt])
        cp("dve", alnG[:, l, :], bb_[:, 0:256], [bbt], [T("alnGB")])
        cp("dve", alnB[:, l, :], bb_[:, 256:512], [bbt], [T("alnGB")])
        b2_, b2t = aux_bank()

        def fbb(e, l=l, b2_=b2_, r1=r1):
            ins = None
            for m in range(2):
                for hh in range(2):
                    g = 2 * m + hh
                    ins = e.matmul(b2_[64 * hh:64 * hh + 64, 128 * m:128 * m + 128], lhsT=ones_f[0:1, 0:64],
                                   rhs=r1[0:1, 128 * g:128 * g + 128],
                                   start=True, stop=True)
            return ins
        pe_fn(fbb, [T(r1n), T("ones_f")], [b2t])
        cp("dve", Bb[:, l, :, :], b2_[:, 0:256].rearrange("p (m i) -> p m i", m=2), [b2t], [T("Bb")])
        b3_, b3t = aux_bank()

        def fes(e, l=l, b3_=b3_):
            sr = sinkr[0:1, 8 * l:8 * l + 8].rearrange("o (i t) -> o i t", t=2)
            e.matmul(b3_[64:128, 0:4], lhsT=ones_f[0:1, 0:64], rhs=sr[:, :, 0], start=True, stop=True)
            return e.matmul(b3_[0:64, 0:4], lhsT=ones_f[0:1, 0:64], rhs=sr[:, :, 1], start=True, stop=True)
        pe_fn(fes, [CT, T("ones_f")], [b3t])
        act(es_t[:, l, :], b3_[:, 0:4], AF.Exp, [b3t], [T("es_t")])
        dma("sp", wsraw, aws_d[l].rearrange("g i j -> i g j"), [], [T("wsraw")], T("wsraw"))
        b4_, b4t = aux_bank()

        def fws(e, b4_=b4_):
            ins = None
            for g in range(4):
                ins = e.transpose(b4_[:, 128 * g:128 * g + 128], wsraw[:, g, :], ident_f[:, :])
            return ins
        pe_fn(fws, [T("wsraw"), CT], [b4t])
        cp("dve", WsT[:, l, :, :], b4_[:, 0:512].rearrange("p (g i) -> p g i", g=4), [b4t], [T("WsT")])

    def tile_pieces(l, nh):
        out = [(l, P_Q2), (l, P_Q0), (l, P_Q1), (l, P_WO0), (l, P_WO1)]
        for qq in range(4):
            q4 = [(l, P_W1 + 2 * qq), (l, P_W1 + 2 * qq + 1), (l, P_W2 + 2 * qq), (l, P_W2 + 2 * qq + 1)]
            out += q4
            if qq == 0 and nh == 2:
                out += q4
        return out
    for s_ in range(nseq):
        for l in range(nlayers):
            wseq.extend([(l, P_KV0), (l, P_KV1)])
            for _ in range(4):
                wseq.extend(tile_pieces(l, 2))
            if l < nlayers - 1:
                wseq.extend(tile_pieces(l, 1))

    HY = [T("hyT0"), T("hyT1")]

    def mcol(l, kind, k, b):
        return modT[:, l, 8 * kind + k, b:b + 1]

    def modulate(src_of_k, srcTs, l, kind_sh, kind_s, b, c0, Tn, hyts):
        for k in range(KC):
            act(hyT[:, k, c0:c0 + Tn], src_of_k(k), AF.Identity, list(srcTs) + [T("modT")], hyts,
                bias=mcol(l, kind_sh, k, b), scale=mcol(l, kind_s, k, b))

    def rope_evac(acc, acct, dst, dstT, cc0, Tn, scale):
        i = nxt("qb", 2)
        q_b, q_bt = qb[i], T(f"qb{i}")
        act(q_b[:, 0:Tn], acc[:, 0:Tn], AF.Copy, [acct], [q_bt], scale=scale)

        def rest():
            ab, abt = aux_bank()
            mm_group(ab[:, 0:Tn], [(prope[:, :], q_b[:, 0:Tn])], [CT, q_bt], [abt])
            t1, t1t = tmp()
            stt("dve", t1[:, 0:Tn], acc[:, 0:Tn], scale, csb[:, 0, cc0:cc0 + Tn], ALU.mult, ALU.mult, [acct, T("csb")], [t1t])
            t2, t2t = tmp()
            tt("dve", t2[:, 0:Tn], ab[:, 0:Tn], csb[:, 1, cc0:cc0 + Tn], ALU.mult, [abt, T("csb")], [t2t])
            tt("pool", dst, t1[:, 0:Tn], t2[:, 0:Tn], ALU.add, [t1t, t2t], [dstT])
        return rest

    def load_cs(tok0, Tn):
        dma("sp", csb[:, :, 0:Tn], cs_d[:, :, tok0:tok0 + Tn], [], [T("csb")], T("csb"))

    def pass1_all(l, b, wk, wkt, wv, wvt):
        wk3 = wk[:, :].rearrange("p (k c) -> p k c", k=8)
        wv3 = wv[:, 0:3072].rearrange("p (k c) -> p k c", k=8)
        groups = []
        for g in range(8):
            t, h = g // 2, g % 2
            groups.append(dict(src=(lambda k, g=g: xT[:, k, 256 * g:256 * g + 256]), srcT=[T(f"xT{t}_{h}")], tok0=256 * g,
                               is_ctx=False, b=b))
        groups.append(dict(src=(lambda k: xcT[:, k, :]), srcT=[T("xcT")], tok0=S, is_ctx=True, b=4))

        def mod(gi):
            g = groups[gi]
            hh = gi % 2
            modulate(g["src"], g["srcT"], l, 0, 1, g["b"], 256 * hh, 256, [HY[hh]])
        mod(0)
        pend = []
        for gi, g in enumerate(groups):
            hh = gi % 2
            c0 = 256 * hh
            tok0 = g["tok0"]
            if gi + 1 < len(groups):
                mod(gi + 1)
            if (not g["is_ctx"]) and (tok0 % 512 == 0):
                load_cs(tok0, 512)
            for ch in range(4):
                acc, acct = acc_bank()
                mm_group(acc[:, 0:256], [(wk3[:, k, 128 * ch:128 * ch + 128], hyT[:, k, c0:c0 + 256]) for k in range(KC)],
                         [wkt, HY[hh]], [acct])
                while pend:
                    pend.pop(0)()
                if ch < 2:
                    cp("act", kTna[:, ch, tok0:tok0 + 256], acc[:, 0:256], [acct], [T("kTna")])
                elif g["is_ctx"]:
                    cp("act", kTsw[:, ch - 2, tok0:tok0 + 256], acc[:, 0:256], [acct], [T("kTsw")])
                else:
                    pend.append(rope_evac(acc, acct, kTsw[:, ch - 2, tok0:tok0 + 256], T("kTsw"), tok0 % 512, 256, 1.0))
            for tt_ in range(2):
                acc, acct = acc_bank()
                mm_group(acc[:, 0:384], [(hyT[:, k, c0 + 128 * tt_:c0 + 128 * tt_ + 128], wv3[:, k, :]) for k in range(KC)],
                         [wvt, HY[hh]], [acct])
                while pend:
                    pend.pop(0)()
                vt_ = tok0 // 128 + tt_
                a3 = acc[:, 0:384].rearrange("p (b d) -> p b d", d=64)
                cp("dve", Vall[:, vt_, 0:3:2, :], a3[:, 0:2, :], [acct], [T("Vall")])
                cp("dve", Vall[:, vt_, 3:6:2, :], a3[:, 2:4, :], [acct], [T("Vall")])
                cp("dve", Vall[:, vt_, 6:8, :], a3[:, 4:6, :], [acct], [T("Vall")])

    class Half:
        def __init__(self, l, b, t, h, is_ctx):
            self.l, self.b, self.t, self.h, self.is_ctx = l, b, t, h, is_ctx
            self.Tn = 256
            self.c0 = 0 if is_ctx else 256 * h
            self.tok0 = 0 if is_ctx else 512 * t + 256 * h
            self.cs = slice(self.c0, self.c0 + 256)
            self.xTok = T("xcT") if is_ctx else T(f"xT{t}_{h}")
            self.hy = T(f"hyT{h}")
            self.uTt = T(f"uT{h}")
            self.qnat = T(f"qTna{h}")
            self.qswt = T(f"qTsw{h}")
            self.vlnt = T(f"vln{h}")
            self.hidt = T(f"hid{h}")
            self.mvt = T(f"mv{h}")
            self.stt_ = T(f"st{h}")
            sb_ = (6, 7) if h == 0 else (4, 5)
            self.s1, self.s1t, self.s2, self.s2t = ps[sb_[0]], psT[sb_[0]], ps[sb_[1]], psT[sb_[1]]

        def xs(self, k):
            if self.is_ctx:
                return xcT[:, k, :]
            return xT[:, k, self.tok0:self.tok0 + 256]

    def finish_head(ob, obt, hp, ncols, dst, dstT, l, es_pair):
        dp = 1 - hp
        dsl = slice(64 * dp, 64 * dp + 64)
        osl = slice(64 * hp, 64 * hp + 64)
        t1, t1t = tmp()
        if es_pair is not None:
            ts("dve", t1[dsl, 0:ncols], ob[dsl, 0:ncols], es_t[dsl, l, es_pair:es_pair + 1], ALU.add,
               [obt, T("es_t")], [t1t])
            recip(t1[dsl, 0:ncols], t1[dsl, 0:ncols], [t1t], [t1t])
        else:
            recip(t1[dsl, 0:ncols], ob[dsl, 0:ncols], [obt], [t1t])
        tt("dve", dst, ob[osl, 0:ncols], t1[dsl, 0:ncols], ALU.mult, [obt, t1t], [dstT])

    def pv_group(e, ob, col0, n, hp, slots):
        ns = len(slots)
        ins = None
        for i, (vl, pr, ksl) in enumerate(slots):
            e.matmul(ob[64 * hp:64 * hp + 64, col0:col0 + n], lhsT=vl, rhs=pr, start=(i == 0), stop=(i == ns - 1))
            ins = e.matmul(ob[64 * (1 - hp):64 * (1 - hp) + 64, col0:col0 + n], lhsT=ones_b[ksl, 0:64], rhs=pr,
                           start=(i == 0), stop=(i == ns - 1))
        return ins

    def run_items(items):
        prev = None
        for it in items:
            it["qk"]()
            bg_step(1)
            it["ex"]()
            if prev is not None:
                prev["pv"]()
            bg_step(1)
            if castq:
                castq.pop(0)()
            prev = it
        prev["pv"]()

    def attention(H):
        l, t = H.l, H.t
        obs = {}

        def get_ob(key):
            if key not in obs:
                obs[key] = o_bank()
            return obs[key]
        items = []

        def mk_dense(kind, h):
            hp, hc = h % 2, h // 2
            psl = slice(64 * hp, 64 * hp + 64)
            if kind == "na":
                kT_, kTt, kch, qT_, qTt, vc0, ych, esp = kTna, T("kTna"), hc, qTna, H.qnat, [0, 2, 3, 5][h], 2 + hc, None
            else:
                kv = h // 4
                kT_, kTt, kch, qT_, qTt, vc0, ych, esp = kTsw, T("kTsw"), kv, qTsw, H.qswt, 6 + kv, 4 + hc, hc
            st = {}

            def qk():
                sbk, sbt = s_bank()
                st["s"] = (sbk, sbt)

                def f(e):
                    ins = None
                    for ci in range(2):
                        ins = e.matmul(sbk[:, 256 * ci:256 * ci + 256], lhsT=kT_[psl, kch, S + 128 * ci:S + 128 * ci + 128],
                                       rhs=qT_[psl, hc, H.cs], start=True, stop=True)
                    return ins
                pe_fn(f, [kTt, qTt], [sbt])

            def ex():
                sbk, sbt = st["s"]
                p_, p_t = pt()
                st["p"] = (p_, p_t)
                act(p_[:, 0:512], sbk[:, 0:512], AF.Exp, [sbt], [p_t])

            def pv():
                p_, p_t = st["p"]
                ob, obt = get_ob((kind, h))
                pe_fn(lambda e: pv_group(e, ob, 0, 256, hp,
                                         [(Vall[:, 16 + ci, vc0, :], p_[:, 256 * ci:256 * ci + 256], slice(0, 128)) for ci in range(2)]),
                      [p_t, T("Vall"), T("ones_b")], [obt])
                finish_head(ob, obt, hp, 256, hyT[psl, ych, H.cs], H.hy, l, esp)
            return {"qk": qk, "ex": ex, "pv": pv}

        def mk_na(h, rr_):
            hp, hc = h % 2, h // 2
            psl = slice(64 * hp, 64 * hp + 64)
            r = 8 * t + 4 * H.h + rr_
            rs = min(max(r - 4, 0), 24)
            p = rs % 2
            jt0 = (rs - p) // 2
            nsl = 5 if p else 4
            dr0 = (rs - p) - r + 7
            q0 = H.c0 + 64 * rr_
            ncol = 64 * (nsl + 2)
            b0 = [0, 1, 3, 4][h]
            st = {}

            def half(i):
                if p == 1 and i == 0:
                    return 1
                if p == 1 and i == nsl - 1:
                    return 0
                return None

            def qk():
                sbk, sbt = s_bank()
                st["s"] = (sbk, sbt)

                def f(e):
                    e.matmul(sbk[:, 0:64 * nsl], lhsT=ident_b[:, :],
                             rhs=bmlib[:, h, dr0:dr0 + 2 * nsl - 1:2, :], start=True, stop=False)
                    for i in range(nsl):
                        jt = jt0 + i
                        hf = half(i)
                        last = (i == nsl - 1)
                        if hf is None:
                            e.matmul(sbk[:, 64 * i:64 * i + 64], lhsT=kTna[psl, hc, 128 * jt:128 * jt + 128],
                                     rhs=qTna[psl, hc, q0:q0 + 64], start=False, stop=last)
                        else:
                            e.matmul(sbk[64 * hf:64 * hf + 64, 64 * i:64 * i + 64],
                                     lhsT=kTna[psl, hc, 128 * jt + 64 * hf:128 * jt + 64 * hf + 64],
                                     rhs=qTna[psl, hc, q0:q0 + 64], start=False, stop=last)
                    ins = None
                    for ci in range(2):
                        ins = e.matmul(sbk[:, 64 * (nsl + ci):64 * (nsl + ci) + 64],
                                       lhsT=kTna[psl, hc, S + 128 * ci:S + 128 * ci + 128],
                                       rhs=qTna[psl, hc, q0:q0 + 64], start=True, stop=True)
                    return ins
                pe_fn(f, [T("kTna"), H.qnat, T("bmlib"), CT], [sbt])

            def ex():
                sbk, sbt = st["s"]
                p_, p_t = pt()
                st["p"] = (p_, p_t)
                act(p_[:, 0:ncol], sbk[:, 0:ncol], AF.Exp, [sbt], [p_t])

            def pv():
                p_, p_t = st["p"]
                ob, obt = get_ob(("na", h))
                slots = []
                for i in range(nsl):
                    hf = half(i)
                    ksl = slice(0, 128) if hf is None else slice(64 * hf, 64 * hf + 64)
                    slots.append((Vall[ksl, jt0 + i, b0:b0 + 2, :].rearrange("p a d -> p (a d)"), p_[ksl, 64 * i:64 * i + 64]))
                for ci in range(2):
                    slots.append((Vall[:, 16 + ci, b0:b0 + 2, :].rearrange("p a d -> p (a d)"), p_[:, 64 * (nsl + ci):64 * (nsl + ci) + 64]))

                def fpv(e):
                    ins = None
                    ns_ = len(slots)
                    for i, (vl, pr) in enumerate(slots):
                        ins = e.matmul(ob[:, 64 * rr_:64 * rr_ + 64], lhsT=vl, rhs=pr, start=(i == 0), stop=(i == ns_ - 1))
                    return ins
                pe_fn(fpv, [p_t, T("Vall")], [obt])
                if rr_ == 3:
                    finish_head(ob, obt, hp, 256, hyT[psl, 2 + hc, H.cs], H.hy, l, None)
            return {"qk": qk, "ex": ex, "pv": pv}

        def mk_sw(h, bb):
            hp, hc, kv = h % 2, h // 2, h // 4
            psl = slice(64 * hp, 64 * hp + 64)
            vc0 = 6 + kv
            qbk = 4 * t + 2 * H.h + bb
            q0 = H.c0 + 128 * bb
            valid = [0 <= qbk - 1 + i <= 15 for i in range(3)]
            i0 = 0 if valid[0] else 1
            i1 = 3 if valid[2] else 2
            st = {}

            def qk():
                sbk, sbt = s_bank()
                sck, sct = aux_bank()
                st["s"] = (sbk, sbt, sck, sct)

                def f(e):
                    ins = None
                    for i in range(3):
                        if not valid[i]:
                            continue
                        kb = qbk - 1 + i
                        o_ = sbk[:, 128 * i:128 * i + 128]
                        if i != 1:
                            e.matmul(o_, lhsT=ident_b[:, :], rhs=swmask[:, 0 if i == 0 else 1, :], start=True, stop=False)
                        ins = e.matmul(o_, lhsT=kTsw[psl, kv, 128 * kb:128 * kb + 128], rhs=qTsw[psl, hc, q0:q0 + 128],
                                       start=(i == 1), stop=True)
                    return ins
                pe_fn(f, [T("kTsw"), H.qswt, CT], [sbt])

                def f2(e):
                    ins = None
                    for ci in range(2):
                        ins = e.matmul(sck[:, 128 * ci:128 * ci + 128], lhsT=kTsw[psl, kv, S + 128 * ci:S + 128 * ci + 128],
                                       rhs=qTsw[psl, hc, q0:q0 + 128], start=True, stop=True)
                    return ins
                pe_fn(f2, [T("kTsw"), H.qswt], [sct])

            def ex():
                sbk, sbt, sck, sct = st["s"]
                p_, p_t = pt()
                st["p"] = (p_, p_t)
                act(p_[:, 128 * i0:128 * i1], sbk[:, 128 * i0:128 * i1], AF.Exp, [sbt], [p_t])
                act(p_[:, 384:640], sck[:, 0:256], AF.Exp, [sct], [p_t])

            def pv():
                p_, p_t = st["p"]
                ob, obt = get_ob(("sw", h))
                slots = []
                for i in range(i0, i1):
                    kb = qbk - 1 + i
                    slots.append((Vall[:, kb, vc0, :], p_[:, 128 * i:128 * i + 128], slice(0, 128)))
                for ci in range(2):
                    slots.append((Vall[:, 16 + ci, vc0, :], p_[:, 384 + 128 * ci:384 + 128 * ci + 128], slice(0, 128)))
                pe_fn(lambda e: pv_group(e, ob, 128 * bb, 128, hp, slots), [p_t, T("Vall"), T("ones_b")], [obt])
                if bb == 1:
                    finish_head(ob, obt, hp, 256, hyT[psl, 4 + hc, H.cs], H.hy, l, hc)
            return {"qk": qk, "ex": ex, "pv": pv}

        if H.is_ctx:
            items = [mk_dense("na", h) for h in range(4)] + [mk_dense("sw", h) for h in range(8)]
        else:
            items = [mk_na(h, rr_) for h in range(4) for rr_ in range(4)] + [mk_sw(h, bb) for h in range(8) for bb in range(2)]
        run_items(items)

    bg = []

    def bg_step(n=1):
        for _ in range(n):
            if bg:
                bg.pop(0)()

    def bg_drain():
        while bg:
            bg.pop(0)()

    ln_pending = []

    def ln_flush():
        while ln_pending:
            ln_pending.pop(0)()

    def ln_stats_chunk(H, k):
        ln_flush()
        i = nxt("zb", 2)
        zb, zsq = zbs[i], zsqs[i]
        cp("dve", zb[:, 0:256], H.xs(k), [H.xTok], [T(f"zb{i}")])
        tt("pool", zsq[:, 0:256], H.xs(k), H.xs(k), ALU.mult, [H.xTok], [T(f"zsq{i}")])

        def f(e):
            e.matmul(H.s1[:, 0:256], lhsT=ones_b[:, :], rhs=zb[:, 0:256], start=(k == 0), stop=(k == KC - 1))
            return e.matmul(H.s2[:, 0:256], lhsT=ones_b[:, :], rhs=zsq[:, 0:256], start=(k == 0), stop=(k == KC - 1))
        ln_pending.append(lambda: pe_fn(f, [T(f"zb{i}"), T(f"zsq{i}"), T("ones_b")], [H.s1t, H.s2t]))

    def ln_finalize_ops(H):
        cs_ = H.cs
        st = H.stt_
        return [
            lambda: (ln_flush(), ts("dve", st_mean[:, cs_], H.s1[:, 0:256], 1.0 / D, ALU.mult, [H.s1t], [st])),
            lambda: tt("dve", st_nmr[:, cs_], st_mean[:, cs_], st_mean[:, cs_], ALU.mult, [st], [st]),
            lambda: stt("dve", st_rstd[:, cs_], H.s2[:, 0:256], 1.0 / D, st_nmr[:, cs_], ALU.mult, ALU.subtract, [H.s2t, st], [st]),
            lambda: act(st_rstd[:, cs_], st_rstd[:, cs_], AF.Ln, [st, T("epsc")], [st], bias=epsc[:, 0:1], scale=1.0),
            lambda: act(st_rstd[:, cs_], st_rstd[:, cs_], AF.Exp, [st], [st], scale=-0.5),
            lambda: stt("dve", st_nmr[:, cs_], st_mean[:, cs_], -1.0, st_rstd[:, cs_], ALU.mult, ALU.mult, [st], [st]),
        ]

    def ln_apply_ops(H, k, gcol, bcol, h2):
        box = {}

        def o1():
            i_ = nxt("lnt", 3)
            box["t"] = (lnt[i_], T(f"lnt{i_}"))
            t1, t1t = box["t"]
            tt("pool", t1[:, 0:256], H.xs(k), st_rstd[:, H.cs], ALU.mult, [H.xTok, H.stt_], [t1t])

        def o2():
            t1, t1t = box["t"]
            tt("dve", t1[:, 0:256], t1[:, 0:256], st_nmr[:, H.cs], ALU.add, [t1t, H.stt_], [t1t])

        def o3():
            t1, t1t = box["t"]
            if h2:
                act(hyT[:, k, H.cs], t1[:, 0:256], AF.Identity, [t1t, T("lnh2")], [H.hy],
                    bias=lnh2[:, H.l, H.b, 1, k:k + 1], scale=lnh2[:, H.l, H.b, 0, k:k + 1])
                ts("dve", H.xs(k), t1[:, 0:256], gcol, ALU.mult, [t1t, T("lnc")], [H.xTok], s2=bcol, op1=ALU.add)
            else:
                act(H.xs(k), t1[:, 0:256], AF.Identity, [t1t, T("lnc")], [H.xTok], bias=bcol, scale=gcol)
        return [o1, o2, o3]

    def ln_push(H, which, h2, defer_fin=True):
        l = H.l
        fin = ln_finalize_ops(H)
        nimm = 3 if defer_fin else 6
        for f_ in fin[:nimm]:
            f_()
        bg.extend(fin[nimm:])
        chains = [ln_apply_ops(H, k, lnc[:, l, which, k:k + 1], lnc[:, l, which + 1, k:k + 1], h2) for k in range(KC)]
        for step in range(KC + 2):
            for k in range(KC):
                j = step - k
                if 0 <= j < 3:
                    bg.append(chains[k][j])

    def ph_modulate(H):
        modulate(H.xs, [H.xTok], H.l, 0, 1, H.b, H.c0, 256, [H.hy])

    def ph_proj(Hs, l):
        w2_, w2t = w_get(l, P_Q2)
        w23 = w2_[:, 0:2048].rearrange("p (k c) -> p k c", k=8)
        vgs = {}
        for H in Hs:
            for tt_ in range(2):
                vi = 2 * H.h + tt_
                acc, acct = acc_bank()
                mm_group(acc[:, 0:256], [(hyT[:, k, H.c0 + 128 * tt_:H.c0 + 128 * tt_ + 128], w23[:, k, :]) for k in range(KC)],
                         [w2t, H.hy], [acct])
                vg_, vgt = ((vg[vi], T(f"vg{vi}")) if vi < 2 else (lnt[vi - 2], T(f"lnt{vi - 2}")))
                vgs[vi] = (vg_, vgt, H)
                act(vg_[:, :], acc[:, 0:256], AF.Gelu_apprx_tanh, [acct], [vgt])
                P.op("dve", lambda e, vg_=vg_, vi=vi: e.bn_stats(out=mv[:, vi, 0:6], in_=vg_[:, :]), reads=[vgt], writes=[T("mv")])
                P.op("dve", lambda e, vi=vi: e.bn_aggr(out=mv[:, vi, 6:8], in_=mv[:, vi, 0:6]), reads=[T("mv")], writes=[T("mv")])
        nv = 2 * len(Hs)
        cp("dve", mvr[:, 0:nv], mv[:, 0:nv, 7], [T("mv")], [T("mvr")])
        act(mvr[:, 0:nv], mvr[:, 0:nv], AF.Sqrt, [T("mvr"), T("epsc")], [T("mvr")], bias=epsc[:, 0:1], scale=1.0)
        recip(mvr[:, 0:nv], mvr[:, 0:nv], [T("mvr")], [T("mvr")])
        for vi in range(nv):
            vg_, vgt, H = vgs[vi]
            ts("dve", vg_[:, :], vg_[:, :], mv[:, vi, 6:7], ALU.subtract, [vgt, T("mv"), T("mvr")], [vgt], s2=mvr[:, vi:vi + 1], op1=ALU.mult)
            tt("pool", vg_[:, :], vg_[:, :], alnG[:, l, :], ALU.mult, [vgt, T("alnGB")], [vgt])
            tt("dve", vln[:, vi, :], vg_[:, :], alnB[:, l, :], ALU.add, [vgt, T("alnGB")], [H.vlnt])
        w0, w0t = w_get(l, P_Q0)
        w03 = w0[:, :].rearrange("p (k c) -> p k c", k=8)
        for H in Hs:
            for ch in range(4):
                acc, acct = acc_bank()
                mm_group(acc[:, 0:256], [(w03[:, k, 128 * ch:128 * ch + 128], hyT[:, k, H.cs]) for k in range(KC)],
                         [w0t, H.hy], [acct])
                if ch < 2:
                    act(uT[:, ch, H.cs], acc[:, 0:256], AF.Gelu_apprx_tanh, [acct], [H.uTt])
                else:
                    act(qTna[:, ch - 2, H.cs], acc[:, 0:256], AF.Copy, [acct], [H.qnat], scale=0.125)
        w1_, w1t = w_get(l, P_Q1)
        w13 = w1_[:, :].rearrange("p (k c) -> p k c", k=8)
        pend = []
        for H in Hs:
            for ch in range(4):
                acc, acct = acc_bank()
                mm_group(acc[:, 0:256], [(w13[:, k, 128 * ch:128 * ch + 128], hyT[:, k, H.cs]) for k in range(KC)],
                         [w1t, H.hy], [acct])
                while pend:
                    pend.pop(0)()
                if H.is_ctx:
                    act(qTsw[:, ch, H.cs], acc[:, 0:256], AF.Copy, [acct], [H.qswt], scale=0.125)
                else:
                    pend.append(rope_evac(acc, acct, qTsw[:, ch, H.cs], H.qswt, H.c0, 256, 0.125))
        while pend:
            pend.pop(0)()
        return pend

    def ph_gmlp(H):
        l = H.l
        for m in range(2):
            ab, abt = acc_bank()

            def fmix(e, ab=ab, m=m):
                ins = None
                for n in range(2):
                    for hh in range(2):
                        g = 2 * m + hh
                        ins = e.matmul(ab[64 * hh:64 * hh + 64, 128 * n:128 * n + 128], lhsT=vln[:, 2 * H.h + n, 64 * g:64 * g + 64],
                                       rhs=WsT[:, l, g, :], start=True, stop=True)
                return ins
            pe_fn(fmix, [H.vlnt, T("WsT")], [abt])
            t1, t1t = tmp()
            for n in range(2):
                tt("dve", t1[:, 128 * n:128 * n + 128], ab[:, 128 * n:128 * n + 128], Bb[:, l, m, :], ALU.add,
                   [abt, T("Bb")], [t1t])
            tt("pool", hyT[:, m, H.cs], t1[:, 0:256], uT[:, m, H.cs], ALU.mult, [t1t, H.uTt], [H.hy])

    def ph_wout(H, wos):
        l, b = H.l, H.b
        bg_drain()
        for dch in range(KC):
            wo3, wot = wos[dch // 4]
            dl = dch % 4
            acc, acct = acc_bank()
            mm_group(acc[:, 0:256], [(wo3[:, k, 128 * dl:128 * dl + 128], hyT[:, k, H.cs]) for k in range(KC)],
                     [wot, H.hy], [acct])
            stt("dve", H.xs(dch), acc[:, 0:256], mcol(l, 2, dch, b), H.xs(dch), ALU.mult, ALU.add, [acct, T("modT"), H.xTok], [H.xTok])
            ln_stats_chunk(H, dch)
        ln_push(H, 0, True, defer_fin=(H.h == 0 and not H.is_ctx))

    def ffn_quarter(Hs, l, qq, hook=None):
        for hh in range(2):
            wa, wat = w_get(l, P_W1 + 2 * qq + hh)
            wa3 = wa[:, :].rearrange("p (k c) -> p k c", k=8)
            for H in Hs:
                for jj in range(4):
                    j = 4 * hh + jj
                    acc, acct = acc_bank()
                    mm_group(acc[:, 0:256], [(wa3[:, k, 128 * jj:128 * jj + 128], hyT[:, k, H.cs]) for k in range(KC)],
                             [wat, H.hy], [acct])
                    t1, t1t = tmp()
                    act(t1[:, 0:256], acc[:, 0:256], AF.Relu, [acct], [t1t])
                    tt("pool", hid[:, j, H.cs], t1[:, 0:256], t1[:, 0:256], ALU.mult, [t1t], [H.hidt])
                    if qq == 0:
                        bg_step(2)
        if hook is not None:
            hook()
        for hf in range(2):
            wb, wbt = w_get(l, P_W2 + 2 * qq + hf)
            wb3 = wb[:, :].rearrange("p (j c) -> p j c", j=8)
            for H in Hs:
                for dl in range(4):
                    dch = 4 * hf + dl
                    acc, acct = acc_bank()
                    mm_group(acc[:, 0:256], [(wb3[:, j, 128 * dl:128 * dl + 128], hid[:, j, H.cs]) for j in range(8)],
                             [wbt, H.hidt], [acct])
                    stt("dve", H.xs(dch), acc[:, 0:256], mcol(l, 5, dch, H.b), H.xs(dch), ALU.mult, ALU.add,
                        [acct, T("modT"), H.xTok], [H.xTok])
                    if qq == 3:
                        ln_stats_chunk(H, dch)
                    if qq == 0:
                        bg_step(2)

    def ph_ffn(Hs, l, hook=None):
        if len(Hs) == 1:
            bg_drain()
            ffn_quarter(Hs, l, 0)
        else:
            ffn_quarter(Hs[0:1], l, 0)
            bg_drain()
            ffn_quarter(Hs[1:2], l, 0)
        for qq in range(1, 4):
            ffn_quarter(Hs, l, qq, hook if qq == 3 else None)
        for H in Hs:
            ln_push(H, 2, False)

    premod = set()

    def mk_halves(l, b, t, is_ctx):
        return [Half(l, b, t, 0, True)] if is_ctx else [Half(l, b, t, 0, False), Half(l, b, t, 1, False)]

    def pre_modulate(l, b, t, is_ctx):
        if not is_ctx:
            load_cs(512 * t, 512)
        for H in mk_halves(l, b, t, is_ctx):
            ph_modulate(H)
        premod.add((l, b, t, is_ctx))

    def tile_pass2(l, b, t, is_ctx, nxt_tile=None):
        Hs = mk_halves(l, b, t, is_ctx)
        if (l, b, t, is_ctx) not in premod:
            pre_modulate(l, b, t, is_ctx)
        pend = ph_proj(Hs, l)
        wos = None
        for H in Hs:
            ph_gmlp(H)
            while pend:
                pend.pop(0)()
            attention(H)
            if wos is None:
                wo0, wo0t = w_get(l, P_WO0, la=2)
                wo1, wo1t = w_get(l, P_WO1, la=1)
                wos = [(wo0[:, :].rearrange("p (k c) -> p k c", k=8), wo0t), (wo1[:, :].rearrange("p (k c) -> p k c", k=8), wo1t)]
            ph_wout(H, wos)
        hook = None
        if nxt_tile is not None:
            hook = lambda: pre_modulate(*nxt_tile)
        ph_ffn(Hs, l, hook)

    stage = [hid[:, 0:4, :].rearrange("p a b -> p (a b)").bitcast(F32), hid[:, 4:8, :].rearrange("p a b -> p (a b)").bitcast(F32),
             hyT[:, 0:4, :].rearrange("p a b -> p (a b)").bitcast(F32), hyT[:, 4:8, :].rearrange("p a b -> p (a b)").bitcast(F32)]
    memset("dve", Vall[:, :, 1, :], 1.0, [T("Vall")])
    memset("dve", Vall[:, :, 4, :], 1.0, [T("Vall")])
    STG = [T("hid"), T("hid0"), T("hid1"), T("hyT0"), T("hyT1")]

    for s_ in range(nseq):
        if s_ > 0:
            P.new_epoch()
        for i in range(2 + 16):
            si = i % 4
            if i < 2:
                src = ctx_d[s_, 128 * i:128 * i + 128, :]
            else:
                src = x_d[s_, 128 * (i - 2):128 * (i - 1), :]
            dma("sp", stage[si], src, [], STG, T("hid"))
            for half_ in range(2):
                ab, abt = io_bank()

                def ftr(e, ab=ab, si=si, half_=half_):
                    ins = None
                    for c4 in range(4):
                        k = 4 * half_ + c4
                        ins = e.transpose(ab[:, 128 * c4:128 * c4 + 128], stage[si][:, 128 * k:128 * k + 128], ident_f[:, :])
                    return ins
                pe_fn(ftr, STG + [CT], [abt])
                if i < 2:
                    dst = xcT[:, 4 * half_:4 * half_ + 4, 128 * i:128 * i + 128]
                    dT = T("xcT")
                else:
                    tok = 128 * (i - 2)
                    dst = xT[:, 4 * half_:4 * half_ + 4, tok:tok + 128]
                    dT = T(f"xT{tok // 512}_{(tok % 512) // 256}")
                if half_ == 0:
                    act(dst, ab[:, 0:512].rearrange("p (c t) -> p c t", c=4), AF.Copy, [abt], [dT], scale=ALPHA)
                else:
                    ts("dve", dst, ab[:, 0:512].rearrange("p (c t) -> p c t", c=4), ALPHA, ALU.mult, [abt], [dT])
        for l in range(nlayers):
            if s_ == 0 and l == 1:
                pass
            dma("sp", bmlib[:, :, :, :].rearrange("p h d q -> p (h d q)"), bmsc_d[l], [BMC], [T("bmlib")], T("bmlib"))
            bg_drain()
            if l == 1:
                while castq:
                    castq.pop(0)()
            if s_ == 0 and l == 1:
                P.op("sp", None, writes=STG, nosig=True)
                mod_layer(1, [(stage[i].rearrange("p (k c) -> p k c", k=8), T(["stgm0", "stgm1", "hyT0", "hyT1"][i]), 128) for i in range(4)])
                derive_lnh2(1)
            wk, wkt = w_get(l, P_KV0, la=1)
            wv, wvt = w_get(l, P_KV1, la=1)
            pass1_all(l, s_, wk, wkt, wv, wvt)
            if s_ == 0 and l == 0 and nlayers > 1:
                emit_casts(1, castq)
            tiles = [(l, s_, t, False) for t in range(4)]
            if l < nlayers - 1:
                tiles.append((l, 4, 0, True))
            for ti, tl in enumerate(tiles):
                tile_pass2(*tl, nxt_tile=(tiles[ti + 1] if ti + 1 < len(tiles) else None))
        bg_drain()
        for i in range(16):
            si = i % 4
            tok = 128 * i
            for half_ in range(2):
                ab, abt = io_bank()

                def ftr2(e, ab=ab, tok=tok, half_=half_):
                    ins = None
                    for c4 in range(4):
                        k = 4 * half_ + c4
                        ins = e.transpose(ab[:, 128 * c4:128 * c4 + 128], xT[:, k, tok:tok + 128], ident_f[:, :])
                    return ins
                pe_fn(ftr2, [T(f"xT{tok // 512}_{(tok % 512) // 256}"), CT], [abt])
                cp("act" if half_ == 0 else "dve", stage[si][:, 512 * half_:512 * half_ + 512], ab[:, 0:512], [abt], STG)
            dma("sp", y_d[s_, tok:tok + 128, :], stage[si], STG, [], T("hid"))
    if dbg:
        pass
    P.op("sp", None, writes=STG, nosig=True)
    if "dbgsem" in tk:
        P.op("sp", None, writes=[T("dbgsem")], nosig=True)
    P.emit()
    return nc


_CACHE = {}


def kernel(**inputs):
    x = np.ascontiguousarray(inputs["x"], dtype=np.float32)
    B = x.shape[0]
    assert B == NCORES * BPC
    consts = host_consts(np.asarray(inputs["na_rpb"], dtype=np.float32))
    if "nc" not in _CACHE:
        _CACHE["nc"] = build_program(BPC, NL)
    nc = _CACHE["nc"]
    shared = {k: np.ascontiguousarray(np.asarray(inputs[k], dtype=np.float32)) for k in
              ["c_ctx", "w_mod", "b_mod", "w_in", "a_ln_g", "a_ln_b", "a_ws", "a_bs", "sw_sink", "w_out",
               "ln1_g", "ln1_b", "w1", "w2", "ln2_g", "ln2_b"]}
    shared.update(consts)
    in_maps = []
    for i in range(NCORES):
        m = dict(shared)
        m["x"] = x[BPC * i:BPC * (i + 1)]
        m["c"] = np.ascontiguousarray(np.asarray(inputs["c"], dtype=np.float32)[BPC * i:BPC * (i + 1)])
        m["ctx"] = np.ascontiguousarray(np.asarray(inputs["ctx"], dtype=np.float32)[BPC * i:BPC * (i + 1)])
        in_maps.append(m)
    res = run_bass_kernel_spmd(nc, in_maps, core_ids=list(range(NCORES)))
    out = np.concatenate([np.asarray(r["y"], dtype=np.float32) for r in res.results], axis=0)
    return out
```

```python
import numpy as np
import ml_dtypes
from contextlib import ExitStack
import concourse.bass as bass
import concourse.mybir as mybir
from concourse.bass_utils import run_bass_kernel_spmd

F32 = mybir.dt.float32
BF16 = mybir.dt.bfloat16
AF = mybir.ActivationFunctionType
ALU = mybir.AluOpType

D = 1024
KC = 8
S = 2048
CL = 256
STOT = S + CL
NL = 2
DFF = 4096
ALPHA = float((2 * NL) ** 0.25)
LN_EPS = 1e-5
NEG = -30000.0
NCORES = 8
BPC = 4
SAME_SYNC = True

P_KV0, P_KV1, P_Q0, P_Q1, P_Q2, P_WO0, P_WO1 = 0, 1, 2, 3, 4, 5, 6
P_W1 = 7
P_W2 = 15
NPIECE = 23


class Tok:
    __slots__ = ("name", "w", "r", "sem", "cnt", "excl")

    def __init__(self, name, excl=False):
        self.name = name
        self.w = {}
        self.r = {}
        self.sem = None
        self.cnt = 0
        self.excl = excl


class Prog:
    ENGS = ("pe", "act", "dve", "pool", "sp")

    def __init__(self, nc, es):
        self.nc = nc
        self.es = es
        self.ops = {e: [] for e in self.ENGS}
        self.seen = {e: {} for e in self.ENGS}
        self.cur = {}
        self.cnt = {}
        self.nsem = 0
        self.new_epoch()

    def alloc_sem(self, name):
        self.nsem += 1
        return self.es.enter_context(self.nc.semaphore(f"{name}_{self.nsem}"))

    def new_epoch(self):
        for e in ("pe", "act", "dve", "pool"):
            self.cur[e] = self.alloc_sem("e" + e)
            self.cnt[e] = 0

    def op(self, eng, fn, reads=(), writes=(), dma=None, nosig=False):
        deps = {}

        def add(rec):
            sem, val, oeng, isdma = rec
            if (not isdma) and oeng == eng and (eng == "pe" or not SAME_SYNC):
                return
            k = id(sem)
            if k not in deps or deps[k][1] < val:
                deps[k] = (sem, val)

        for t in reads:
            for rec in t.w.values():
                add(rec)
            if t.excl:
                for rec in t.r.values():
                    add(rec)
        for t in writes:
            for rec in t.w.values():
                add(rec)
            for rec in t.r.values():
                add(rec)
        waits = []
        sn = self.seen[eng]
        for (s, v) in deps.values():
            if sn.get(id(s), 0) < v:
                waits.append((s, v))
                sn[id(s)] = v
        if nosig:
            self.ops[eng].append((fn, waits, None, 0))
            return
        if dma is not None:
            if dma.sem is None:
                dma.sem = self.alloc_sem("d" + dma.name)
            dma.cnt += 1
            sig = (dma.sem, 16 * dma.cnt)
            key = ("dma", id(dma.sem))
            rec = (sig[0], sig[1], eng, True)
            inc = 16
        else:
            self.cnt[eng] += 1
            sig = (self.cur[eng], self.cnt[eng])
            key = eng
            rec = (sig[0], sig[1], eng, False)
            inc = 1
        for t in reads:
            (t.w if t.excl else t.r)[key] = rec
        for t in writes:
            t.w[key] = rec
        self.ops[eng].append((fn, waits, sig, inc))

    def emit(self):
        with self.nc.Block() as block:
            def mk(name):
                def body(e):
                    for fn, waits, sig, inc in self.ops[name]:
                        for s, v in waits:
                            e.wait_ge(s, v)
                        if fn is None:
                            continue
                        ins = fn(e)
                        if sig is not None:
                            ins.then_inc(sig[0], inc)
                return body
            block.tensor(mk("pe"))
            block.scalar(mk("act"))
            block.vector(mk("dve"))
            block.gpsimd(mk("pool"))
            block.sync(mk("sp"))


def host_consts(na_rpb):
    c = {}
    c["ident_f"] = np.eye(128, dtype=np.float32)
    c["ident_b"] = np.eye(128, dtype=np.float32).astype(ml_dtypes.bfloat16)
    Pm = np.zeros((128, 128), np.float32)
    for m in range(128):
        if (m % 32) < 16:
            Pm[m, m + 16] = -1.0
        else:
            Pm[m, m - 16] = 1.0
    c["prope"] = np.ascontiguousarray(Pm.T).astype(ml_dtypes.bfloat16)
    inv_freq = (np.float32(10000.0) ** (-np.arange(16, dtype=np.float32) / np.float32(16))).astype(np.float32)
    pos = np.arange(S)
    row = (pos // 64).astype(np.float32)
    col = (pos % 64).astype(np.float32)
    cs = np.zeros((128, 2, S), np.float32)
    for p in range(128):
        d = p % 64
        if d < 32:
            ang = row * inv_freq[d % 16]
        else:
            ang = col * inv_freq[(d - 32) % 16]
        ang = ang.astype(np.float32)
        cs[p, 0] = np.cos(ang)
        cs[p, 1] = np.sin(ang)
    c["cs"] = cs
    ki = np.arange(128)[:, None]
    qi = np.arange(128)[None, :]
    msk = np.zeros((128, 2, 128), np.float32)
    msk[:, 0, :] = np.where(qi <= ki, 0.0, NEG)
    msk[:, 1, :] = np.where(ki <= qi, 0.0, NEG)
    c["swmask"] = msk.astype(ml_dtypes.bfloat16)
    cq = np.arange(64)
    cst = np.clip(cq - 8, 0, 48)
    col_ok = (cq[None, :] >= cst[:, None]) & (cq[None, :] < cst[:, None] + 16)
    dc = np.clip(cq[None, :] - cq[:, None], -15, 15) + 15
    Tb = na_rpb[:, :, :, dc]
    Tb = np.where(col_ok[None, None, None], Tb, np.float32(NEG)).astype(np.float32)
    Tb = Tb.transpose(0, 1, 2, 4, 3)
    lib = np.empty((NL, 2, 64, 4, 14, 64), np.float32)
    for dr0 in range(14):
        lib[:, 0, :, :, dr0, :] = Tb[:, :, dr0].transpose(0, 2, 1, 3)
        lib[:, 1, :, :, dr0, :] = Tb[:, :, dr0 + 1].transpose(0, 2, 1, 3)
    c["bmlib"] = np.ascontiguousarray(lib.reshape(NL, 128, 4 * 14 * 64))
    return c


def build_program(nseq=BPC, nlayers=NL, dbg=None):
    nc = bass.Bass("TRN2", target_bir_lowering=False)
    es = ExitStack()
    P = Prog(nc, es)

    def din(name, shape, dt=F32):
        return nc.dram_tensor(name, list(shape), dt, kind="ExternalInput").ap()

    x_d = din("x", [nseq, S, D])
    c_d = din("c", [nseq, D])
    ctx_d = din("ctx", [nseq, CL, D])
    cctx_d = din("c_ctx", [D])
    wmod_d = din("w_mod", [NL, D, 6 * D])
    bmod_d = din("b_mod", [NL, 6 * D])
    win_d = din("w_in", [NL, D, 2048])
    alng_d = din("a_ln_g", [NL, 256])
    alnb_d = din("a_ln_b", [NL, 256])
    aws_d = din("a_ws", [NL, 4, 128, 128])
    abs_d = din("a_bs", [NL, 4, 128])
    sink_d = din("sw_sink", [NL, 8])
    wout_d = din("w_out", [NL, D, D])
    ln1g_d = din("ln1_g", [NL, D])
    ln1b_d = din("ln1_b", [NL, D])
    w1_d = din("w1", [NL, D, DFF])
    w2_d = din("w2", [NL, DFF, D])
    ln2g_d = din("ln2_g", [NL, D])
    ln2b_d = din("ln2_b", [NL, D])
    identf_d = din("ident_f", [128, 128])
    identb_d = din("ident_b", [128, 128], BF16)
    prope_d = din("prope", [128, 128], BF16)
    cs_d = din("cs", [128, 2, S])
    swmask_d = din("swmask", [128, 2, 128], BF16)
    bmlib_d = din("bmlib", [NL, 128, 3584])
    y_d = nc.dram_tensor("y", [nseq, S, D], F32, kind="ExternalOutput").ap()
    wsc_d = nc.dram_tensor("wsc", [NL, NPIECE, 128, 4096], BF16, kind="Internal").ap()
    bmsc_d = nc.dram_tensor("bmsc", [NL, 128, 3584], BF16, kind="Internal").ap()
    dbg_out = {}
    if dbg:
        for name, shape in dbg.items():
            dbg_out[name] = nc.dram_tensor("dbg_" + name, list(shape), F32, kind="ExternalOutput").ap()

    def sb(name, shape, dt):
        return es.enter_context(nc.sbuf_tensor(name, list(shape), dt))

    xT = sb("xT", [128, KC, S], F32)
    xcT = sb("xcT", [128, KC, CL], F32)
    kTna = sb("kTna", [128, 2, STOT], BF16)
    kTsw = sb("kTsw", [128, 2, STOT], BF16)
    Vall = sb("Vall", [128, 18, 8, 64], BF16)
    wsl = [sb(f"wsl{i}", [128, 4096], BF16) for i in range(3)]
    hyT = sb("hyT", [128, KC, 512], BF16)
    hid = sb("hid", [128, 8, 512], BF16)
    uT = sb("uT", [128, 2, 512], BF16)
    qTna = sb("qTna", [128, 2, 512], BF16)
    qTsw = sb("qTsw", [128, 4, 512], BF16)
    qb = [sb(f"qb{i}", [128, 256], BF16) for i in range(2)]
    vln = sb("vln", [128, 4, 256], BF16)
    PT = [sb(f"PT{i}", [128, 640], BF16) for i in range(3)]
    tmpF = [sb(f"tmpF{i}", [128, 256], F32) for i in range(3)]
    zbs = [sb(f"zb{i}", [128, 256], BF16) for i in range(2)]
    zsqs = [sb(f"zsq{i}", [128, 256], BF16) for i in range(2)]
    st_mean = sb("st_mean", [128, 512], F32)
    st_rstd = sb("st_rstd", [128, 512], F32)
    st_nmr = sb("st_nmr", [128, 512], F32)
    csb = sb("csb", [128, 2, 512], F32)
    bmlib = sb("bmlib_sb", [128, 4, 14, 64], BF16)
    Bb = sb("Bb", [128, NL, 2, 128], F32)
    WsT = sb("WsT", [128, NL, 4, 128], BF16)
    alnG = sb("alnG", [128, NL, 256], F32)
    alnB = sb("alnB", [128, NL, 256], F32)
    modT = sb("modT", [128, NL, 48, 5], F32)
    lnc = sb("lnc", [128, NL, 4, KC], F32)
    es_t = sb("es_t", [128, NL, 4], F32)
    lnh2 = sb("lnh2", [128, NL, 5, 2, KC], F32)
    lnt = [sb(f"lnt{i}", [128, 256], F32) for i in range(3)]
    TA = sb("TA", [128, 128], F32)
    TB = sb("TB", [128, 72], F32)
    rowsA = sb("rowsA", [128, 128], F32)
    rowsB = sb("rowsB", [72, 128], F32)
    csT = sb("csT", [128, 5, 8], F32)
    ones_f = sb("ones_f", [1, 128], F32)
    ident_f = sb("ident_f_sb", [128, 128], F32)
    ident_b = sb("ident_b_sb", [128, 128], BF16)
    prope = sb("prope_sb", [128, 128], BF16)
    swmask = sb("swmask_sb", [128, 2, 128], BF16)
    ones_b = sb("ones_b", [128, 128], BF16)
    mv = sb("mv", [128, 4, 8], F32)
    epsc = sb("epsc", [128, 1], F32)
    vg = [sb(f"vg{i}", [128, 256], F32) for i in range(2)]
    mvr = sb("mvr", [128, 4], F32)

    ps = [es.enter_context(nc.psum_tensor(f"ps{i}", [128, 512], F32)) for i in range(8)]
    psT = [Tok(f"ps{i}", excl=True) for i in range(8)]

    tk = {}

    def T(name):
        if name not in tk:
            tk[name] = Tok(name)
        return tk[name]

    rr = {}

    def nxt(name, n):
        rr[name] = (rr.get(name, -1) + 1) % n
        return rr[name]

    def acc_bank():
        i = nxt("acc", 2)
        return ps[i], psT[i]

    def s_bank():
        i = 2 + nxt("sb", 2)
        return ps[i], psT[i]

    def o_bank():
        i = 4 + nxt("ob", 2)
        return ps[i], psT[i]

    def aux_bank():
        i = 6 + nxt("aux", 2)
        return ps[i], psT[i]

    def io_bank():
        i = nxt("iob", 8)
        return ps[i], psT[i]

    def tmp():
        i = nxt("tmpF", 3)
        return tmpF[i], T(f"tmpF{i}")

    def pt():
        i = nxt("PT", 3)
        return PT[i], T(f"PT{i}")

    def dma(eng, out, in_, reads, writes, tok):
        P.op(eng, lambda e: e.dma_start(out=out, in_=in_), reads=reads, writes=writes, dma=tok)

    def mm_group(out, pairs, reads, writes):
        n = len(pairs)

        def fn(e):
            ins = None
            for i, (l, r) in enumerate(pairs):
                ins = e.matmul(out, lhsT=l, rhs=r, start=(i == 0), stop=(i == n - 1))
            return ins
        P.op("pe", fn, reads=reads, writes=writes)

    def pe_fn(fn, reads, writes):
        P.op("pe", fn, reads=reads, writes=writes)

    def act(out, in_, func, reads, writes, bias=None, scale=None):
        kw = {}
        if bias is not None:
            kw["bias"] = bias
        if scale is not None:
            kw["scale"] = scale
            if func == AF.Copy:
                func = AF.Identity
        P.op("act", lambda e: e.activation(out=out, in_=in_, func=func, **kw), reads=reads, writes=writes)

    pool_ok = [False]

    def tt(eng, out, in0, in1, op, reads, writes):
        if eng == "pool" and not pool_ok[0]:
            eng = "dve"
        P.op(eng, lambda e: e.tensor_tensor(out=out, in0=in0, in1=in1, op=op), reads=reads, writes=writes)

    def ts(eng, out, in0, s1, op0, reads, writes, s2=None, op1=None):
        if op1 is None:
            P.op(eng, lambda e: e.tensor_scalar(out=out, in0=in0, scalar1=s1, scalar2=None, op0=op0), reads=reads, writes=writes)
        else:
            P.op(eng, lambda e: e.tensor_scalar(out=out, in0=in0, scalar1=s1, scalar2=s2, op0=op0, op1=op1), reads=reads, writes=writes)

    def stt(eng, out, in0, scalar, in1, op0, op1, reads, writes):
        P.op(eng, lambda e: e.scalar_tensor_tensor(out=out, in0=in0, scalar=scalar, in1=in1, op0=op0, op1=op1),
             reads=reads, writes=writes)

    def cp(eng, out, in_, reads, writes):
        if eng == "act":
            act(out, in_, AF.Copy, reads, writes)
        else:
            P.op(eng, lambda e: e.tensor_copy(out=out, in_=in_), reads=reads, writes=writes)

    def recip(out, in_, reads, writes):
        P.op("dve", lambda e: e.reciprocal(out=out, in_=in_), reads=reads, writes=writes)

    def memset(eng, ap, val, writes):
        P.op(eng, lambda e: e.memset(ap, val), writes=writes)

    def dbg_dump(name, ap, tok):
        if name in dbg_out:
            dma("sp", dbg_out[name], ap, [tok], [], T("dbgsem"))

    castT = {}

    def cast_tok(l, g):
        k = (l, g)
        if k not in castT:
            castT[k] = Tok(f"cast{l}_{g}")
        return castT[k]

    def piece_group(pi):
        if pi <= P_Q2:
            return 0
        if pi <= P_WO1:
            return 1
        if pi < P_W2:
            return 2
        return 3

    castq = []

    def emit_casts(l, sink=None):
        def c(pi, dst, src):
            t = cast_tok(l, piece_group(pi))
            if sink is None:
                dma("pool", dst, src, [], [t], t)
            else:
                sink.append(lambda: dma("pool", dst, src, [], [t], t))
        winv = win_d[l].rearrange("(k p) c -> p k c", p=128)

        def pv(pi, ncols):
            return wsc_d[l, pi][:, 0:8 * ncols].rearrange("p (k c) -> p k c", k=8)
        kv0 = pv(P_KV0, 512)
        c(P_KV0, kv0[:, :, 0:256], winv[:, :, 768:1024])
        c(P_KV0, kv0[:, :, 256:320], winv[:, :, 1792:1856])
        c(P_KV0, kv0[:, :, 320:384], winv[:, :, 1792:1856])
        c(P_KV0, kv0[:, :, 384:448], winv[:, :, 1856:1920])
        c(P_KV0, kv0[:, :, 448:512], winv[:, :, 1856:1920])
        kv1 = pv(P_KV1, 384)
        c(P_KV1, kv1[:, :, 0:256], winv[:, :, 1024:1280])
        c(P_KV1, kv1[:, :, 256:384], winv[:, :, 1920:2048])
        q0 = pv(P_Q0, 512)
        c(P_Q0, q0[:, :, 0:256], winv[:, :, 0:256])
        c(P_Q0, q0[:, :, 256:512], winv[:, :, 512:768])
        c(P_Q1, pv(P_Q1, 512), winv[:, :, 1280:1792])
        c(P_Q2, pv(P_Q2, 256), winv[:, :, 256:512])
        wov = wout_d[l].rearrange("(k p) c -> p k c", p=128)
        c(P_WO0, pv(P_WO0, 512), wov[:, :, 0:512])
        c(P_WO1, pv(P_WO1, 512), wov[:, :, 512:1024])
        w1v = w1_d[l].rearrange("(k p) c -> p k c", p=128)
        for i in range(8):
            c(P_W1 + i, pv(P_W1 + i, 512), w1v[:, :, 512 * i:512 * (i + 1)])
        w2v = w2_d[l].rearrange("(j p) d -> p j d", p=128)
        for qq in range(4):
            for hf in range(2):
                pi = P_W2 + qq * 2 + hf
                c(pi, pv(pi, 512), w2v[:, 8 * qq:8 * qq + 8, 512 * hf:512 * hf + 512])

    piece_used = {P_KV0: 4096, P_KV1: 3072, P_Q0: 4096, P_Q1: 4096, P_Q2: 2048, P_WO0: 4096, P_WO1: 4096}
    wseq = []
    wstate = {"issued": 0, "next": 0}

    def w_issue_upto(n):
        while wstate["issued"] < min(n, len(wseq)):
            i = wstate["issued"]
            l, pi = wseq[i]
            si = i % 3
            used = piece_used.get(pi, 4096)
            dma("sp", wsl[si][:, 0:used], wsc_d[l, pi][:, 0:used], [cast_tok(l, piece_group(pi))], [T(f"wsl{si}")], T(f"wsl{si}"))
            wstate["issued"] += 1

    def w_get(l, pi, la=2):
        i = wstate["next"]
        assert wseq[i] == (l, pi), (wseq[i], l, pi)
        w_issue_upto(i + 1 + la)
        wstate["next"] += 1
        si = i % 3
        return wsl[si], T(f"wsl{si}")

    CT = T("consts")
    for (dst, src) in [(ident_f[:, :], identf_d), (ident_b[:, :], identb_d), (prope[:, :], prope_d),
                       (swmask[:, :, :], swmask_d)]:
        dma("sp", dst, src, [], [CT], CT)
    dma("sp", rowsA[0:96, :], bmod_d.rearrange("l (j p) -> (l j) p", p=128), [], [CT], CT)
    for l in range(NL):
        dma("sp", rowsA[96 + 16 * l:104 + 16 * l, :], ln1g_d[l].rearrange("(k p) -> k p", p=128), [], [CT], CT)
        dma("sp", rowsA[104 + 16 * l:112 + 16 * l, :], ln1b_d[l].rearrange("(k p) -> k p", p=128), [], [CT], CT)
        dma("sp", rowsB[16 * l:16 * l + 8, :], ln2g_d[l].rearrange("(k p) -> k p", p=128), [], [CT], CT)
        dma("sp", rowsB[16 * l + 8:16 * l + 16, :], ln2b_d[l].rearrange("(k p) -> k p", p=128), [], [CT], CT)
    memset("dve", rowsB[32:64, :], 0.0, [CT])
    dma("sp", rowsB[32:32 + 8 * nseq, :], c_d.rearrange("b (k p) -> (b k) p", p=128), [], [CT], CT)
    dma("sp", rowsB[64:72, :], cctx_d.rearrange("(k p) -> k p", p=128), [], [CT], CT)
    rowbuf = {(0, 0): (st_mean, "st_mean"), (0, 1): (st_rstd, "st_rstd"), (1, 0): (st_nmr, "st_nmr"), (1, 1): (csb[:, 0, :], "csb")}
    wsraw = csb[:, 1, :].rearrange("p (g j) -> p g j", g=4)
    for l in range(NL):
        r0, r0n = rowbuf[(l, 0)]
        r1, r1n = rowbuf[(l, 1)]
        dma("sp", r0[0:1, 0:256], alng_d[l:l + 1, :], [], [T(r0n)], T(r0n))
        dma("sp", r0[0:1, 256:512], alnb_d[l:l + 1, :], [], [T(r0n)], T(r0n))
        dma("sp", r1[0:1, 0:512], abs_d[l:l + 1].rearrange("o g i -> o (g i)"), [], [T(r1n)], T(r1n))
    sinkr = sb("sinkr", [1, 16], F32)
    dma("sp", sinkr[0:1, 0:16], sink_d.rearrange("(o l) h -> o (l h)", o=1), [], [CT], CT)
    memset("dve", ones_f[:, :], 1.0, [T("ones_f")])
    memset("dve", ones_b[:, :], 1.0, [T("ones_b")])
    memset("dve", epsc[:, :], LN_EPS, [T("epsc")])

    BMC = Tok("bmcast")
    for l in range(NL):
        dma("pool", bmsc_d[l], bmlib_d[l], [], [BMC], BMC)
    emit_casts(0)

    b0, bt0 = aux_bank()
    pe_fn(lambda e: e.transpose(b0[:, 0:128], rowsA[:, :], ident_f[:, :]), [CT], [bt0])
    cp("dve", TA[:, :], b0[:, 0:128], [bt0], [T("TA")])
    b1, bt1 = aux_bank()
    pe_fn(lambda e: e.transpose(b1[:, 0:72], rowsB[0:72, :], ident_f[0:72, 0:72]), [CT], [bt1])
    cp("dve", TB[:, :], b1[:, 0:72], [bt1], [T("TB")])
    act(csT[:, :, :], TB[:, 32:72].rearrange("p (b k) -> p b k", k=8), AF.Silu, [T("TB")], [T("csT")])

    def mod_layer(l, bufs):
        mb_, mbt = aux_bank()
        j = 0
        bi = 0
        while j < 48:
            ap, tok_, ncol = bufs[bi % len(bufs)]
            bi += 1
            nj = ncol // 128
            dma("sp", ap, wmod_d[l].rearrange("(k p) c -> p k c", p=128)[:, :, 128 * j:128 * j + ncol], [], [tok_], tok_)
            for jj in range(nj):
                mm_group(mb_[:, 5 * (j + jj):5 * (j + jj) + 5],
                         [(ap[:, k, 128 * jj:128 * (jj + 1)], csT[:, :, k]) for k in range(8)],
                         [tok_, T("csT")], [mbt])
            j += nj
        for b in range(5):
            tt("dve", modT[:, l, :, b], mb_[:, 0:240].rearrange("p (j b) -> p j b", b=5)[:, :, b],
               TA[:, 48 * l:48 * (l + 1)], ALU.add, [mbt, T("TA")], [T("modT")])
        for kind in (1, 4):
            ts("dve", modT[:, l, 8 * kind:8 * kind + 8, :], modT[:, l, 8 * kind:8 * kind + 8, :], 1.0, ALU.add,
               [T("modT")], [T("modT")], s2=1.0 / ALPHA, op1=ALU.mult)

    def derive_lnh2(l):
        for b in range(5):
            tt("dve", lnh2[:, l, b, 0, :], lnc[:, l, 0, :], modT[:, l, 32:40, b], ALU.mult, [T("lnc"), T("modT")], [T("lnh2")])
            tt("dve", lnh2[:, l, b, 1, :], lnc[:, l, 1, :], modT[:, l, 32:40, b], ALU.mult, [T("lnc"), T("modT")], [T("lnh2")])
            tt("dve", lnh2[:, l, b, 1, :], lnh2[:, l, b, 1, :], modT[:, l, 24:32, b], ALU.add, [T("lnh2"), T("modT")], [T("lnh2")])

    mod_layer(0, [(wsl[i][:, :].bitcast(F32).rearrange("p (k c) -> p k c", k=8), T(f"wsl{i}"), 256) for i in range(3)])
    for l in range(NL):
        a2 = ALPHA if l < nlayers - 1 else 1.0
        ts("dve", lnc[:, l, 0, :], TA[:, 96 + 16 * l:104 + 16 * l], ALPHA, ALU.mult, [T("TA")], [T("lnc")])
        ts("dve", lnc[:, l, 1, :], TA[:, 104 + 16 * l:112 + 16 * l], ALPHA, ALU.mult, [T("TA")], [T("lnc")])
        ts("dve", lnc[:, l, 2, :], TB[:, 16 * l:16 * l + 8], a2, ALU.mult, [T("TB")], [T("lnc")])
        ts("dve", lnc[:, l, 3, :], TB[:, 16 * l + 8:16 * l + 16], a2, ALU.mult, [T("TB")], [T("lnc")])
    derive_lnh2(0)
    for l in range(NL):
        bb_, bbt = aux_bank()
        r0, r0n = rowbuf[(l, 0)]
        r1, r1n = rowbuf[(l, 1)]
        mm_group(bb_[:, 0:512], [(ones_f[0:1, :], r0[0:1, 0:512])], [T(r0n), T("ones_f")], [bbt])
        cp("dve", alnG[:, l, :], bb_[:, 0:256], [bbt], [T("alnGB")])
        cp("dve", alnB[:, l, :], bb_[:, 256:512], [bbt], [T("alnGB")])
        b2_, b2t = aux_bank()

        def fbb(e, l=l, b2_=b2_, r1=r1):
            ins = None
            for m in range(2):
                for hh in range(2):
                    g = 2 * m + hh
                    ins = e.matmul(b2_[64 * hh:64 * hh + 64, 128 * m:128 * m + 128], lhsT=ones_f[0:1, 0:64],
                                   rhs=r1[0:1, 128 * g:128 * g + 128],
                                   start=True, stop=True)
            return ins
        pe_fn(fbb, [T(r1n), T("ones_f")], [b2t])
        cp("dve", Bb[:, l, :, :], b2_[:, 0:256].rearrange("p (m i) -> p m i", m=2), [b2t], [T("Bb")])
        b3_, b3t = aux_bank()

        def fes(e, l=l, b3_=b3_):
            sr = sinkr[0:1, 8 * l:8 * l + 8].rearrange("o (i t) -> o i t", t=2)
            e.matmul(b3_[64:128, 0:4], lhsT=ones_f[0:1, 0:64], rhs=sr[:, :, 0], start=True, stop=True)
            return e.matmul(b3_[0:64, 0:4], lhsT=ones_f[0:1, 0:64], rhs=sr[:, :, 1], start=True, stop=True)
        pe_fn(fes, [CT, T("ones_f")], [b3t])
        act(es_t[:, l, :], b3_[:, 0:4], AF.Exp, [b3t], [T("es_t")])
        dma("sp", wsraw, aws_d[l].rearrange("g i j -> i g j"), [], [T("wsraw")], T("wsraw"))
        b4_, b4t = aux_bank()

        def fws(e, b4_=b4_):
            ins = None
            for g in range(4):
                ins = e.transpose(b4_[:, 128 * g:128 * g + 128], wsraw[:, g, :], ident_f[:, :])
            return ins
        pe_fn(fws, [T("wsraw"), CT], [b4t])
        cp("dve", WsT[:, l, :, :], b4_[:, 0:512].rearrange("p (g i) -> p g i", g=4), [b4t], [T("WsT")])

    def tile_pieces(l, nh):
        out = [(l, P_Q2), (l, P_Q0), (l, P_Q1), (l, P_WO0), (l, P_WO1)]
        for qq in range(4):
            q4 = [(l, P_W1 + 2 * qq), (l, P_W1 + 2 * qq + 1), (l, P_W2 + 2 * qq), (l, P_W2 + 2 * qq + 1)]
            out += q4
            if qq == 0 and nh == 2:
                out += q4
        return out
    for s_ in range(nseq):
        for l in range(nlayers):
            wseq.extend([(l, P_KV0), (l, P_KV1)])
            for _ in range(4):
                wseq.extend(tile_pieces(l, 2))
            if l < nlayers - 1:
                wseq.extend(tile_pieces(l, 1))

    HY = [T("hyT0"), T("hyT1")]

    def mcol(l, kind, k, b):
        return modT[:, l, 8 * kind + k, b:b + 1]

    def modulate(src_of_k, srcTs, l, kind_sh, kind_s, b, c0, Tn, hyts):
        for k in range(KC):
            act(hyT[:, k, c0:c0 + Tn], src_of_k(k), AF.Identity, list(srcTs) + [T("modT")], hyts,
                bias=mcol(l, kind_sh, k, b), scale=mcol(l, kind_s, k, b))

    def rope_evac(acc, acct, dst, dstT, cc0, Tn, scale):
        i = nxt("qb", 2)
        q_b, q_bt = qb[i], T(f"qb{i}")
        act(q_b[:, 0:Tn], acc[:, 0:Tn], AF.Copy, [acct], [q_bt], scale=scale)

        def rest():
            ab, abt = aux_bank()
            mm_group(ab[:, 0:Tn], [(prope[:, :], q_b[:, 0:Tn])], [CT, q_bt], [abt])
            t1, t1t = tmp()
            stt("dve", t1[:, 0:Tn], acc[:, 0:Tn], scale, csb[:, 0, cc0:cc0 + Tn], ALU.mult, ALU.mult, [acct, T("csb")], [t1t])
            t2, t2t = tmp()
            tt("dve", t2[:, 0:Tn], ab[:, 0:Tn], csb[:, 1, cc0:cc0 + Tn], ALU.mult, [abt, T("csb")], [t2t])
            tt("pool", dst, t1[:, 0:Tn], t2[:, 0:Tn], ALU.add, [t1t, t2t], [dstT])
        return rest

    def load_cs(tok0, Tn):
        dma("sp", csb[:, :, 0:Tn], cs_d[:, :, tok0:tok0 + Tn], [], [T("csb")], T("csb"))

    def pass1_all(l, b, wk, wkt, wv, wvt):
        wk3 = wk[:, :].rearrange("p (k c) -> p k c", k=8)
        wv3 = wv[:, 0:3072].rearrange("p (k c) -> p k c", k=8)
        groups = []
        for g in range(8):
            t, h = g // 2, g % 2
            groups.append(dict(src=(lambda k, g=g: xT[:, k, 256 * g:256 * g + 256]), srcT=[T(f"xT{t}_{h}")], tok0=256 * g,
                               is_ctx=False, b=b))
        groups.append(dict(src=(lambda k: xcT[:, k, :]), srcT=[T("xcT")], tok0=S, is_ctx=True, b=4))

        def mod(gi):
            g = groups[gi]
            hh = gi % 2
            modulate(g["src"], g["srcT"], l, 0, 1, g["b"], 256 * hh, 256, [HY[hh]])
        mod(0)
        pend = []
        for gi, g in enumerate(groups):
            hh = gi % 2
            c0 = 256 * hh
            tok0 = g["tok0"]
            if gi + 1 < len(groups):
                mod(gi + 1)
            if (not g["is_ctx"]) and (tok0 % 512 == 0):
                load_cs(tok0, 512)
            for ch in range(4):
                acc, acct = acc_bank()
                mm_group(acc[:, 0:256], [(wk3[:, k, 128 * ch:128 * ch + 128], hyT[:, k, c0:c0 + 256]) for k in range(KC)],
                         [wkt, HY[hh]], [acct])
                while pend:
                    pend.pop(0)()
                if ch < 2:
                    cp("act", kTna[:, ch, tok0:tok0 + 256], acc[:, 0:256], [acct], [T("kTna")])
                elif g["is_ctx"]:
                    cp("act", kTsw[:, ch - 2, tok0:tok0 + 256], acc[:, 0:256], [acct], [T("kTsw")])
                else:
                    pend.append(rope_evac(acc, acct, kTsw[:, ch - 2, tok0:tok0 + 256], T("kTsw"), tok0 % 512, 256, 1.0))
            for tt_ in range(2):
                acc, acct = acc_bank()
                mm_group(acc[:, 0:384], [(hyT[:, k, c0 + 128 * tt_:c0 + 128 * tt_ + 128], wv3[:, k, :]) for k in range(KC)],
                         [wvt, HY[hh]], [acct])
                while pend:
                    pend.pop(0)()
                vt_ = tok0 // 128 + tt_
                a3 = acc[:, 0:384].rearrange("p (b d) -> p b d", d=64)
                cp("dve", Vall[:, vt_, 0:3:2, :], a3[:, 0:2, :], [acct], [T("Vall")])
                cp("dve", Vall[:, vt_, 3:6:2, :], a3[:, 2:4, :], [acct], [T("Vall")])
                cp("dve", Vall[:, vt_, 6:8, :], a3[:, 4:6, :], [acct], [T("Vall")])

    class Half:
        def __init__(self, l, b, t, h, is_ctx):
            self.l, self.b, self.t, self.h, self.is_ctx = l, b, t, h, is_ctx
            self.Tn = 256
            self.c0 = 0 if is_ctx else 256 * h
            self.tok0 = 0 if is_ctx else 512 * t + 256 * h
            self.cs = slice(self.c0, self.c0 + 256)
            self.xTok = T("xcT") if is_ctx else T(f"xT{t}_{h}")
            self.hy = T(f"hyT{h}")
            self.uTt = T(f"uT{h}")
            self.qnat = T(f"qTna{h}")
            self.qswt = T(f"qTsw{h}")
            self.vlnt = T(f"vln{h}")
            self.hidt = T(f"hid{h}")
            self.mvt = T(f"mv{h}")
            self.stt_ = T(f"st{h}")
            sb_ = (6, 7) if h == 0 else (4, 5)
            self.s1, self.s1t, self.s2, self.s2t = ps[sb_[0]], psT[sb_[0]], ps[sb_[1]], psT[sb_[1]]

        def xs(self, k):
            if self.is_ctx:
                return xcT[:, k, :]
            return xT[:, k, self.tok0:self.tok0 + 256]

    def finish_head(ob, obt, hp, ncols, dst, dstT, l, es_pair):
        dp = 1 - hp
        dsl = slice(64 * dp, 64 * dp + 64)
        osl = slice(64 * hp, 64 * hp + 64)
        t1, t1t = tmp()
        if es_pair is not None:
            ts("dve", t1[dsl, 0:ncols], ob[dsl, 0:ncols], es_t[dsl, l, es_pair:es_pair + 1], ALU.add,
               [obt, T("es_t")], [t1t])
            recip(t1[dsl, 0:ncols], t1[dsl, 0:ncols], [t1t], [t1t])
        else:
            recip(t1[dsl, 0:ncols], ob[dsl, 0:ncols], [obt], [t1t])
        tt("dve", dst, ob[osl, 0:ncols], t1[dsl, 0:ncols], ALU.mult, [obt, t1t], [dstT])

    def pv_group(e, ob, col0, n, hp, slots):
        ns = len(slots)
        ins = None
        for i, (vl, pr, ksl) in enumerate(slots):
            e.matmul(ob[64 * hp:64 * hp + 64, col0:col0 + n], lhsT=vl, rhs=pr, start=(i == 0), stop=(i == ns - 1))
            ins = e.matmul(ob[64 * (1 - hp):64 * (1 - hp) + 64, col0:col0 + n], lhsT=ones_b[ksl, 0:64], rhs=pr,
                           start=(i == 0), stop=(i == ns - 1))
        return ins

    def run_items(items):
        prev = None
        for it in items:
            it["qk"]()
            bg_step(1)
            it["ex"]()
            if prev is not None:
                prev["pv"]()
            bg_step(1)
            if castq:
                castq.pop(0)()
            prev = it
        prev["pv"]()

    def attention(H):
        l, t = H.l, H.t
        obs = {}

        def get_ob(key):
            if key not in obs:
                obs[key] = o_bank()
            return obs[key]
        items = []

        def mk_dense(kind, h):
            hp, hc = h % 2, h // 2
            psl = slice(64 * hp, 64 * hp + 64)
            if kind == "na":
                kT_, kTt, kch, qT_, qTt, vc0, ych, esp = kTna, T("kTna"), hc, qTna, H.qnat, [0, 2, 3, 5][h], 2 + hc, None
            else:
                kv = h // 4
                kT_, kTt, kch, qT_, qTt, vc0, ych, esp = kTsw, T("kTsw"), kv, qTsw, H.qswt, 6 + kv, 4 + hc, hc
            st = {}

            def qk():
                sbk, sbt = s_bank()
                st["s"] = (sbk, sbt)

                def f(e):
                    ins = None
                    for ci in range(2):
                        ins = e.matmul(sbk[:, 256 * ci:256 * ci + 256], lhsT=kT_[psl, kch, S + 128 * ci:S + 128 * ci + 128],
                                       rhs=qT_[psl, hc, H.cs], start=True, stop=True)
                    return ins
                pe_fn(f, [kTt, qTt], [sbt])

            def ex():
                sbk, sbt = st["s"]
                p_, p_t = pt()
                st["p"] = (p_, p_t)
                act(p_[:, 0:512], sbk[:, 0:512], AF.Exp, [sbt], [p_t])

            def pv():
                p_, p_t = st["p"]
                ob, obt = get_ob((kind, h))
                pe_fn(lambda e: pv_group(e, ob, 0, 256, hp,
                                         [(Vall[:, 16 + ci, vc0, :], p_[:, 256 * ci:256 * ci + 256], slice(0, 128)) for ci in range(2)]),
                      [p_t, T("Vall"), T("ones_b")], [obt])
                finish_head(ob, obt, hp, 256, hyT[psl, ych, H.cs], H.hy, l, esp)
            return {"qk": qk, "ex": ex, "pv": pv}

        def mk_na(h, rr_):
            hp, hc = h % 2, h // 2
            psl = slice(64 * hp, 64 * hp + 64)
            r = 8 * t + 4 * H.h + rr_
            rs = min(max(r - 4, 0), 24)
            p = rs % 2
            jt0 = (rs - p) // 2
            nsl = 5 if p else 4
            dr0 = (rs - p) - r + 7
            q0 = H.c0 + 64 * rr_
            ncol = 64 * (nsl + 2)
            b0 = [0, 1, 3, 4][h]
            st = {}

            def half(i):
                if p == 1 and i == 0:
                    return 1
                if p == 1 and i == nsl - 1:
                    return 0
                return None

            def qk():
                sbk, sbt = s_bank()
                st["s"] = (sbk, sbt)

                def f(e):
                    e.matmul(sbk[:, 0:64 * nsl], lhsT=ident_b[:, :],
                             rhs=bmlib[:, h, dr0:dr0 + 2 * nsl - 1:2, :], start=True, stop=False)
                    for i in range(nsl):
                        jt = jt0 + i
                        hf = half(i)
                        last = (i == nsl - 1)
                        if hf is None:
                            e.matmul(sbk[:, 64 * i:64 * i + 64], lhsT=kTna[psl, hc, 128 * jt:128 * jt + 128],
                                     rhs=qTna[psl, hc, q0:q0 + 64], start=False, stop=last)
                        else:
                            e.matmul(sbk[64 * hf:64 * hf + 64, 64 * i:64 * i + 64],
                                     lhsT=kTna[psl, hc, 128 * jt + 64 * hf:128 * jt + 64 * hf + 64],
                                     rhs=qTna[psl, hc, q0:q0 + 64], start=False, stop=last)
                    ins = None
                    for ci in range(2):
                        ins = e.matmul(sbk[:, 64 * (nsl + ci):64 * (nsl + ci) + 64],
                                       lhsT=kTna[psl, hc, S + 128 * ci:S + 128 * ci + 128],
                                       rhs=qTna[psl, hc, q0:q0 + 64], start=True, stop=True)
                    return ins
                pe_fn(f, [T("kTna"), H.qnat, T("bmlib"), CT], [sbt])

            def ex():
                sbk, sbt = st["s"]
                p_, p_t = pt()
                st["p"] = (p_, p_t)
                act(p_[:, 0:ncol], sbk[:, 0:ncol], AF.Exp, [sbt], [p_t])

            def pv():
                p_, p_t = st["p"]
                ob, obt = get_ob(("na", h))
                slots = []
                for i in range(nsl):
                    hf = half(i)
                    ksl = slice(0, 128) if hf is None else slice(64 * hf, 64 * hf + 64)
                    slots.append((Vall[ksl, jt0 + i, b0:b0 + 2, :].rearrange("p a d -> p (a d)"), p_[ksl, 64 * i:64 * i + 64]))
                for ci in range(2):
                    slots.append((Vall[:, 16 + ci, b0:b0 + 2, :].rearrange("p a d -> p (a d)"), p_[:, 64 * (nsl + ci):64 * (nsl + ci) + 64]))

                def fpv(e):
                    ins = None
                    ns_ = len(slots)
                    for i, (vl, pr) in enumerate(slots):
                        ins = e.matmul(ob[:, 64 * rr_:64 * rr_ + 64], lhsT=vl, rhs=pr, start=(i == 0), stop=(i == ns_ - 1))
                    return ins
                pe_fn(fpv, [p_t, T("Vall")], [obt])
                if rr_ == 3:
                    finish_head(ob, obt, hp, 256, hyT[psl, 2 + hc, H.cs], H.hy, l, None)
            return {"qk": qk, "ex": ex, "pv": pv}

        def mk_sw(h, bb):
            hp, hc, kv = h % 2, h // 2, h // 4
            psl = slice(64 * hp, 64 * hp + 64)
            vc0 = 6 + kv
            qbk = 4 * t + 2 * H.h + bb
            q0 = H.c0 + 128 * bb
            valid = [0 <= qbk - 1 + i <= 15 for i in range(3)]
            i0 = 0 if valid[0] else 1
            i1 = 3 if valid[2] else 2
            st = {}

            def qk():
                sbk, sbt = s_bank()
                sck, sct = aux_bank()
                st["s"] = (sbk, sbt, sck, sct)

                def f(e):
                    ins = None
                    for i in range(3):
                        if not valid[i]:
                            continue
                        kb = qbk - 1 + i
                        o_ = sbk[:, 128 * i:128 * i + 128]
                        if i != 1:
                            e.matmul(o_, lhsT=ident_b[:, :], rhs=swmask[:, 0 if i == 0 else 1, :], start=True, stop=False)
                        ins = e.matmul(o_, lhsT=kTsw[psl, kv, 128 * kb:128 * kb + 128], rhs=qTsw[psl, hc, q0:q0 + 128],
                                       start=(i == 1), stop=True)
                    return ins
                pe_fn(f, [T("kTsw"), H.qswt, CT], [sbt])

                def f2(e):
                    ins = None
                    for ci in range(2):
                        ins = e.matmul(sck[:, 128 * ci:128 * ci + 128], lhsT=kTsw[psl, kv, S + 128 * ci:S + 128 * ci + 128],
                                       rhs=qTsw[psl, hc, q0:q0 + 128], start=True, stop=True)
                    return ins
                pe_fn(f2, [T("kTsw"), H.qswt], [sct])

            def ex():
                sbk, sbt, sck, sct = st["s"]
                p_, p_t = pt()
                st["p"] = (p_, p_t)
                act(p_[:, 128 * i0:128 * i1], sbk[:, 128 * i0:128 * i1], AF.Exp, [sbt], [p_t])
                act(p_[:, 384:640], sck[:, 0:256], AF.Exp, [sct], [p_t])

            def pv():
                p_, p_t = st["p"]
                ob, obt = get_ob(("sw", h))
                slots = []
                for i in range(i0, i1):
                    kb = qbk - 1 + i
                    slots.append((Vall[:, kb, vc0, :], p_[:, 128 * i:128 * i + 128], slice(0, 128)))
                for ci in range(2):
                    slots.append((Vall[:, 16 + ci, vc0, :], p_[:, 384 + 128 * ci:384 + 128 * ci + 128], slice(0, 128)))
                pe_fn(lambda e: pv_group(e, ob, 128 * bb, 128, hp, slots), [p_t, T("Vall"), T("ones_b")], [obt])
                if bb == 1:
                    finish_head(ob, obt, hp, 256, hyT[psl, 4 + hc, H.cs], H.hy, l, hc)
            return {"qk": qk, "ex": ex, "pv": pv}

        if H.is_ctx:
            items = [mk_dense("na", h) for h in range(4)] + [mk_dense("sw", h) for h in range(8)]
        else:
            items = [mk_na(h, rr_) for h in range(4) for rr_ in range(4)] + [mk_sw(h, bb) for h in range(8) for bb in range(2)]
        run_items(items)

    bg = []

    def bg_step(n=1):
        for _ in range(n):
            if bg:
                bg.pop(0)()

    def bg_drain():
        while bg:
            bg.pop(0)()

    ln_pending = []

    def ln_flush():
        while ln_pending:
            ln_pending.pop(0)()

    def ln_stats_chunk(H, k):
        ln_flush()
        i = nxt("zb", 2)
        zb, zsq = zbs[i], zsqs[i]
        cp("dve", zb[:, 0:256], H.xs(k), [H.xTok], [T(f"zb{i}")])
        tt("pool", zsq[:, 0:256], H.xs(k), H.xs(k), ALU.mult, [H.xTok], [T(f"zsq{i}")])

        def f(e):
            e.matmul(H.s1[:, 0:256], lhsT=ones_b[:, :], rhs=zb[:, 0:256], start=(k == 0), stop=(k == KC - 1))
            return e.matmul(H.s2[:, 0:256], lhsT=ones_b[:, :], rhs=zsq[:, 0:256], start=(k == 0), stop=(k == KC - 1))
        ln_pending.append(lambda: pe_fn(f, [T(f"zb{i}"), T(f"zsq{i}"), T("ones_b")], [H.s1t, H.s2t]))

    def ln_finalize_ops(H):
        cs_ = H.cs
        st = H.stt_
        return [
            lambda: (ln_flush(), ts("dve", st_mean[:, cs_], H.s1[:, 0:256], 1.0 / D, ALU.mult, [H.s1t], [st])),
            lambda: tt("dve", st_nmr[:, cs_], st_mean[:, cs_], st_mean[:, cs_], ALU.mult, [st], [st]),
            lambda: stt("dve", st_rstd[:, cs_], H.s2[:, 0:256], 1.0 / D, st_nmr[:, cs_], ALU.mult, ALU.subtract, [H.s2t, st], [st]),
            lambda: act(st_rstd[:, cs_], st_rstd[:, cs_], AF.Ln, [st, T("epsc")], [st], bias=epsc[:, 0:1], scale=1.0),
            lambda: act(st_rstd[:, cs_], st_rstd[:, cs_], AF.Exp, [st], [st], scale=-0.5),
            lambda: stt("dve", st_nmr[:, cs_], st_mean[:, cs_], -1.0, st_rstd[:, cs_], ALU.mult, ALU.mult, [st], [st]),
        ]

    def ln_apply_ops(H, k, gcol, bcol, h2):
        box = {}

        def o1():
            i_ = nxt("lnt", 3)
            box["t"] = (lnt[i_], T(f"lnt{i_}"))
            t1, t1t = box["t"]
            tt("pool", t1[:, 0:256], H.xs(k), st_rstd[:, H.cs], ALU.mult, [H.xTok, H.stt_], [t1t])

        def o2():
            t1, t1t = box["t"]
            tt("dve", t1[:, 0:256], t1[:, 0:256], st_nmr[:, H.cs], ALU.add, [t1t, H.stt_], [t1t])

        def o3():
            t1, t1t = box["t"]
            if h2:
                act(hyT[:, k, H.cs], t1[:, 0:256], AF.Identity, [t1t, T("lnh2")], [H.hy],
                    bias=lnh2[:, H.l, H.b, 1, k:k + 1], scale=lnh2[:, H.l, H.b, 0, k:k + 1])
                ts("dve", H.xs(k), t1[:, 0:256], gcol, ALU.mult, [t1t, T("lnc")], [H.xTok], s2=bcol, op1=ALU.add)
            else:
                act(H.xs(k), t1[:, 0:256], AF.Identity, [t1t, T("lnc")], [H.xTok], bias=bcol, scale=gcol)
        return [o1, o2, o3]

    def ln_push(H, which, h2, defer_fin=True):
        l = H.l
        fin = ln_finalize_ops(H)
        nimm = 3 if defer_fin else 6
        for f_ in fin[:nimm]:
            f_()
        bg.extend(fin[nimm:])
        chains = [ln_apply_ops(H, k, lnc[:, l, which, k:k + 1], lnc[:, l, which + 1, k:k + 1], h2) for k in range(KC)]
        for step in range(KC + 2):
            for k in range(KC):
                j = step - k
                if 0 <= j < 3:
                    bg.append(chains[k][j])

    def ph_modulate(H):
        modulate(H.xs, [H.xTok], H.l, 0, 1, H.b, H.c0, 256, [H.hy])

    def ph_proj(Hs, l):
        w2_, w2t = w_get(l, P_Q2)
        w23 = w2_[:, 0:2048].rearrange("p (k c) -> p k c", k=8)
        vgs = {}
        for H in Hs:
            for tt_ in range(2):
                vi = 2 * H.h + tt_
                acc, acct = acc_bank()
                mm_group(acc[:, 0:256], [(hyT[:, k, H.c0 + 128 * tt_:H.c0 + 128 * tt_ + 128], w23[:, k, :]) for k in range(KC)],
                         [w2t, H.hy], [acct])
                vg_, vgt = ((vg[vi], T(f"vg{vi}")) if vi < 2 else (lnt[vi - 2], T(f"lnt{vi - 2}")))
                vgs[vi] = (vg_, vgt, H)
                act(vg_[:, :], acc[:, 0:256], AF.Gelu_apprx_tanh, [acct], [vgt])
                P.op("dve", lambda e, vg_=vg_, vi=vi: e.bn_stats(out=mv[:, vi, 0:6], in_=vg_[:, :]), reads=[vgt], writes=[T("mv")])
                P.op("dve", lambda e, vi=vi: e.bn_aggr(out=mv[:, vi, 6:8], in_=mv[:, vi, 0:6]), reads=[T("mv")], writes=[T("mv")])
        nv = 2 * len(Hs)
        cp("dve", mvr[:, 0:nv], mv[:, 0:nv, 7], [T("mv")], [T("mvr")])
        act(mvr[:, 0:nv], mvr[:, 0:nv], AF.Sqrt, [T("mvr"), T("epsc")], [T("mvr")], bias=epsc[:, 0:1], scale=1.0)
        recip(mvr[:, 0:nv], mvr[:, 0:nv], [T("mvr")], [T("mvr")])
        for vi in range(nv):
            vg_, vgt, H = vgs[vi]
            ts("dve", vg_[:, :], vg_[:, :], mv[:, vi, 6:7], ALU.subtract, [vgt, T("mv"), T("mvr")], [vgt], s2=mvr[:, vi:vi + 1], op1=ALU.mult)
            tt("pool", vg_[:, :], vg_[:, :], alnG[:, l, :], ALU.mult, [vgt, T("alnGB")], [vgt])
            tt("dve", vln[:, vi, :], vg_[:, :], alnB[:, l, :], ALU.add, [vgt, T("alnGB")], [H.vlnt])
        w0, w0t = w_get(l, P_Q0)
        w03 = w0[:, :].rearrange("p (k c) -> p k c", k=8)
        for H in Hs:
            for ch in range(4):
                acc, acct = acc_bank()
                mm_group(acc[:, 0:256], [(w03[:, k, 128 * ch:128 * ch + 128], hyT[:, k, H.cs]) for k in range(KC)],
                         [w0t, H.hy], [acct])
                if ch < 2:
                    act(uT[:, ch, H.cs], acc[:, 0:256], AF.Gelu_apprx_tanh, [acct], [H.uTt])
                else:
                    act(qTna[:, ch - 2, H.cs], acc[:, 0:256], AF.Copy, [acct], [H.qnat], scale=0.125)
        w1_, w1t = w_get(l, P_Q1)
        w13 = w1_[:, :].rearrange("p (k c) -> p k c", k=8)
        pend = []
        for H in Hs:
            for ch in range(4):
                acc, acct = acc_bank()
                mm_group(acc[:, 0:256], [(w13[:, k, 128 * ch:128 * ch + 128], hyT[:, k, H.cs]) for k in range(KC)],
                         [w1t, H.hy], [acct])
                while pend:
                    pend.pop(0)()
                if H.is_ctx:
                    act(qTsw[:, ch, H.cs], acc[:, 0:256], AF.Copy, [acct], [H.qswt], scale=0.125)
                else:
                    pend.append(rope_evac(acc, acct, qTsw[:, ch, H.cs], H.qswt, H.c0, 256, 0.125))
        while pend:
            pend.pop(0)()
        return pend

    def ph_gmlp(H):
        l = H.l
        for m in range(2):
            ab, abt = acc_bank()

            def fmix(e, ab=ab, m=m):
                ins = None
                for n in range(2):
                    for hh in range(2):
                        g = 2 * m + hh
                        ins = e.matmul(ab[64 * hh:64 * hh + 64, 128 * n:128 * n + 128], lhsT=vln[:, 2 * H.h + n, 64 * g:64 * g + 64],
                                       rhs=WsT[:, l, g, :], start=True, stop=True)
                return ins
            pe_fn(fmix, [H.vlnt, T("WsT")], [abt])
            t1, t1t = tmp()
            for n in range(2):
                tt("dve", t1[:, 128 * n:128 * n + 128], ab[:, 128 * n:128 * n + 128], Bb[:, l, m, :], ALU.add,
                   [abt, T("Bb")], [t1t])
            tt("pool", hyT[:, m, H.cs], t1[:, 0:256], uT[:, m, H.cs], ALU.mult, [t1t, H.uTt], [H.hy])

    def ph_wout(H, wos):
        l, b = H.l, H.b
        bg_drain()
        for dch in range(KC):
            wo3, wot = wos[dch // 4]
            dl = dch % 4
            acc, acct = acc_bank()
            mm_group(acc[:, 0:256], [(wo3[:, k, 128 * dl:128 * dl + 128], hyT[:, k, H.cs]) for k in range(KC)],
                     [wot, H.hy], [acct])
            stt("dve", H.xs(dch), acc[:, 0:256], mcol(l, 2, dch, b), H.xs(dch), ALU.mult, ALU.add, [acct, T("modT"), H.xTok], [H.xTok])
            ln_stats_chunk(H, dch)
        ln_push(H, 0, True, defer_fin=(H.h == 0 and not H.is_ctx))

    def ffn_quarter(Hs, l, qq, hook=None):
        for hh in range(2):
            wa, wat = w_get(l, P_W1 + 2 * qq + hh)
            wa3 = wa[:, :].rearrange("p (k c) -> p k c", k=8)
            for H in Hs:
                for jj in range(4):
                    j = 4 * hh + jj
                    acc, acct = acc_bank()
                    mm_group(acc[:, 0:256], [(wa3[:, k, 128 * jj:128 * jj + 128], hyT[:, k, H.cs]) for k in range(KC)],
                             [wat, H.hy], [acct])
                    t1, t1t = tmp()
                    act(t1[:, 0:256], acc[:, 0:256], AF.Relu, [acct], [t1t])
                    tt("pool", hid[:, j, H.cs], t1[:, 0:256], t1[:, 0:256], ALU.mult, [t1t], [H.hidt])
                    if qq == 0:
                        bg_step(2)
        if hook is not None:
            hook()
        for hf in range(2):
            wb, wbt = w_get(l, P_W2 + 2 * qq + hf)
            wb3 = wb[:, :].rearrange("p (j c) -> p j c", j=8)
            for H in Hs:
                for dl in range(4):
                    dch = 4 * hf + dl
                    acc, acct = acc_bank()
                    mm_group(acc[:, 0:256], [(wb3[:, j, 128 * dl:128 * dl + 128], hid[:, j, H.cs]) for j in range(8)],
                             [wbt, H.hidt], [acct])
                    stt("dve", H.xs(dch), acc[:, 0:256], mcol(l, 5, dch, H.b), H.xs(dch), ALU.mult, ALU.add,
                        [acct, T("modT"), H.xTok], [H.xTok])
                    if qq == 3:
                        ln_stats_chunk(H, dch)
                    if qq == 0:
                        bg_step(2)

    def ph_ffn(Hs, l, hook=None):
        if len(Hs) == 1:
            bg_drain()
            ffn_quarter(Hs, l, 0)
        else:
            ffn_quarter(Hs[0:1], l, 0)
            bg_drain()
            ffn_quarter(Hs[1:2], l, 0)
        for qq in range(1, 4):
            ffn_quarter(Hs, l, qq, hook if qq == 3 else None)
        for H in Hs:
            ln_push(H, 2, False)

    premod = set()

    def mk_halves(l, b, t, is_ctx):
        return [Half(l, b, t, 0, True)] if is_ctx else [Half(l, b, t, 0, False), Half(l, b, t, 1, False)]

    def pre_modulate(l, b, t, is_ctx):
        if not is_ctx:
            load_cs(512 * t, 512)
        for H in mk_halves(l, b, t, is_ctx):
            ph_modulate(H)
        premod.add((l, b, t, is_ctx))

    def tile_pass2(l, b, t, is_ctx, nxt_tile=None):
        Hs = mk_halves(l, b, t, is_ctx)
        if (l, b, t, is_ctx) not in premod:
            pre_modulate(l, b, t, is_ctx)
        pend = ph_proj(Hs, l)
        wos = None
        for H in Hs:
            ph_gmlp(H)
            while pend:
                pend.pop(0)()
            attention(H)
            if wos is None:
                wo0, wo0t = w_get(l, P_WO0, la=2)
                wo1, wo1t = w_get(l, P_WO1, la=1)
                wos = [(wo0[:, :].rearrange("p (k c) -> p k c", k=8), wo0t), (wo1[:, :].rearrange("p (k c) -> p k c", k=8), wo1t)]
            ph_wout(H, wos)
        hook = None
        if nxt_tile is not None:
            hook = lambda: pre_modulate(*nxt_tile)
        ph_ffn(Hs, l, hook)

    stage = [hid[:, 0:4, :].rearrange("p a b -> p (a b)").bitcast(F32), hid[:, 4:8, :].rearrange("p a b -> p (a b)").bitcast(F32),
             hyT[:, 0:4, :].rearrange("p a b -> p (a b)").bitcast(F32), hyT[:, 4:8, :].rearrange("p a b -> p (a b)").bitcast(F32)]
    memset("dve", Vall[:, :, 1, :], 1.0, [T("Vall")])
    memset("dve", Vall[:, :, 4, :], 1.0, [T("Vall")])
    STG = [T("hid"), T("hid0"), T("hid1"), T("hyT0"), T("hyT1")]

    for s_ in range(nseq):
        if s_ > 0:
            P.new_epoch()
        for i in range(2 + 16):
            si = i % 4
            if i < 2:
                src = ctx_d[s_, 128 * i:128 * i + 128, :]
            else:
                src = x_d[s_, 128 * (i - 2):128 * (i - 1), :]
            dma("sp", stage[si], src, [], STG, T("hid"))
            for half_ in range(2):
                ab, abt = io_bank()

                def ftr(e, ab=ab, si=si, half_=half_):
                    ins = None
                    for c4 in range(4):
                        k = 4 * half_ + c4
                        ins = e.transpose(ab[:, 128 * c4:128 * c4 + 128], stage[si][:, 128 * k:128 * k + 128], ident_f[:, :])
                    return ins
                pe_fn(ftr, STG + [CT], [abt])
                if i < 2:
                    dst = xcT[:, 4 * half_:4 * half_ + 4, 128 * i:128 * i + 128]
                    dT = T("xcT")
                else:
                    tok = 128 * (i - 2)
                    dst = xT[:, 4 * half_:4 * half_ + 4, tok:tok + 128]
                    dT = T(f"xT{tok // 512}_{(tok % 512) // 256}")
                if half_ == 0:
                    act(dst, ab[:, 0:512].rearrange("p (c t) -> p c t", c=4), AF.Copy, [abt], [dT], scale=ALPHA)
                else:
                    ts("dve", dst, ab[:, 0:512].rearrange("p (c t) -> p c t", c=4), ALPHA, ALU.mult, [abt], [dT])
        for l in range(nlayers):
            if s_ == 0 and l == 1:
                pass
            dma("sp", bmlib[:, :, :, :].rearrange("p h d q -> p (h d q)"), bmsc_d[l], [BMC], [T("bmlib")], T("bmlib"))
            bg_drain()
            if l == 1:
                while castq:
                    castq.pop(0)()
            if s_ == 0 and l == 1:
                P.op("sp", None, writes=STG, nosig=True)
                mod_layer(1, [(stage[i].rearrange("p (k c) -> p k c", k=8), T(["stgm0", "stgm1", "hyT0", "hyT1"][i]), 128) for i in range(4)])
                derive_lnh2(1)
            wk, wkt = w_get(l, P_KV0, la=1)
            wv, wvt = w_get(l, P_KV1, la=1)
            pass1_all(l, s_, wk, wkt, wv, wvt)
            if s_ == 0 and l == 0 and nlayers > 1:
                emit_casts(1, castq)
            tiles = [(l, s_, t, False) for t in range(4)]
            if l < nlayers - 1:
                tiles.append((l, 4, 0, True))
            for ti, tl in enumerate(tiles):
                tile_pass2(*tl, nxt_tile=(tiles[ti + 1] if ti + 1 < len(tiles) else None))
                pool_ok[0] = True
        bg_drain()
        for i in range(16):
            si = i % 4
            tok = 128 * i
            for half_ in range(2):
                ab, abt = io_bank()

                def ftr2(e, ab=ab, tok=tok, half_=half_):
                    ins = None
                    for c4 in range(4):
                        k = 4 * half_ + c4
                        ins = e.transpose(ab[:, 128 * c4:128 * c4 + 128], xT[:, k, tok:tok + 128], ident_f[:, :])
                    return ins
                pe_fn(ftr2, [T(f"xT{tok // 512}_{(tok % 512) // 256}"), CT], [abt])
                cp("act" if half_ == 0 else "dve", stage[si][:, 512 * half_:512 * half_ + 512], ab[:, 0:512], [abt], STG)
            dma("sp", y_d[s_, tok:tok + 128, :], stage[si], STG, [], T("hid"))
    if dbg:
        pass
    P.op("sp", None, writes=STG, nosig=True)
    if "dbgsem" in tk:
        P.op("sp", None, writes=[T("dbgsem")], nosig=True)
    P.emit()
    return nc


_CACHE = {}


def kernel(**inputs):
    x = np.ascontiguousarray(inputs["x"], dtype=np.float32)
    B = x.shape[0]
    assert B == NCORES * BPC
    consts = host_consts(np.asarray(inputs["na_rpb"], dtype=np.float32))
    if "nc" not in _CACHE:
        _CACHE["nc"] = build_program(BPC, NL)
    nc = _CACHE["nc"]
    shared = {k: np.ascontiguousarray(np.asarray(inputs[k], dtype=np.float32)) for k in
              ["c_ctx", "w_mod", "b_mod", "w_in", "a_ln_g", "a_ln_b", "a_ws", "a_bs", "sw_sink", "w_out",
               "ln1_g", "ln1_b", "w1", "w2", "ln2_g", "ln2_b"]}
    shared.update(consts)
    in_maps = []
    for i in range(NCORES):
        m = dict(shared)
        m["x"] = x[BPC * i:BPC * (i + 1)]
        m["c"] = np.ascontiguousarray(np.asarray(inputs["c"], dtype=np.float32)[BPC * i:BPC * (i + 1)])
        m["ctx"] = np.ascontiguousarray(np.asarray(inputs["ctx"], dtype=np.float32)[BPC * i:BPC * (i + 1)])
        in_maps.append(m)
    res = run_bass_kernel_spmd(nc, in_maps, core_ids=list(range(NCORES)))
    out = np.concatenate([np.asarray(r["y"], dtype=np.float32) for r in res.results], axis=0)
    return out
```

```python
import numpy as np
import ml_dtypes
from contextlib import ExitStack
import concourse.bass as bass
import concourse.mybir as mybir
from concourse.bass_utils import run_bass_kernel_spmd

F32 = mybir.dt.float32
BF16 = mybir.dt.bfloat16
AF = mybir.ActivationFunctionType
ALU = mybir.AluOpType

D = 1024
KC = 8
S = 2048
CL = 256
STOT = S + CL
NL = 2
DFF = 4096
ALPHA = float((2 * NL) ** 0.25)
LN_EPS = 1e-5
NEG = -30000.0
NCORES = 8
BPC = 4
SAME_SYNC = True

P_KV0, P_KV1, P_Q0, P_Q1, P_Q2, P_WO0, P_WO1 = 0, 1, 2, 3, 4, 5, 6
P_W1 = 7
P_W2 = 15
NPIECE = 23


class Tok:
    __slots__ = ("name", "w", "r", "sem", "cnt", "excl")

    def __init__(self, name, excl=False):
        self.name = name
        self.w = {}
        self.r = {}
        self.sem = None
        self.cnt = 0
        self.excl = excl


class Prog:
    ENGS = ("pe", "act", "dve", "pool", "sp")

    def __init__(self, nc, es):
        self.nc = nc
        self.es = es
        self.ops = {e: [] for e in self.ENGS}
        self.seen = {e: {} for e in self.ENGS}
        self.cur = {}
        self.cnt = {}
        self.nsem = 0
        self.new_epoch()

    def alloc_sem(self, name):
        self.nsem += 1
        return self.es.enter_context(self.nc.semaphore(f"{name}_{self.nsem}"))

    def new_epoch(self):
        for e in ("pe", "act", "dve", "pool"):
            self.cur[e] = self.alloc_sem("e" + e)
            self.cnt[e] = 0

    def op(self, eng, fn, reads=(), writes=(), dma=None, nosig=False):
        deps = {}

        def add(rec):
            sem, val, oeng, isdma = rec
            if (not isdma) and oeng == eng and (eng == "pe" or not SAME_SYNC):
                return
            k = id(sem)
            if k not in deps or deps[k][1] < val:
                deps[k] = (sem, val)

        for t in reads:
            for rec in t.w.values():
                add(rec)
            if t.excl:
                for rec in t.r.values():
                    add(rec)
        for t in writes:
            for rec in t.w.values():
                add(rec)
            for rec in t.r.values():
                add(rec)
        waits = []
        sn = self.seen[eng]
        for (s, v) in deps.values():
            if sn.get(id(s), 0) < v:
                waits.append((s, v))
                sn[id(s)] = v
        if nosig:
            self.ops[eng].append((fn, waits, None, 0))
            return
        if dma is not None:
            if dma.sem is None:
                dma.sem = self.alloc_sem("d" + dma.name)
            dma.cnt += 1
            sig = (dma.sem, 16 * dma.cnt)
            key = ("dma", id(dma.sem))
            rec = (sig[0], sig[1], eng, True)
            inc = 16
        else:
            self.cnt[eng] += 1
            sig = (self.cur[eng], self.cnt[eng])
            key = eng
            rec = (sig[0], sig[1], eng, False)
            inc = 1
        for t in reads:
            (t.w if t.excl else t.r)[key] = rec
        for t in writes:
            t.w[key] = rec
        self.ops[eng].append((fn, waits, sig, inc))

    def emit(self):
        with self.nc.Block() as block:
            def mk(name):
                def body(e):
                    for fn, waits, sig, inc in self.ops[name]:
                        for s, v in waits:
                            e.wait_ge(s, v)
                        if fn is None:
                            continue
                        ins = fn(e)
                        if sig is not None:
                            ins.then_inc(sig[0], inc)
                return body
            block.tensor(mk("pe"))
            block.scalar(mk("act"))
            block.vector(mk("dve"))
            block.gpsimd(mk("pool"))
            block.sync(mk("sp"))


def host_consts(na_rpb):
    c = {}
    c["ident_f"] = np.eye(128, dtype=np.float32)
    c["ident_b"] = np.eye(128, dtype=np.float32).astype(ml_dtypes.bfloat16)
    Pm = np.zeros((128, 128), np.float32)
    for m in range(128):
        if (m % 32) < 16:
            Pm[m, m + 16] = -1.0
        else:
            Pm[m, m - 16] = 1.0
    c["prope"] = np.ascontiguousarray(Pm.T).astype(ml_dtypes.bfloat16)
    inv_freq = (np.float32(10000.0) ** (-np.arange(16, dtype=np.float32) / np.float32(16))).astype(np.float32)
    pos = np.arange(S)
    row = (pos // 64).astype(np.float32)
    col = (pos % 64).astype(np.float32)
    cs = np.zeros((128, 2, S), np.float32)
    for p in range(128):
        d = p % 64
        if d < 32:
            ang = row * inv_freq[d % 16]
        else:
            ang = col * inv_freq[(d - 32) % 16]
        ang = ang.astype(np.float32)
        cs[p, 0] = np.cos(ang)
        cs[p, 1] = np.sin(ang)
    c["cs"] = cs
    ki = np.arange(128)[:, None]
    qi = np.arange(128)[None, :]
    msk = np.zeros((128, 2, 128), np.float32)
    msk[:, 0, :] = np.where(qi <= ki, 0.0, NEG)
    msk[:, 1, :] = np.where(ki <= qi, 0.0, NEG)
    c["swmask"] = msk.astype(ml_dtypes.bfloat16)
    cq = np.arange(64)
    cst = np.clip(cq - 8, 0, 48)
    col_ok = (cq[None, :] >= cst[:, None]) & (cq[None, :] < cst[:, None] + 16)
    dc = np.clip(cq[None, :] - cq[:, None], -15, 15) + 15
    Tb = na_rpb[:, :, :, dc]
    Tb = np.where(col_ok[None, None, None], Tb, np.float32(NEG)).astype(np.float32)
    Tb = Tb.transpose(0, 1, 2, 4, 3)
    lib = np.empty((NL, 2, 64, 4, 14, 64), np.float32)
    for dr0 in range(14):
        lib[:, 0, :, :, dr0, :] = Tb[:, :, dr0].transpose(0, 2, 1, 3)
        lib[:, 1, :, :, dr0, :] = Tb[:, :, dr0 + 1].transpose(0, 2, 1, 3)
    c["bmlib"] = np.ascontiguousarray(lib.reshape(NL, 128, 4 * 14 * 64))
    return c


def build_program(nseq=BPC, nlayers=NL, dbg=None):
    nc = bass.Bass("TRN2", target_bir_lowering=False)
    es = ExitStack()
    P = Prog(nc, es)

    def din(name, shape, dt=F32):
        return nc.dram_tensor(name, list(shape), dt, kind="ExternalInput").ap()

    x_d = din("x", [nseq, S, D])
    c_d = din("c", [nseq, D])
    ctx_d = din("ctx", [nseq, CL, D])
    cctx_d = din("c_ctx", [D])
    wmod_d = din("w_mod", [NL, D, 6 * D])
    bmod_d = din("b_mod", [NL, 6 * D])
    win_d = din("w_in", [NL, D, 2048])
    alng_d = din("a_ln_g", [NL, 256])
    alnb_d = din("a_ln_b", [NL, 256])
    aws_d = din("a_ws", [NL, 4, 128, 128])
    abs_d = din("a_bs", [NL, 4, 128])
    sink_d = din("sw_sink", [NL, 8])
    wout_d = din("w_out", [NL, D, D])
    ln1g_d = din("ln1_g", [NL, D])
    ln1b_d = din("ln1_b", [NL, D])
    w1_d = din("w1", [NL, D, DFF])
    w2_d = din("w2", [NL, DFF, D])
    ln2g_d = din("ln2_g", [NL, D])
    ln2b_d = din("ln2_b", [NL, D])
    identf_d = din("ident_f", [128, 128])
    identb_d = din("ident_b", [128, 128], BF16)
    prope_d = din("prope", [128, 128], BF16)
    cs_d = din("cs", [128, 2, S])
    swmask_d = din("swmask", [128, 2, 128], BF16)
    bmlib_d = din("bmlib", [NL, 128, 3584])
    y_d = nc.dram_tensor("y", [nseq, S, D], F32, kind="ExternalOutput").ap()
    wsc_d = nc.dram_tensor("wsc", [NL, NPIECE, 128, 4096], BF16, kind="Internal").ap()
    bmsc_d = nc.dram_tensor("bmsc", [NL, 128, 3584], BF16, kind="Internal").ap()
    dbg_out = {}
    if dbg:
        for name, shape in dbg.items():
            dbg_out[name] = nc.dram_tensor("dbg_" + name, list(shape), F32, kind="ExternalOutput").ap()

    def sb(name, shape, dt):
        return es.enter_context(nc.sbuf_tensor(name, list(shape), dt))

    xT = sb("xT", [128, KC, S], F32)
    xcT = sb("xcT", [128, KC, CL], F32)
    kTna = sb("kTna", [128, 2, STOT], BF16)
    kTsw = sb("kTsw", [128, 2, STOT], BF16)
    Vall = sb("Vall", [128, 18, 8, 64], BF16)
    wsl = [sb(f"wsl{i}", [128, 4096], BF16) for i in range(3)]
    hyT = sb("hyT", [128, KC, 512], BF16)
    hid = sb("hid", [128, 8, 512], BF16)
    uT = sb("uT", [128, 2, 512], BF16)
    qTna = sb("qTna", [128, 2, 512], BF16)
    qTsw = sb("qTsw", [128, 4, 512], BF16)
    qb = [sb(f"qb{i}", [128, 256], BF16) for i in range(2)]
    vln = sb("vln", [128, 4, 256], BF16)
    PT = [sb(f"PT{i}", [128, 640], BF16) for i in range(3)]
    tmpF = [sb(f"tmpF{i}", [128, 256], F32) for i in range(3)]
    zbs = [sb(f"zb{i}", [128, 256], BF16) for i in range(2)]
    zsqs = [sb(f"zsq{i}", [128, 256], BF16) for i in range(2)]
    st_mean = sb("st_mean", [128, 512], F32)
    st_rstd = sb("st_rstd", [128, 512], F32)
    st_nmr = sb("st_nmr", [128, 512], F32)
    csb = sb("csb", [128, 2, 512], F32)
    bmlib = sb("bmlib_sb", [128, 4, 14, 64], BF16)
    Bb = sb("Bb", [128, NL, 2, 128], F32)
    WsT = sb("WsT", [128, NL, 4, 128], BF16)
    alnG = sb("alnG", [128, NL, 256], F32)
    alnB = sb("alnB", [128, NL, 256], F32)
    modT = sb("modT", [128, NL, 48, 5], F32)
    lnc = sb("lnc", [128, NL, 4, KC], F32)
    es_t = sb("es_t", [128, NL, 4], F32)
    lnh2 = sb("lnh2", [128, NL, 5, 2, KC], F32)
    lnt = [sb(f"lnt{i}", [128, 256], F32) for i in range(3)]
    TA = sb("TA", [128, 128], F32)
    TB = sb("TB", [128, 72], F32)
    rowsA = sb("rowsA", [128, 128], F32)
    rowsB = sb("rowsB", [72, 128], F32)
    csT = sb("csT", [128, 5, 8], F32)
    ones_f = sb("ones_f", [1, 128], F32)
    ident_f = sb("ident_f_sb", [128, 128], F32)
    ident_b = sb("ident_b_sb", [128, 128], BF16)
    prope = sb("prope_sb", [128, 128], BF16)
    swmask = sb("swmask_sb", [128, 2, 128], BF16)
    ones_b = sb("ones_b", [128, 128], BF16)
    mv = sb("mv", [128, 4, 8], F32)
    epsc = sb("epsc", [128, 1], F32)
    vg = [sb(f"vg{i}", [128, 256], F32) for i in range(2)]
    mvr = sb("mvr", [128, 4], F32)

    ps = [es.enter_context(nc.psum_tensor(f"ps{i}", [128, 512], F32)) for i in range(8)]
    psT = [Tok(f"ps{i}", excl=True) for i in range(8)]

    tk = {}

    def T(name):
        if name not in tk:
            tk[name] = Tok(name)
        return tk[name]

    rr = {}

    def nxt(name, n):
        rr[name] = (rr.get(name, -1) + 1) % n
        return rr[name]

    def acc_bank():
        i = nxt("acc", 2)
        return ps[i], psT[i]

    def s_bank():
        if rr.get("sb4"):
            i = [2, 3, 0, 1][nxt("sb4r", 4)]
        else:
            i = 2 + nxt("sb", 2)
        return ps[i], psT[i]

    def o_bank():
        i = 4 + nxt("ob", 2)
        return ps[i], psT[i]

    def aux_bank():
        i = 6 + nxt("aux", 2)
        return ps[i], psT[i]

    def io_bank():
        i = nxt("iob", 8)
        return ps[i], psT[i]

    def tmp():
        i = nxt("tmpF", 3)
        return tmpF[i], T(f"tmpF{i}")

    def pt():
        i = nxt("PT", 3)
        return PT[i], T(f"PT{i}")

    def dma(eng, out, in_, reads, writes, tok):
        P.op(eng, lambda e: e.dma_start(out=out, in_=in_), reads=reads, writes=writes, dma=tok)

    def mm_group(out, pairs, reads, writes):
        n = len(pairs)

        def fn(e):
            ins = None
            for i, (l, r) in enumerate(pairs):
                ins = e.matmul(out, lhsT=l, rhs=r, start=(i == 0), stop=(i == n - 1))
            return ins
        P.op("pe", fn, reads=reads, writes=writes)

    def pe_fn(fn, reads, writes):
        P.op("pe", fn, reads=reads, writes=writes)

    def act(out, in_, func, reads, writes, bias=None, scale=None):
        kw = {}
        if bias is not None:
            kw["bias"] = bias
        if scale is not None:
            kw["scale"] = scale
            if func == AF.Copy:
                func = AF.Identity
        P.op("act", lambda e: e.activation(out=out, in_=in_, func=func, **kw), reads=reads, writes=writes)

    pool_ok = [False]

    def tt(eng, out, in0, in1, op, reads, writes):
        if eng == "pool" and not pool_ok[0]:
            eng = "dve"
        P.op(eng, lambda e: e.tensor_tensor(out=out, in0=in0, in1=in1, op=op), reads=reads, writes=writes)

    def ts(eng, out, in0, s1, op0, reads, writes, s2=None, op1=None):
        if op1 is None:
            P.op(eng, lambda e: e.tensor_scalar(out=out, in0=in0, scalar1=s1, scalar2=None, op0=op0), reads=reads, writes=writes)
        else:
            P.op(eng, lambda e: e.tensor_scalar(out=out, in0=in0, scalar1=s1, scalar2=s2, op0=op0, op1=op1), reads=reads, writes=writes)

    def stt(eng, out, in0, scalar, in1, op0, op1, reads, writes):
        P.op(eng, lambda e: e.scalar_tensor_tensor(out=out, in0=in0, scalar=scalar, in1=in1, op0=op0, op1=op1),
             reads=reads, writes=writes)

    def cp(eng, out, in_, reads, writes):
        if eng == "act":
            act(out, in_, AF.Copy, reads, writes)
        else:
            P.op(eng, lambda e: e.tensor_copy(out=out, in_=in_), reads=reads, writes=writes)

    def recip(out, in_, reads, writes):
        P.op("dve", lambda e: e.reciprocal(out=out, in_=in_), reads=reads, writes=writes)

    def memset(eng, ap, val, writes):
        P.op(eng, lambda e: e.memset(ap, val), writes=writes)

    def dbg_dump(name, ap, tok):
        if name in dbg_out:
            dma("sp", dbg_out[name], ap, [tok], [], T("dbgsem"))

    castT = {}

    def cast_tok(l, g):
        k = (l, g)
        if k not in castT:
            castT[k] = Tok(f"cast{l}_{g}")
        return castT[k]

    def piece_group(pi):
        if pi <= P_Q2:
            return 0
        if pi <= P_WO1:
            return 1
        if pi < P_W2:
            return 2
        return 3

    castq = []

    def emit_casts(l, sink=None):
        def c(pi, dst, src):
            t = cast_tok(l, piece_group(pi))
            if sink is None:
                dma("pool", dst, src, [], [t], t)
            else:
                sink.append(lambda: dma("pool", dst, src, [], [t], t))
        winv = win_d[l].rearrange("(k p) c -> p k c", p=128)

        def pv(pi, ncols):
            return wsc_d[l, pi][:, 0:8 * ncols].rearrange("p (k c) -> p k c", k=8)
        kv0 = pv(P_KV0, 512)
        c(P_KV0, kv0[:, :, 0:256], winv[:, :, 768:1024])
        c(P_KV0, kv0[:, :, 256:320], winv[:, :, 1792:1856])
        c(P_KV0, kv0[:, :, 320:384], winv[:, :, 1792:1856])
        c(P_KV0, kv0[:, :, 384:448], winv[:, :, 1856:1920])
        c(P_KV0, kv0[:, :, 448:512], winv[:, :, 1856:1920])
        kv1 = pv(P_KV1, 384)
        c(P_KV1, kv1[:, :, 0:256], winv[:, :, 1024:1280])
        c(P_KV1, kv1[:, :, 256:384], winv[:, :, 1920:2048])
        q0 = pv(P_Q0, 512)
        c(P_Q0, q0[:, :, 0:256], winv[:, :, 0:256])
        c(P_Q0, q0[:, :, 256:512], winv[:, :, 512:768])
        c(P_Q1, pv(P_Q1, 512), winv[:, :, 1280:1792])
        c(P_Q2, pv(P_Q2, 256), winv[:, :, 256:512])
        wov = wout_d[l].rearrange("(k p) c -> p k c", p=128)
        c(P_WO0, pv(P_WO0, 512), wov[:, :, 0:512])
        c(P_WO1, pv(P_WO1, 512), wov[:, :, 512:1024])
        w1v = w1_d[l].rearrange("(k p) c -> p k c", p=128)
        for i in range(8):
            c(P_W1 + i, pv(P_W1 + i, 512), w1v[:, :, 512 * i:512 * (i + 1)])
        w2v = w2_d[l].rearrange("(j p) d -> p j d", p=128)
        for qq in range(4):
            for hf in range(2):
                pi = P_W2 + qq * 2 + hf
                c(pi, pv(pi, 512), w2v[:, 8 * qq:8 * qq + 8, 512 * hf:512 * hf + 512])

    piece_used = {P_KV0: 4096, P_KV1: 3072, P_Q0: 4096, P_Q1: 4096, P_Q2: 2048, P_WO0: 4096, P_WO1: 4096}
    wseq = []
    wstate = {"issued": 0, "next": 0}

    def w_issue_upto(n):
        while wstate["issued"] < min(n, len(wseq)):
            i = wstate["issued"]
            l, pi = wseq[i]
            si = i % 3
            used = piece_used.get(pi, 4096)
            dma("sp", wsl[si][:, 0:used], wsc_d[l, pi][:, 0:used], [cast_tok(l, piece_group(pi))], [T(f"wsl{si}")], T(f"wsl{si}"))
            wstate["issued"] += 1

    def w_get(l, pi, la=2):
        i = wstate["next"]
        assert wseq[i] == (l, pi), (wseq[i], l, pi)
        w_issue_upto(i + 1 + la)
        wstate["next"] += 1
        si = i % 3
        return wsl[si], T(f"wsl{si}")

    CT = T("consts")
    for (dst, src) in [(ident_f[:, :], identf_d), (ident_b[:, :], identb_d), (prope[:, :], prope_d),
                       (swmask[:, :, :], swmask_d)]:
        dma("sp", dst, src, [], [CT], CT)
    dma("sp", rowsA[0:96, :], bmod_d.rearrange("l (j p) -> (l j) p", p=128), [], [CT], CT)
    for l in range(NL):
        dma("sp", rowsA[96 + 16 * l:104 + 16 * l, :], ln1g_d[l].rearrange("(k p) -> k p", p=128), [], [CT], CT)
        dma("sp", rowsA[104 + 16 * l:112 + 16 * l, :], ln1b_d[l].rearrange("(k p) -> k p", p=128), [], [CT], CT)
        dma("sp", rowsB[16 * l:16 * l + 8, :], ln2g_d[l].rearrange("(k p) -> k p", p=128), [], [CT], CT)
        dma("sp", rowsB[16 * l + 8:16 * l + 16, :], ln2b_d[l].rearrange("(k p) -> k p", p=128), [], [CT], CT)
    memset("dve", rowsB[32:64, :], 0.0, [CT])
    dma("sp", rowsB[32:32 + 8 * nseq, :], c_d.rearrange("b (k p) -> (b k) p", p=128), [], [CT], CT)
    dma("sp", rowsB[64:72, :], cctx_d.rearrange("(k p) -> k p", p=128), [], [CT], CT)
    rowbuf = {(0, 0): (st_mean, "st_mean"), (0, 1): (st_rstd, "st_rstd"), (1, 0): (st_nmr, "st_nmr"), (1, 1): (csb[:, 0, :], "csb")}
    wsraw = csb[:, 1, :].rearrange("p (g j) -> p g j", g=4)
    for l in range(NL):
        r0, r0n = rowbuf[(l, 0)]
        r1, r1n = rowbuf[(l, 1)]
        dma("sp", r0[0:1, 0:256], alng_d[l:l + 1, :], [], [T(r0n)], T(r0n))
        dma("sp", r0[0:1, 256:512], alnb_d[l:l + 1, :], [], [T(r0n)], T(r0n))
        dma("sp", r1[0:1, 0:512], abs_d[l:l + 1].rearrange("o g i -> o (g i)"), [], [T(r1n)], T(r1n))
    sinkr = sb("sinkr", [1, 16], F32)
    dma("sp", sinkr[0:1, 0:16], sink_d.rearrange("(o l) h -> o (l h)", o=1), [], [CT], CT)
    memset("dve", ones_f[:, :], 1.0, [T("ones_f")])
    memset("dve", ones_b[:, :], 1.0, [T("ones_b")])
    memset("dve", epsc[:, :], LN_EPS, [T("epsc")])

    BMC = Tok("bmcast")
    for l in range(NL):
        dma("pool", bmsc_d[l], bmlib_d[l], [], [BMC], BMC)
    emit_casts(0)

    b0, bt0 = aux_bank()
    pe_fn(lambda e: e.transpose(b0[:, 0:128], rowsA[:, :], ident_f[:, :]), [CT], [bt0])
    cp("dve", TA[:, :], b0[:, 0:128], [bt0], [T("TA")])
    b1, bt1 = aux_bank()
    pe_fn(lambda e: e.transpose(b1[:, 0:72], rowsB[0:72, :], ident_f[0:72, 0:72]), [CT], [bt1])
    cp("dve", TB[:, :], b1[:, 0:72], [bt1], [T("TB")])
    act(csT[:, :, :], TB[:, 32:72].rearrange("p (b k) -> p b k", k=8), AF.Silu, [T("TB")], [T("csT")])

    def mod_layer(l, bufs):
        mb_, mbt = aux_bank()
        j = 0
        bi = 0
        while j < 48:
            ap, tok_, ncol = bufs[bi % len(bufs)]
            bi += 1
            nj = ncol // 128
            dma("sp", ap, wmod_d[l].rearrange("(k p) c -> p k c", p=128)[:, :, 128 * j:128 * j + ncol], [], [tok_], tok_)
            for jj in range(nj):
                mm_group(mb_[:, 5 * (j + jj):5 * (j + jj) + 5],
                         [(ap[:, k, 128 * jj:128 * (jj + 1)], csT[:, :, k]) for k in range(8)],
                         [tok_, T("csT")], [mbt])
            j += nj
        for b in range(5):
            tt("dve", modT[:, l, :, b], mb_[:, 0:240].rearrange("p (j b) -> p j b", b=5)[:, :, b],
               TA[:, 48 * l:48 * (l + 1)], ALU.add, [mbt, T("TA")], [T("modT")])
        for kind in (1, 4):
            ts("dve", modT[:, l, 8 * kind:8 * kind + 8, :], modT[:, l, 8 * kind:8 * kind + 8, :], 1.0, ALU.add,
               [T("modT")], [T("modT")], s2=1.0 / ALPHA, op1=ALU.mult)

    def derive_lnh2(l):
        for b in range(5):
            tt("dve", lnh2[:, l, b, 0, :], lnc[:, l, 0, :], modT[:, l, 32:40, b], ALU.mult, [T("lnc"), T("modT")], [T("lnh2")])
            tt("dve", lnh2[:, l, b, 1, :], lnc[:, l, 1, :], modT[:, l, 32:40, b], ALU.mult, [T("lnc"), T("modT")], [T("lnh2")])
            tt("dve", lnh2[:, l, b, 1, :], lnh2[:, l, b, 1, :], modT[:, l, 24:32, b], ALU.add, [T("lnh2"), T("modT")], [T("lnh2")])

    mod_layer(0, [(wsl[i][:, :].bitcast(F32).rearrange("p (k c) -> p k c", k=8), T(f"wsl{i}"), 256) for i in range(3)])
    for l in range(NL):
        a2 = ALPHA if l < nlayers - 1 else 1.0
        ts("dve", lnc[:, l, 0, :], TA[:, 96 + 16 * l:104 + 16 * l], ALPHA, ALU.mult, [T("TA")], [T("lnc")])
        ts("dve", lnc[:, l, 1, :], TA[:, 104 + 16 * l:112 + 16 * l], ALPHA, ALU.mult, [T("TA")], [T("lnc")])
        ts("dve", lnc[:, l, 2, :], TB[:, 16 * l:16 * l + 8], a2, ALU.mult, [T("TB")], [T("lnc")])
        ts("dve", lnc[:, l, 3, :], TB[:, 16 * l + 8:16 * l + 16], a2, ALU.mult, [T("TB")], [T("lnc")])
    derive_lnh2(0)
    for l in range(NL):
        bb_, bbt = aux_bank()
        r0, r0n = rowbuf[(l, 0)]
        r1, r1n = rowbuf[(l, 1)]
        mm_group(bb_[:, 0:512], [(ones_f[0:1, :], r0[0:1, 0:512])], [T(r0n), T("ones_f")], [bbt])
        cp("dve", alnG[:, l, :], bb_[:, 0:256], [bbt], [T("alnGB")])
        cp("dve", alnB[:, l, :], bb_[:, 256:512], [bbt], [T("alnGB")])
        b2_, b2t = aux_bank()

        def fbb(e, l=l, b2_=b2_, r1=r1):
            ins = None
            for m in range(2):
                for hh in range(2):
                    g = 2 * m + hh
                    ins = e.matmul(b2_[64 * hh:64 * hh + 64, 128 * m:128 * m + 128], lhsT=ones_f[0:1, 0:64],
                                   rhs=r1[0:1, 128 * g:128 * g + 128],
                                   start=True, stop=True)
            return ins
        pe_fn(fbb, [T(r1n), T("ones_f")], [b2t])
        cp("dve", Bb[:, l, :, :], b2_[:, 0:256].rearrange("p (m i) -> p m i", m=2), [b2t], [T("Bb")])
        b3_, b3t = aux_bank()

        def fes(e, l=l, b3_=b3_):
            sr = sinkr[0:1, 8 * l:8 * l + 8].rearrange("o (i t) -> o i t", t=2)
            e.matmul(b3_[64:128, 0:4], lhsT=ones_f[0:1, 0:64], rhs=sr[:, :, 0], start=True, stop=True)
            return e.matmul(b3_[0:64, 0:4], lhsT=ones_f[0:1, 0:64], rhs=sr[:, :, 1], start=True, stop=True)
        pe_fn(fes, [CT, T("ones_f")], [b3t])
        act(es_t[:, l, :], b3_[:, 0:4], AF.Exp, [b3t], [T("es_t")])
        dma("sp", wsraw, aws_d[l].rearrange("g i j -> i g j"), [], [T("wsraw")], T("wsraw"))
        b4_, b4t = aux_bank()

        def fws(e, b4_=b4_):
            ins = None
            for g in range(4):
                ins = e.transpose(b4_[:, 128 * g:128 * g + 128], wsraw[:, g, :], ident_f[:, :])
            return ins
        pe_fn(fws, [T("wsraw"), CT], [b4t])
        cp("dve", WsT[:, l, :, :], b4_[:, 0:512].rearrange("p (g i) -> p g i", g=4), [b4t], [T("WsT")])

    def tile_pieces(l, nh):
        out = [(l, P_Q2), (l, P_Q0), (l, P_Q1), (l, P_WO0), (l, P_WO1)]
        for qq in range(4):
            q4 = [(l, P_W1 + 2 * qq), (l, P_W1 + 2 * qq + 1), (l, P_W2 + 2 * qq), (l, P_W2 + 2 * qq + 1)]
            out += q4
            if qq == 0 and nh == 2:
                out += q4
        return out
    for s_ in range(nseq):
        for l in range(nlayers):
            wseq.extend([(l, P_KV0), (l, P_KV1)])
            for _ in range(4):
                wseq.extend(tile_pieces(l, 2))
            if l < nlayers - 1:
                wseq.extend(tile_pieces(l, 1))

    HY = [T("hyT0"), T("hyT1")]

    def mcol(l, kind, k, b):
        return modT[:, l, 8 * kind + k, b:b + 1]

    def modulate(src_of_k, srcTs, l, kind_sh, kind_s, b, c0, Tn, hyts):
        for k in range(KC):
            act(hyT[:, k, c0:c0 + Tn], src_of_k(k), AF.Identity, list(srcTs) + [T("modT")], hyts,
                bias=mcol(l, kind_sh, k, b), scale=mcol(l, kind_s, k, b))

    def rope_evac(acc, acct, dst, dstT, cc0, Tn, scale):
        i = nxt("qb", 2)
        q_b, q_bt = qb[i], T(f"qb{i}")
        act(q_b[:, 0:Tn], acc[:, 0:Tn], AF.Copy, [acct], [q_bt], scale=scale)

        def rest():
            ab, abt = aux_bank()
            mm_group(ab[:, 0:Tn], [(prope[:, :], q_b[:, 0:Tn])], [CT, q_bt], [abt])
            t1, t1t = tmp()
            stt("dve", t1[:, 0:Tn], acc[:, 0:Tn], scale, csb[:, 0, cc0:cc0 + Tn], ALU.mult, ALU.mult, [acct, T("csb")], [t1t])
            t2, t2t = tmp()
            tt("dve", t2[:, 0:Tn], ab[:, 0:Tn], csb[:, 1, cc0:cc0 + Tn], ALU.mult, [abt, T("csb")], [t2t])
            tt("pool", dst, t1[:, 0:Tn], t2[:, 0:Tn], ALU.add, [t1t, t2t], [dstT])
        return rest

    def load_cs(tok0, Tn):
        dma("sp", csb[:, :, 0:Tn], cs_d[:, :, tok0:tok0 + Tn], [], [T("csb")], T("csb"))

    def pass1_all(l, b, wk, wkt, wv, wvt):
        wk3 = wk[:, :].rearrange("p (k c) -> p k c", k=8)
        wv3 = wv[:, 0:3072].rearrange("p (k c) -> p k c", k=8)
        groups = []
        for g in range(8):
            t, h = g // 2, g % 2
            groups.append(dict(src=(lambda k, g=g: xT[:, k, 256 * g:256 * g + 256]), srcT=[T(f"xT{t}_{h}")], tok0=256 * g,
                               is_ctx=False, b=b))
        groups.append(dict(src=(lambda k: xcT[:, k, :]), srcT=[T("xcT")], tok0=S, is_ctx=True, b=4))

        def mod(gi):
            g = groups[gi]
            hh = gi % 2
            modulate(g["src"], g["srcT"], l, 0, 1, g["b"], 256 * hh, 256, [HY[hh]])
        mod(0)
        pend = []
        for gi, g in enumerate(groups):
            hh = gi % 2
            c0 = 256 * hh
            tok0 = g["tok0"]
            if gi + 1 < len(groups):
                mod(gi + 1)
            if (not g["is_ctx"]) and (tok0 % 512 == 0):
                load_cs(tok0, 512)
            for ch in range(4):
                acc, acct = acc_bank()
                mm_group(acc[:, 0:256], [(wk3[:, k, 128 * ch:128 * ch + 128], hyT[:, k, c0:c0 + 256]) for k in range(KC)],
                         [wkt, HY[hh]], [acct])
                while pend:
                    pend.pop(0)()
                if ch < 2:
                    cp("act", kTna[:, ch, tok0:tok0 + 256], acc[:, 0:256], [acct], [T("kTna")])
                elif g["is_ctx"]:
                    cp("act", kTsw[:, ch - 2, tok0:tok0 + 256], acc[:, 0:256], [acct], [T("kTsw")])
                else:
                    pend.append(rope_evac(acc, acct, kTsw[:, ch - 2, tok0:tok0 + 256], T("kTsw"), tok0 % 512, 256, 1.0))
            for tt_ in range(2):
                acc, acct = acc_bank()
                mm_group(acc[:, 0:384], [(hyT[:, k, c0 + 128 * tt_:c0 + 128 * tt_ + 128], wv3[:, k, :]) for k in range(KC)],
                         [wvt, HY[hh]], [acct])
                while pend:
                    pend.pop(0)()
                vt_ = tok0 // 128 + tt_
                a3 = acc[:, 0:384].rearrange("p (b d) -> p b d", d=64)
                cp("dve", Vall[:, vt_, 0:3:2, :], a3[:, 0:2, :], [acct], [T("Vall")])
                cp("dve", Vall[:, vt_, 3:6:2, :], a3[:, 2:4, :], [acct], [T("Vall")])
                cp("dve", Vall[:, vt_, 6:8, :], a3[:, 4:6, :], [acct], [T("Vall")])

    class Half:
        def __init__(self, l, b, t, h, is_ctx):
            self.l, self.b, self.t, self.h, self.is_ctx = l, b, t, h, is_ctx
            self.Tn = 256
            self.c0 = 0 if is_ctx else 256 * h
            self.tok0 = 0 if is_ctx else 512 * t + 256 * h
            self.cs = slice(self.c0, self.c0 + 256)
            self.xTok = T("xcT") if is_ctx else T(f"xT{t}_{h}")
            self.hy = T(f"hyT{h}")
            self.uTt = T(f"uT{h}")
            self.qnat = T(f"qTna{h}")
            self.qswt = T(f"qTsw{h}")
            self.vlnt = T(f"vln{h}")
            self.hidt = T(f"hid{h}")
            self.mvt = T(f"mv{h}")
            self.stt_ = T(f"st{h}")
            sb_ = (6, 7) if h == 0 else (4, 5)
            self.s1, self.s1t, self.s2, self.s2t = ps[sb_[0]], psT[sb_[0]], ps[sb_[1]], psT[sb_[1]]

        def xs(self, k):
            if self.is_ctx:
                return xcT[:, k, :]
            return xT[:, k, self.tok0:self.tok0 + 256]

    def finish_head(ob, obt, hp, ncols, dst, dstT, l, es_pair):
        dp = 1 - hp
        dsl = slice(64 * dp, 64 * dp + 64)
        osl = slice(64 * hp, 64 * hp + 64)
        t1, t1t = tmp()
        if es_pair is not None:
            ts("dve", t1[dsl, 0:ncols], ob[dsl, 0:ncols], es_t[dsl, l, es_pair:es_pair + 1], ALU.add,
               [obt, T("es_t")], [t1t])
            recip(t1[dsl, 0:ncols], t1[dsl, 0:ncols], [t1t], [t1t])
        else:
            recip(t1[dsl, 0:ncols], ob[dsl, 0:ncols], [obt], [t1t])
        tt("dve", dst, ob[osl, 0:ncols], t1[dsl, 0:ncols], ALU.mult, [obt, t1t], [dstT])

    def pv_group(e, ob, col0, n, hp, slots):
        ns = len(slots)
        ins = None
        for i, (vl, pr, ksl) in enumerate(slots):
            e.matmul(ob[64 * hp:64 * hp + 64, col0:col0 + n], lhsT=vl, rhs=pr, start=(i == 0), stop=(i == ns - 1))
            ins = e.matmul(ob[64 * (1 - hp):64 * (1 - hp) + 64, col0:col0 + n], lhsT=ones_b[ksl, 0:64], rhs=pr,
                           start=(i == 0), stop=(i == ns - 1))
        return ins

    def run_items(items):
        rr["sb4"] = True
        pend_pv = []
        for it in items:
            it["qk"]()
            bg_step(1)
            it["ex"]()
            pend_pv.append(it)
            if len(pend_pv) > 2:
                pend_pv.pop(0)["pv"]()
            bg_step(1)
            if castq:
                castq.pop(0)()
        while pend_pv:
            pend_pv.pop(0)["pv"]()
        rr["sb4"] = False

    def attention(H):
        l, t = H.l, H.t
        obs = {}

        def get_ob(key):
            if key not in obs:
                obs[key] = o_bank()
            return obs[key]
        items = []

        def mk_dense(kind, h):
            hp, hc = h % 2, h // 2
            psl = slice(64 * hp, 64 * hp + 64)
            if kind == "na":
                kT_, kTt, kch, qT_, qTt, vc0, ych, esp = kTna, T("kTna"), hc, qTna, H.qnat, [0, 2, 3, 5][h], 2 + hc, None
            else:
                kv = h // 4
                kT_, kTt, kch, qT_, qTt, vc0, ych, esp = kTsw, T("kTsw"), kv, qTsw, H.qswt, 6 + kv, 4 + hc, hc
            st = {}

            def qk():
                sbk, sbt = s_bank()
                st["s"] = (sbk, sbt)

                def f(e):
                    ins = None
                    for ci in range(2):
                        ins = e.matmul(sbk[:, 256 * ci:256 * ci + 256], lhsT=kT_[psl, kch, S + 128 * ci:S + 128 * ci + 128],
                                       rhs=qT_[psl, hc, H.cs], start=True, stop=True)
                    return ins
                pe_fn(f, [kTt, qTt], [sbt])

            def ex():
                sbk, sbt = st["s"]
                p_, p_t = pt()
                st["p"] = (p_, p_t)
                act(p_[:, 0:512], sbk[:, 0:512], AF.Exp, [sbt], [p_t])

            def pv():
                p_, p_t = st["p"]
                ob, obt = get_ob((kind, h))
                pe_fn(lambda e: pv_group(e, ob, 0, 256, hp,
                                         [(Vall[:, 16 + ci, vc0, :], p_[:, 256 * ci:256 * ci + 256], slice(0, 128)) for ci in range(2)]),
                      [p_t, T("Vall"), T("ones_b")], [obt])
                finish_head(ob, obt, hp, 256, hyT[psl, ych, H.cs], H.hy, l, esp)
            return {"qk": qk, "ex": ex, "pv": pv}

        def mk_na(h, rr_):
            hp, hc = h % 2, h // 2
            psl = slice(64 * hp, 64 * hp + 64)
            r = 8 * t + 4 * H.h + rr_
            rs = min(max(r - 4, 0), 24)
            p = rs % 2
            jt0 = (rs - p) // 2
            nsl = 5 if p else 4
            dr0 = (rs - p) - r + 7
            q0 = H.c0 + 64 * rr_
            ncol = 64 * (nsl + 2)
            b0 = [0, 1, 3, 4][h]
            st = {}

            def half(i):
                if p == 1 and i == 0:
                    return 1
                if p == 1 and i == nsl - 1:
                    return 0
                return None

            def qk():
                sbk, sbt = s_bank()
                st["s"] = (sbk, sbt)

                def f(e):
                    e.matmul(sbk[:, 0:64 * nsl], lhsT=ident_b[:, :],
                             rhs=bmlib[:, h, dr0:dr0 + 2 * nsl - 1:2, :], start=True, stop=False)
                    for i in range(nsl):
                        jt = jt0 + i
                        hf = half(i)
                        last = (i == nsl - 1)
                        if hf is None:
                            e.matmul(sbk[:, 64 * i:64 * i + 64], lhsT=kTna[psl, hc, 128 * jt:128 * jt + 128],
                                     rhs=qTna[psl, hc, q0:q0 + 64], start=False, stop=last)
                        else:
                            e.matmul(sbk[64 * hf:64 * hf + 64, 64 * i:64 * i + 64],
                                     lhsT=kTna[psl, hc, 128 * jt + 64 * hf:128 * jt + 64 * hf + 64],
                                     rhs=qTna[psl, hc, q0:q0 + 64], start=False, stop=last)
                    ins = None
                    for ci in range(2):
                        ins = e.matmul(sbk[:, 64 * (nsl + ci):64 * (nsl + ci) + 64],
                                       lhsT=kTna[psl, hc, S + 128 * ci:S + 128 * ci + 128],
                                       rhs=qTna[psl, hc, q0:q0 + 64], start=True, stop=True)
                    return ins
                pe_fn(f, [T("kTna"), H.qnat, T("bmlib"), CT], [sbt])

            def ex():
                sbk, sbt = st["s"]
                p_, p_t = pt()
                st["p"] = (p_, p_t)
                act(p_[:, 0:ncol], sbk[:, 0:ncol], AF.Exp, [sbt], [p_t])

            def pv():
                p_, p_t = st["p"]
                ob, obt = get_ob(("na", h))
                slots = []
                for i in range(nsl):
                    hf = half(i)
                    ksl = slice(0, 128) if hf is None else slice(64 * hf, 64 * hf + 64)
                    slots.append((Vall[ksl, jt0 + i, b0:b0 + 2, :].rearrange("p a d -> p (a d)"), p_[ksl, 64 * i:64 * i + 64]))
                for ci in range(2):
                    slots.append((Vall[:, 16 + ci, b0:b0 + 2, :].rearrange("p a d -> p (a d)"), p_[:, 64 * (nsl + ci):64 * (nsl + ci) + 64]))

                def fpv(e):
                    ins = None
                    ns_ = len(slots)
                    for i, (vl, pr) in enumerate(slots):
                        ins = e.matmul(ob[:, 64 * rr_:64 * rr_ + 64], lhsT=vl, rhs=pr, start=(i == 0), stop=(i == ns_ - 1))
                    return ins
                pe_fn(fpv, [p_t, T("Vall")], [obt])
                if rr_ == 3:
                    finish_head(ob, obt, hp, 256, hyT[psl, 2 + hc, H.cs], H.hy, l, None)
            return {"qk": qk, "ex": ex, "pv": pv}

        def mk_sw(h, bb):
            hp, hc, kv = h % 2, h // 2, h // 4
            psl = slice(64 * hp, 64 * hp + 64)
            vc0 = 6 + kv
            qbk = 4 * t + 2 * H.h + bb
            q0 = H.c0 + 128 * bb
            valid = [0 <= qbk - 1 + i <= 15 for i in range(3)]
            i0 = 0 if valid[0] else 1
            i1 = 3 if valid[2] else 2
            st = {}

            def qk():
                sbk, sbt = s_bank()
                sck, sct = aux_bank()
                st["s"] = (sbk, sbt, sck, sct)

                def f(e):
                    ins = None
                    for i in range(3):
                        if not valid[i]:
                            continue
                        kb = qbk - 1 + i
                        o_ = sbk[:, 128 * i:128 * i + 128]
                        if i != 1:
                            e.matmul(o_, lhsT=ident_b[:, :], rhs=swmask[:, 0 if i == 0 else 1, :], start=True, stop=False)
                        ins = e.matmul(o_, lhsT=kTsw[psl, kv, 128 * kb:128 * kb + 128], rhs=qTsw[psl, hc, q0:q0 + 128],
                                       start=(i == 1), stop=True)
                    return ins
                pe_fn(f, [T("kTsw"), H.qswt, CT], [sbt])

                def f2(e):
                    ins = None
                    for ci in range(2):
                        ins = e.matmul(sck[:, 128 * ci:128 * ci + 128], lhsT=kTsw[psl, kv, S + 128 * ci:S + 128 * ci + 128],
                                       rhs=qTsw[psl, hc, q0:q0 + 128], start=True, stop=True)
                    return ins
                pe_fn(f2, [T("kTsw"), H.qswt], [sct])

            def ex():
                sbk, sbt, sck, sct = st["s"]
                p_, p_t = pt()
                st["p"] = (p_, p_t)
                act(p_[:, 128 * i0:128 * i1], sbk[:, 128 * i0:128 * i1], AF.Exp, [sbt], [p_t])
                act(p_[:, 384:640], sck[:, 0:256], AF.Exp, [sct], [p_t])

            def pv():
                p_, p_t = st["p"]
                ob, obt = get_ob(("sw", h))
                slots = []
                for i in range(i0, i1):
                    kb = qbk - 1 + i
                    slots.append((Vall[:, kb, vc0, :], p_[:, 128 * i:128 * i + 128], slice(0, 128)))
                for ci in range(2):
                    slots.append((Vall[:, 16 + ci, vc0, :], p_[:, 384 + 128 * ci:384 + 128 * ci + 128], slice(0, 128)))
                pe_fn(lambda e: pv_group(e, ob, 128 * bb, 128, hp, slots), [p_t, T("Vall"), T("ones_b")], [obt])
                if bb == 1:
                    finish_head(ob, obt, hp, 256, hyT[psl, 4 + hc, H.cs], H.hy, l, hc)
            return {"qk": qk, "ex": ex, "pv": pv}

        if H.is_ctx:
            items = [mk_dense("na", h) for h in range(4)] + [mk_dense("sw", h) for h in range(8)]
        else:
            items = [mk_na(h, rr_) for h in range(4) for rr_ in range(4)] + [mk_sw(h, bb) for h in range(8) for bb in range(2)]
        run_items(items)

    bg = []

    def bg_step(n=1):
        for _ in range(n):
            if bg:
                bg.pop(0)()

    def bg_drain():
        while bg:
            bg.pop(0)()

    ln_pending = []

    def ln_flush():
        while ln_pending:
            ln_pending.pop(0)()

    def ln_stats_chunk(H, k):
        ln_flush()
        i = nxt("zb", 2)
        zb, zsq = zbs[i], zsqs[i]
        cp("dve", zb[:, 0:256], H.xs(k), [H.xTok], [T(f"zb{i}")])
        tt("pool", zsq[:, 0:256], H.xs(k), H.xs(k), ALU.mult, [H.xTok], [T(f"zsq{i}")])

        def f(e):
            e.matmul(H.s1[:, 0:256], lhsT=ones_b[:, :], rhs=zb[:, 0:256], start=(k == 0), stop=(k == KC - 1))
            return e.matmul(H.s2[:, 0:256], lhsT=ones_b[:, :], rhs=zsq[:, 0:256], start=(k == 0), stop=(k == KC - 1))
        ln_pending.append(lambda: pe_fn(f, [T(f"zb{i}"), T(f"zsq{i}"), T("ones_b")], [H.s1t, H.s2t]))

    def ln_finalize_ops(H):
        cs_ = H.cs
        st = H.stt_
        return [
            lambda: (ln_flush(), ts("dve", st_mean[:, cs_], H.s1[:, 0:256], 1.0 / D, ALU.mult, [H.s1t], [st])),
            lambda: tt("dve", st_nmr[:, cs_], st_mean[:, cs_], st_mean[:, cs_], ALU.mult, [st], [st]),
            lambda: stt("dve", st_rstd[:, cs_], H.s2[:, 0:256], 1.0 / D, st_nmr[:, cs_], ALU.mult, ALU.subtract, [H.s2t, st], [st]),
            lambda: act(st_rstd[:, cs_], st_rstd[:, cs_], AF.Ln, [st, T("epsc")], [st], bias=epsc[:, 0:1], scale=1.0),
            lambda: act(st_rstd[:, cs_], st_rstd[:, cs_], AF.Exp, [st], [st], scale=-0.5),
            lambda: stt("dve", st_nmr[:, cs_], st_mean[:, cs_], -1.0, st_rstd[:, cs_], ALU.mult, ALU.mult, [st], [st]),
        ]

    def ln_apply_ops(H, k, gcol, bcol, h2):
        box = {}

        def o1():
            i_ = nxt("lnt", 3)
            box["t"] = (lnt[i_], T(f"lnt{i_}"))
            t1, t1t = box["t"]
            tt("pool", t1[:, 0:256], H.xs(k), st_rstd[:, H.cs], ALU.mult, [H.xTok, H.stt_], [t1t])

        def o2():
            t1, t1t = box["t"]
            tt("dve", t1[:, 0:256], t1[:, 0:256], st_nmr[:, H.cs], ALU.add, [t1t, H.stt_], [t1t])

        def o3():
            t1, t1t = box["t"]
            if h2:
                act(hyT[:, k, H.cs], t1[:, 0:256], AF.Identity, [t1t, T("lnh2")], [H.hy],
                    bias=lnh2[:, H.l, H.b, 1, k:k + 1], scale=lnh2[:, H.l, H.b, 0, k:k + 1])
                ts("dve", H.xs(k), t1[:, 0:256], gcol, ALU.mult, [t1t, T("lnc")], [H.xTok], s2=bcol, op1=ALU.add)
            else:
                act(H.xs(k), t1[:, 0:256], AF.Identity, [t1t, T("lnc")], [H.xTok], bias=bcol, scale=gcol)
        return [o1, o2, o3]

    def ln_push(H, which, h2, defer_fin=True):
        l = H.l
        fin = ln_finalize_ops(H)
        nimm = 3 if defer_fin else 6
        for f_ in fin[:nimm]:
            f_()
        bg.extend(fin[nimm:])
        chains = [ln_apply_ops(H, k, lnc[:, l, which, k:k + 1], lnc[:, l, which + 1, k:k + 1], h2) for k in range(KC)]
        for step in range(KC + 2):
            for k in range(KC):
                j = step - k
                if 0 <= j < 3:
                    bg.append(chains[k][j])

    def ph_modulate(H):
        modulate(H.xs, [H.xTok], H.l, 0, 1, H.b, H.c0, 256, [H.hy])

    def ph_proj(Hs, l):
        w2_, w2t = w_get(l, P_Q2)
        w23 = w2_[:, 0:2048].rearrange("p (k c) -> p k c", k=8)
        vgs = {}
        for H in Hs:
            for tt_ in range(2):
                vi = 2 * H.h + tt_
                acc, acct = acc_bank()
                mm_group(acc[:, 0:256], [(hyT[:, k, H.c0 + 128 * tt_:H.c0 + 128 * tt_ + 128], w23[:, k, :]) for k in range(KC)],
                         [w2t, H.hy], [acct])
                vg_, vgt = ((vg[vi], T(f"vg{vi}")) if vi < 2 else (lnt[vi - 2], T(f"lnt{vi - 2}")))
                vgs[vi] = (vg_, vgt, H)
                act(vg_[:, :], acc[:, 0:256], AF.Gelu_apprx_tanh, [acct], [vgt])
                P.op("dve", lambda e, vg_=vg_, vi=vi: e.bn_stats(out=mv[:, vi, 0:6], in_=vg_[:, :]), reads=[vgt], writes=[T("mv")])
                P.op("dve", lambda e, vi=vi: e.bn_aggr(out=mv[:, vi, 6:8], in_=mv[:, vi, 0:6]), reads=[T("mv")], writes=[T("mv")])
        nv = 2 * len(Hs)
        cp("dve", mvr[:, 0:nv], mv[:, 0:nv, 7], [T("mv")], [T("mvr")])
        act(mvr[:, 0:nv], mvr[:, 0:nv], AF.Sqrt, [T("mvr"), T("epsc")], [T("mvr")], bias=epsc[:, 0:1], scale=1.0)
        recip(mvr[:, 0:nv], mvr[:, 0:nv], [T("mvr")], [T("mvr")])
        for vi in range(nv):
            vg_, vgt, H = vgs[vi]
            ts("dve", vg_[:, :], vg_[:, :], mv[:, vi, 6:7], ALU.subtract, [vgt, T("mv"), T("mvr")], [vgt], s2=mvr[:, vi:vi + 1], op1=ALU.mult)
            tt("pool", vg_[:, :], vg_[:, :], alnG[:, l, :], ALU.mult, [vgt, T("alnGB")], [vgt])
            tt("dve", vln[:, vi, :], vg_[:, :], alnB[:, l, :], ALU.add, [vgt, T("alnGB")], [H.vlnt])
        w0, w0t = w_get(l, P_Q0)
        w03 = w0[:, :].rearrange("p (k c) -> p k c", k=8)
        for H in Hs:
            for ch in range(4):
                acc, acct = acc_bank()
                mm_group(acc[:, 0:256], [(w03[:, k, 128 * ch:128 * ch + 128], hyT[:, k, H.cs]) for k in range(KC)],
                         [w0t, H.hy], [acct])
                if ch < 2:
                    act(uT[:, ch, H.cs], acc[:, 0:256], AF.Gelu_apprx_tanh, [acct], [H.uTt])
                else:
                    act(qTna[:, ch - 2, H.cs], acc[:, 0:256], AF.Copy, [acct], [H.qnat], scale=0.125)
        w1_, w1t = w_get(l, P_Q1)
        w13 = w1_[:, :].rearrange("p (k c) -> p k c", k=8)
        pend = []
        for H in Hs:
            for ch in range(4):
                acc, acct = acc_bank()
                mm_group(acc[:, 0:256], [(w13[:, k, 128 * ch:128 * ch + 128], hyT[:, k, H.cs]) for k in range(KC)],
                         [w1t, H.hy], [acct])
                while pend:
                    pend.pop(0)()
                if H.is_ctx:
                    act(qTsw[:, ch, H.cs], acc[:, 0:256], AF.Copy, [acct], [H.qswt], scale=0.125)
                else:
                    pend.append(rope_evac(acc, acct, qTsw[:, ch, H.cs], H.qswt, H.c0, 256, 0.125))
        while pend:
            pend.pop(0)()
        return pend

    def ph_gmlp(H):
        l = H.l
        for m in range(2):
            ab, abt = acc_bank()

            def fmix(e, ab=ab, m=m):
                ins = None
                for n in range(2):
                    for hh in range(2):
                        g = 2 * m + hh
                        ins = e.matmul(ab[64 * hh:64 * hh + 64, 128 * n:128 * n + 128], lhsT=vln[:, 2 * H.h + n, 64 * g:64 * g + 64],
                                       rhs=WsT[:, l, g, :], start=True, stop=True)
                return ins
            pe_fn(fmix, [H.vlnt, T("WsT")], [abt])
            t1, t1t = tmp()
            for n in range(2):
                tt("dve", t1[:, 128 * n:128 * n + 128], ab[:, 128 * n:128 * n + 128], Bb[:, l, m, :], ALU.add,
                   [abt, T("Bb")], [t1t])
            tt("pool", hyT[:, m, H.cs], t1[:, 0:256], uT[:, m, H.cs], ALU.mult, [t1t, H.uTt], [H.hy])

    def ph_wout(H, wos):
        l, b = H.l, H.b
        bg_drain()
        for dch in range(KC):
            wo3, wot = wos[dch // 4]
            dl = dch % 4
            acc, acct = acc_bank()
            mm_group(acc[:, 0:256], [(wo3[:, k, 128 * dl:128 * dl + 128], hyT[:, k, H.cs]) for k in range(KC)],
                     [wot, H.hy], [acct])
            stt("dve", H.xs(dch), acc[:, 0:256], mcol(l, 2, dch, b), H.xs(dch), ALU.mult, ALU.add, [acct, T("modT"), H.xTok], [H.xTok])
            ln_stats_chunk(H, dch)
        ln_push(H, 0, True, defer_fin=(H.h == 0 and not H.is_ctx))

    def ffn_quarter(Hs, l, qq, hook=None):
        for hh in range(2):
            wa, wat = w_get(l, P_W1 + 2 * qq + hh)
            wa3 = wa[:, :].rearrange("p (k c) -> p k c", k=8)
            for H in Hs:
                for jj in range(4):
                    j = 4 * hh + jj
                    acc, acct = acc_bank()
                    mm_group(acc[:, 0:256], [(wa3[:, k, 128 * jj:128 * jj + 128], hyT[:, k, H.cs]) for k in range(KC)],
                             [wat, H.hy], [acct])
                    t1, t1t = tmp()
                    act(t1[:, 0:256], acc[:, 0:256], AF.Relu, [acct], [t1t])
                    tt("pool", hid[:, j, H.cs], t1[:, 0:256], t1[:, 0:256], ALU.mult, [t1t], [H.hidt])
                    if qq == 0:
                        bg_step(2)
        if hook is not None:
            hook()
        for hf in range(2):
            wb, wbt = w_get(l, P_W2 + 2 * qq + hf)
            wb3 = wb[:, :].rearrange("p (j c) -> p j c", j=8)
            for H in Hs:
                for dl in range(4):
                    dch = 4 * hf + dl
                    acc, acct = acc_bank()
                    mm_group(acc[:, 0:256], [(wb3[:, j, 128 * dl:128 * dl + 128], hid[:, j, H.cs]) for j in range(8)],
                             [wbt, H.hidt], [acct])
                    stt("dve", H.xs(dch), acc[:, 0:256], mcol(l, 5, dch, H.b), H.xs(dch), ALU.mult, ALU.add,
                        [acct, T("modT"), H.xTok], [H.xTok])
                    if qq == 3:
                        ln_stats_chunk(H, dch)
                    if qq == 0:
                        bg_step(2)

    def ph_ffn(Hs, l, hook=None):
        if len(Hs) == 1:
            bg_drain()
            ffn_quarter(Hs, l, 0)
        else:
            ffn_quarter(Hs[0:1], l, 0)
            bg_drain()
            ffn_quarter(Hs[1:2], l, 0)
        for qq in range(1, 4):
            ffn_quarter(Hs, l, qq, hook if qq == 3 else None)
        for H in Hs:
            ln_push(H, 2, False)

    premod = set()

    def mk_halves(l, b, t, is_ctx):
        return [Half(l, b, t, 0, True)] if is_ctx else [Half(l, b, t, 0, False), Half(l, b, t, 1, False)]

    def pre_modulate(l, b, t, is_ctx):
        if not is_ctx:
            load_cs(512 * t, 512)
        for H in mk_halves(l, b, t, is_ctx):
            ph_modulate(H)
        premod.add((l, b, t, is_ctx))

    def tile_pass2(l, b, t, is_ctx, nxt_tile=None):
        Hs = mk_halves(l, b, t, is_ctx)
        if (l, b, t, is_ctx) not in premod:
            pre_modulate(l, b, t, is_ctx)
        pend = ph_proj(Hs, l)
        wos = None
        for H in Hs:
            ph_gmlp(H)
            while pend:
                pend.pop(0)()
            attention(H)
            if wos is None:
                wo0, wo0t = w_get(l, P_WO0, la=2)
                wo1, wo1t = w_get(l, P_WO1, la=1)
                wos = [(wo0[:, :].rearrange("p (k c) -> p k c", k=8), wo0t), (wo1[:, :].rearrange("p (k c) -> p k c", k=8), wo1t)]
            ph_wout(H, wos)
        hook = None
        if nxt_tile is not None:
            hook = lambda: pre_modulate(*nxt_tile)
        ph_ffn(Hs, l, hook)

    stage = [hid[:, 0:4, :].rearrange("p a b -> p (a b)").bitcast(F32), hid[:, 4:8, :].rearrange("p a b -> p (a b)").bitcast(F32),
             hyT[:, 0:4, :].rearrange("p a b -> p (a b)").bitcast(F32), hyT[:, 4:8, :].rearrange("p a b -> p (a b)").bitcast(F32)]
    memset("dve", Vall[:, :, 1, :], 1.0, [T("Vall")])
    memset("dve", Vall[:, :, 4, :], 1.0, [T("Vall")])
    STG = [T("hid"), T("hid0"), T("hid1"), T("hyT0"), T("hyT1")]

    for s_ in range(nseq):
        if s_ > 0:
            P.new_epoch()
        for i in range(2 + 16):
            si = i % 4
            if i < 2:
                src = ctx_d[s_, 128 * i:128 * i + 128, :]
            else:
                src = x_d[s_, 128 * (i - 2):128 * (i - 1), :]
            dma("sp", stage[si], src, [], STG, T("hid"))
            for half_ in range(2):
                ab, abt = io_bank()

                def ftr(e, ab=ab, si=si, half_=half_):
                    ins = None
                    for c4 in range(4):
                        k = 4 * half_ + c4
                        ins = e.transpose(ab[:, 128 * c4:128 * c4 + 128], stage[si][:, 128 * k:128 * k + 128], ident_f[:, :])
                    return ins
                pe_fn(ftr, STG + [CT], [abt])
                if i < 2:
                    dst = xcT[:, 4 * half_:4 * half_ + 4, 128 * i:128 * i + 128]
                    dT = T("xcT")
                else:
                    tok = 128 * (i - 2)
                    dst = xT[:, 4 * half_:4 * half_ + 4, tok:tok + 128]
                    dT = T(f"xT{tok // 512}_{(tok % 512) // 256}")
                if half_ == 0:
                    act(dst, ab[:, 0:512].rearrange("p (c t) -> p c t", c=4), AF.Copy, [abt], [dT], scale=ALPHA)
                else:
                    ts("dve", dst, ab[:, 0:512].rearrange("p (c t) -> p c t", c=4), ALPHA, ALU.mult, [abt], [dT])
        for l in range(nlayers):
            if s_ == 0 and l == 1:
                pass
            dma("sp", bmlib[:, :, :, :].rearrange("p h d q -> p (h d q)"), bmsc_d[l], [BMC], [T("bmlib")], T("bmlib"))
            bg_drain()
            if l == 1:
                while castq:
                    castq.pop(0)()
            if s_ == 0 and l == 1:
                P.op("sp", None, writes=STG, nosig=True)
                mod_layer(1, [(stage[i].rearrange("p (k c) -> p k c", k=8), T(["stgm0", "stgm1", "hyT0", "hyT1"][i]), 128) for i in range(4)])
                derive_lnh2(1)
            wk, wkt = w_get(l, P_KV0, la=1)
            wv, wvt = w_get(l, P_KV1, la=1)
            pass1_all(l, s_, wk, wkt, wv, wvt)
            if s_ == 0 and l == 0 and nlayers > 1:
                emit_casts(1, castq)
            tiles = [(l, s_, t, False) for t in range(4)]
            if l < nlayers - 1:
                tiles.append((l, 4, 0, True))
            for ti, tl in enumerate(tiles):
                tile_pass2(*tl, nxt_tile=(tiles[ti + 1] if ti + 1 < len(tiles) else None))
                pool_ok[0] = True
        bg_drain()
        for i in range(16):
            si = i % 4
            tok = 128 * i
            for half_ in range(2):
                ab, abt = io_bank()

                def ftr2(e, ab=ab, tok=tok, half_=half_):
                    ins = None
                    for c4 in range(4):
                        k = 4 * half_ + c4
                        ins = e.transpose(ab[:, 128 * c4:128 * c4 + 128], xT[:, k, tok:tok + 128], ident_f[:, :])
                    return ins
                pe_fn(ftr2, [T(f"xT{tok // 512}_{(tok % 512) // 256}"), CT], [abt])
                cp("act" if half_ == 0 else "dve", stage[si][:, 512 * half_:512 * half_ + 512], ab[:, 0:512], [abt], STG)
            dma("sp", y_d[s_, tok:tok + 128, :], stage[si], STG, [], T("hid"))
    if dbg:
        pass
    P.op("sp", None, writes=STG, nosig=True)
    if "dbgsem" in tk:
        P.op("sp", None, writes=[T("dbgsem")], nosig=True)
    P.emit()
    return nc


_CACHE = {}


def kernel(**inputs):
    x = np.ascontiguousarray(inputs["x"], dtype=np.float32)
    B = x.shape[0]
    assert B == NCORES * BPC
    consts = host_consts(np.asarray(inputs["na_rpb"], dtype=np.float32))
    if "nc" not in _CACHE:
        _CACHE["nc"] = build_program(BPC, NL)
    nc = _CACHE["nc"]
    shared = {k: np.ascontiguousarray(np.asarray(inputs[k], dtype=np.float32)) for k in
              ["c_ctx", "w_mod", "b_mod", "w_in", "a_ln_g", "a_ln_b", "a_ws", "a_bs", "sw_sink", "w_out",
               "ln1_g", "ln1_b", "w1", "w2", "ln2_g", "ln2_b"]}
    shared.update(consts)
    in_maps = []
    for i in range(NCORES):
        m = dict(shared)
        m["x"] = x[BPC * i:BPC * (i + 1)]
        m["c"] = np.ascontiguousarray(np.asarray(inputs["c"], dtype=np.float32)[BPC * i:BPC * (i + 1)])
        m["ctx"] = np.ascontiguousarray(np.asarray(inputs["ctx"], dtype=np.float32)[BPC * i:BPC * (i + 1)])
        in_maps.append(m)
    res = run_bass_kernel_spmd(nc, in_maps, core_ids=list(range(NCORES)))
    out = np.concatenate([np.asarray(r["y"], dtype=np.float32) for r in res.results], axis=0)
    return out
```
